# Optimizing a Trainium2 kernel written in Bass

```python
import math
import jax, jax.numpy as jnp
from jax import lax
import numpy as np

D_MODEL = 1024
BATCH = 8
SEQ = 4096
DEPTH = 1
DEC_BATCH = 16
DEC_SEQ = 2048
PAST_LEN = 128

DA_HEADS = 4
DA_HEAD_DIM = 64
DA_V_DIM = 2 * DA_HEAD_DIM
DA_QK = DA_HEADS * 2 * DA_HEAD_DIM
DA_WIDTH = DA_HEADS * DA_V_DIM
GLA_HEADS = 4
GLA_DK = 64
GLA_DV = 128
GLA_QK = GLA_HEADS * GLA_DK
GLA_WIDTH = GLA_HEADS * GLA_DV
GLA_GATE_RANK = 16
GLA_GATE_TAU = 16.0
GLA_CHUNK = 64
N_BRANCHES = 2
REL_BUCKETS = 32
REL_MAX_DIST = 128
Q_BLOCK = 128
PEER_HEADS = 8
PEER_N_KEYS = 128
PEER_N_EXPERTS = PEER_N_KEYS * PEER_N_KEYS
PEER_TOPK = 16
PEER_QUERY_DIM = 256
PEER_HALF = PEER_QUERY_DIM // 2
PEER_TOKEN_BLOCK = 128
PLE_DIM = 256
EPS = 1e-6

IN_SIZES = (DA_QK, DA_QK, DA_WIDTH, GLA_QK, GLA_QK, GLA_WIDTH, GLA_WIDTH, 2 * GLA_GATE_RANK, N_BRANCHES * D_MODEL)
IN_DIM = DA_QK * 2 + DA_WIDTH + GLA_QK * 2 + GLA_WIDTH * 2 + 2 * GLA_GATE_RANK + N_BRANCHES * D_MODEL

kernel_name = "hybrid_diffattn_gla_peer_encoder"


def rmsnorm(x, g):
    xf = x.astype(jnp.float32)
    y = xf * lax.rsqrt(jnp.mean(xf * xf, axis=-1, keepdims=True) + EPS)
    return (y * g.astype(jnp.float32)).astype(x.dtype)


def split_in_proj(proj):
    idx, acc = [], 0
    for s in IN_SIZES[:-1]:
        acc += s
        idx.append(acc)
    return jnp.split(proj, idx, axis=-1)


def t5_bucket(rel):
    nb = REL_BUCKETS // 2
    max_exact = nb // 2
    ret = (rel > 0).astype(jnp.int32) * nb
    n = jnp.abs(rel)
    nf = jnp.maximum(n, 1).astype(jnp.float32)
    large = max_exact + (jnp.log(nf / max_exact) / math.log(REL_MAX_DIST / max_exact) * (nb - max_exact)).astype(jnp.int32)
    large = jnp.minimum(large, nb - 1)
    return ret + jnp.where(n < max_exact, n, large)


def diff_attention(q, k, v, lam, rel_bias):
    B, S, H, _, d = q.shape
    nb = S // Q_BLOCK
    scale = d ** -0.5
    qb = q.reshape(B, nb, Q_BLOCK, H, 2, d).transpose(1, 0, 2, 3, 4, 5)
    starts = jnp.arange(nb, dtype=jnp.int32) * Q_BLOCK
    kpos = jnp.arange(S, dtype=jnp.int32)
    qoff = jnp.arange(Q_BLOCK, dtype=jnp.int32)

    def one_block(args):
        qblk, s0 = args
        rel = kpos[None, :] - (s0 + qoff)[:, None]
        bias = jnp.take(rel_bias, t5_bucket(rel), axis=0)
        bias = bias.transpose(2, 0, 1).astype(jnp.float32)
        logits = jnp.einsum('bqhcd,bkhcd->bhcqk', qblk, k).astype(jnp.float32) * scale + bias[None, :, None]
        a = jax.nn.softmax(logits, axis=-1)
        w = a[:, :, 0] - lam * a[:, :, 1]
        return jnp.einsum('bhqk,bkhv->bqhv', w, v.astype(jnp.float32))

    o = lax.map(one_block, (qb, starts))
    return o.transpose(1, 0, 2, 3, 4).reshape(B, S, H, v.shape[-1])


def gla_chunked(q, k, v, log_a, strict):
    B, S, H, dk = q.shape
    dv = v.shape[-1]
    C = GLA_CHUNK
    nc = S // C
    f32 = jnp.float32
    q = q.astype(f32).reshape(B, nc, C, H, dk)
    k = k.astype(f32).reshape(B, nc, C, H, dk)
    v = v.astype(f32).reshape(B, nc, C, H, dv)
    b = jnp.cumsum(log_a.astype(f32).reshape(B, nc, C, H, dk), axis=2)
    b_last = b[:, :, -1:]
    qe = q * jnp.exp(b)
    ke = k * jnp.exp(-b)
    kd = k * jnp.exp(b_last - b)
    mask = jnp.tril(jnp.ones((C, C), dtype=bool), k=-1 if strict else 0)
    att = jnp.einsum('bnthk,bnshk->bnhts', qe, ke)
    att = jnp.where(mask, att, 0.0)
    o = jnp.einsum('bnhts,bnshv->bnthv', att, v)
    kv = jnp.einsum('bnshk,bnshv->bnhkv', kd, v)
    decay = jnp.exp(b_last[:, :, 0])

    def step(state, inp):
        dec, kvc = inp
        return dec[..., None] * state + kvc, state

    _, states = lax.scan(step, jnp.zeros((B, H, dk, dv), f32),
                         (decay.transpose(1, 0, 2, 3), kv.transpose(1, 0, 2, 3, 4)))
    states = states.transpose(1, 0, 2, 3, 4)
    o = o + jnp.einsum('bnthk,bnhkv->bnthv', qe, states)
    return o.reshape(B, S, H, dv)


def peer_ffn(h, w_q, sub_keys, u_tab, v_tab):
    B, S, D = h.shape
    tokens = h.reshape(-1, PEER_TOKEN_BLOCK, D)

    def one_block(hb):
        q = jnp.einsum('td,de->te', hb, w_q).reshape(-1, PEER_HEADS, 2, PEER_HALF)
        s = jnp.einsum('thcd,hcnd->thcn', q, sub_keys).astype(jnp.float32)
        v1, i1 = lax.top_k(s[:, :, 0], PEER_TOPK)
        v2, i2 = lax.top_k(s[:, :, 1], PEER_TOPK)
        cand = (v1[..., :, None] + v2[..., None, :]).reshape(-1, PEER_HEADS, PEER_TOPK * PEER_TOPK)
        cidx = (i1[..., :, None] * PEER_N_KEYS + i2[..., None, :]).reshape(-1, PEER_HEADS, PEER_TOPK * PEER_TOPK)
        top, pos = lax.top_k(cand, PEER_TOPK)
        idx = jnp.take_along_axis(cidx, pos, axis=-1)
        g = jax.nn.softmax(top, axis=-1)
        u = jnp.take(u_tab, idx, axis=0)
        act = jax.nn.gelu(jnp.einsum('thkd,td->thk', u, hb).astype(jnp.float32))
        vv = jnp.take(v_tab, idx, axis=0)
        return jnp.einsum('thk,thkd->td', (g * act).astype(vv.dtype), vv)

    out = lax.map(one_block, tokens)
    return out.reshape(B, S, D).astype(h.dtype)


def encoder_layer(x, p_i, layer_idx, rel_bias, norm1_g, w_in, lambda_qk, da_norm_g,
                  gla_alpha_w, gla_alpha_b, gla_norm_g, w_up_a, w_up_b, w_out,
                  norm2_g, peer_w_q, peer_sub_keys, peer_u, peer_v, norm3_g, ple_w, ple_gate_w):
    B, S, D = x.shape
    h = rmsnorm(x, norm1_g)
    proj = jnp.einsum('bsd,de->bse', h, w_in)
    qa, ka, va, qg, kg, vg, rg, lr, gl = split_in_proj(proj)

    lam_init = 0.8 - 0.6 * math.exp(-0.3 * layer_idx)
    lq = lambda_qk.astype(jnp.float32)
    lam = jnp.exp(jnp.sum(lq[0] * lq[1])) - jnp.exp(jnp.sum(lq[2] * lq[3])) + lam_init
    oa = diff_attention(qa.reshape(B, S, DA_HEADS, 2, DA_HEAD_DIM),
                        ka.reshape(B, S, DA_HEADS, 2, DA_HEAD_DIM),
                        va.reshape(B, S, DA_HEADS, DA_V_DIM), lam, rel_bias)
    oa = rmsnorm(oa, da_norm_g) * (1.0 - lam_init)
    ya = jnp.einsum('bse,ed->bsd', oa.reshape(B, S, DA_WIDTH).astype(x.dtype), w_up_a)

    qg = qg.reshape(B, S, GLA_HEADS, GLA_DK) * (GLA_DK ** -0.5)
    kg = kg.reshape(B, S, GLA_HEADS, GLA_DK)
    vg = vg.reshape(B, S, GLA_HEADS, GLA_DV)
    z = jnp.einsum('bsjr,jrk->bsjk', lr.reshape(B, S, 2, GLA_GATE_RANK), gla_alpha_w) + gla_alpha_b
    log_a = (jax.nn.log_sigmoid(z.astype(jnp.float32)) / GLA_GATE_TAU).reshape(B, S, 2, GLA_HEADS, GLA_DK)
    o_fwd = gla_chunked(qg, kg, vg, log_a[:, :, 0], strict=False)
    flip = lambda t: jnp.flip(t, axis=1)
    o_bwd = flip(gla_chunked(flip(qg), flip(kg), flip(vg), flip(log_a[:, :, 1]), strict=True))
    og = rmsnorm(o_fwd + o_bwd, gla_norm_g).reshape(B, S, GLA_WIDTH)
    og = (og * jax.nn.silu(rg.astype(jnp.float32))).astype(x.dtype)
    yb = jnp.einsum('bse,ed->bsd', og, w_up_b)

    gates = jax.nn.sigmoid(gl.reshape(B, S, N_BRANCHES, D).astype(jnp.float32)).astype(x.dtype)
    merged = gates[:, :, 0] * ya + gates[:, :, 1] * yb
    x = x + jnp.einsum('bsd,de->bse', merged, w_out)

    x = x + peer_ffn(rmsnorm(x, norm2_g), peer_w_q, peer_sub_keys, peer_u, peer_v)

    h3 = rmsnorm(x, norm3_g)
    gate = jax.nn.sigmoid(jnp.einsum('bsd,de->bse', h3, ple_gate_w).astype(jnp.float32)).astype(x.dtype)
    x = x + jnp.einsum('bsp,pd->bsd', p_i, ple_w) * gate
    return x


def run_trunk(x, p, rel_bias, norm1_g, w_in, lambda_qk, da_norm_g, gla_alpha_w, gla_alpha_b,
              gla_norm_g, w_up_a, w_up_b, w_out, norm2_g, peer_w_q, peer_sub_keys, peer_u,
              peer_v, norm3_g, ple_w, ple_gate_w, final_norm_g):
    for i in range(DEPTH):
        x = encoder_layer(x, p[i], i, rel_bias, norm1_g[i], w_in[i], lambda_qk[i], da_norm_g[i],
                          gla_alpha_w[i], gla_alpha_b[i], gla_norm_g[i], w_up_a[i], w_up_b[i], w_out[i],
                          norm2_g[i], peer_w_q[i], peer_sub_keys[i], peer_u[i], peer_v[i],
                          norm3_g[i], ple_w[i], ple_gate_w[i])
    return rmsnorm(x, final_norm_g)


def setup_inputs(seed: int = 0) -> dict:
    key = jax.random.key(seed)
    ks = jax.random.split(key, 24)
    f32 = jnp.float32

    def nrm(k, shape, scale):
        return jax.random.normal(k, shape, f32) * scale

    def gain(k, shape):
        return 1.0 + 0.05 * jax.random.normal(k, shape, f32)

    return {
        "x_prompt": nrm(ks[0], (BATCH, SEQ, D_MODEL), 1.0),
        "x_sample": nrm(ks[1], (DEC_BATCH, DEC_SEQ, D_MODEL), 1.0),
        "p_prompt": nrm(ks[2], (DEPTH, BATCH, SEQ, PLE_DIM), 1.0),
        "p_sample": nrm(ks[3], (DEPTH, DEC_BATCH, DEC_SEQ, PLE_DIM), 1.0),
        "rel_bias": nrm(ks[4], (REL_BUCKETS, DA_HEADS), 0.5),
        "norm1_g": gain(ks[5], (DEPTH, D_MODEL)),
        "w_in": nrm(ks[6], (DEPTH, D_MODEL, IN_DIM), D_MODEL ** -0.5),
        "lambda_qk": nrm(ks[7], (DEPTH, 4, DA_HEAD_DIM), 0.1),
        "da_norm_g": gain(ks[8], (DEPTH, DA_V_DIM)),
        "gla_alpha_w": nrm(ks[9], (DEPTH, 2, GLA_GATE_RANK, GLA_QK), GLA_GATE_RANK ** -0.5),
        "gla_alpha_b": nrm(ks[10], (DEPTH, 2, GLA_QK), 0.1),
        "gla_norm_g": gain(ks[11], (DEPTH, GLA_DV)),
        "w_up_a": nrm(ks[12], (DEPTH, DA_WIDTH, D_MODEL), DA_WIDTH ** -0.5),
        "w_up_b": nrm(ks[13], (DEPTH, GLA_WIDTH, D_MODEL), GLA_WIDTH ** -0.5),
        "w_out": nrm(ks[14], (DEPTH, D_MODEL, D_MODEL), D_MODEL ** -0.5),
        "norm2_g": gain(ks[15], (DEPTH, D_MODEL)),
        "peer_w_q": nrm(ks[16], (DEPTH, D_MODEL, PEER_HEADS * PEER_QUERY_DIM), D_MODEL ** -0.5),
        "peer_sub_keys": nrm(ks[17], (DEPTH, PEER_HEADS, 2, PEER_N_KEYS, PEER_HALF), PEER_HALF ** -0.5),
        "peer_u": nrm(ks[18], (DEPTH, PEER_N_EXPERTS, D_MODEL), D_MODEL ** -0.5),
        "peer_v": nrm(ks[19], (DEPTH, PEER_N_EXPERTS, D_MODEL), (PEER_HEADS * PEER_TOPK) ** -0.5),
        "norm3_g": gain(ks[20], (DEPTH, D_MODEL)),
        "ple_w": nrm(ks[21], (DEPTH, PLE_DIM, D_MODEL), PLE_DIM ** -0.5),
        "ple_gate_w": nrm(ks[22], (DEPTH, D_MODEL, D_MODEL), D_MODEL ** -0.5),
        "final_norm_g": gain(ks[23], (D_MODEL,)),
    }


def reference(x_prompt, x_sample, p_prompt, p_sample, rel_bias, norm1_g, w_in, lambda_qk, da_norm_g,
              gla_alpha_w, gla_alpha_b, gla_norm_g, w_up_a, w_up_b, w_out, norm2_g, peer_w_q,
              peer_sub_keys, peer_u, peer_v, norm3_g, ple_w, ple_gate_w, final_norm_g):
    y_prompt = run_trunk(x_prompt, p_prompt, rel_bias, norm1_g, w_in, lambda_qk, da_norm_g, gla_alpha_w,
                         gla_alpha_b, gla_norm_g, w_up_a, w_up_b, w_out, norm2_g, peer_w_q, peer_sub_keys,
                         peer_u, peer_v, norm3_g, ple_w, ple_gate_w, final_norm_g)
    y_sample = run_trunk(x_sample, p_sample, rel_bias, norm1_g, w_in, lambda_qk, da_norm_g, gla_alpha_w,
                         gla_alpha_b, gla_norm_g, w_up_a, w_up_b, w_out, norm2_g, peer_w_q, peer_sub_keys,
                         peer_u, peer_v, norm3_g, ple_w, ple_gate_w, final_norm_g)
    return (y_prompt, y_sample)
```

```python
import math
import numpy as np
from contextlib import ExitStack
import concourse.bass as bass
import concourse.mybir as mybir
from concourse.bass_utils import run_bass_kernel_spmd

F32 = mybir.dt.float32
BF16 = mybir.dt.bfloat16
U32 = mybir.dt.uint32
I32 = mybir.dt.int32
AF = mybir.ActivationFunctionType
ALU = mybir.AluOpType
AX = mybir.AxisListType

D = 1024
IN_DIM = 5152
C_QA, C_KA, C_VA, C_QG, C_KG, C_VG, C_RG, C_LR, C_GL = 0, 512, 1024, 1536, 1792, 2048, 2560, 3072, 3104
EPS = 1e-6
NE = 16384
SEQS_FULL = (4096, 2048, 2048)


class T:
    __slots__ = ("name", "w", "r")

    def __init__(self, name=""):
        self.name = name
        self.w = {}
        self.r = {}


class Op:
    __slots__ = ("eng", "key", "fn", "deps", "ticket", "needed", "is_dma", "order")

    def __init__(self, eng, key, fn, is_dma):
        self.eng = eng
        self.key = key
        self.fn = fn
        self.deps = {}
        self.ticket = None
        self.needed = False
        self.is_dma = is_dma
        self.order = 0


class Emitter:
    NRING = 32
    ENGS = ("pe", "act", "dve", "pool", "sp")

    def __init__(self):
        self.progs = {k: [] for k in self.ENGS}
        self.ring_last = [None] * self.NRING
        self.ring_i = 0
        self.all_dma = []
        self.last_op = {}
        self.pending = {k: [] for k in self.ENGS}
        self.nops = 0

    def _add_dep(self, op, d):
        if d is op:
            return
        if d.key == op.key and op.key == "pe":
            return
        cur = op.deps.get(d.key)
        if cur is None or cur.order < d.order:
            op.deps[d.key] = d

    def op(self, eng, fn, reads=(), writes=(), dma=False):
        self.nops += 1
        if dma:
            slot = self.ring_i % self.NRING
            self.ring_i += 1
            key = "dma%d" % slot
        else:
            key = eng
        o = Op(eng, key, fn, dma)
        o.order = self.nops
        if dma:
            prev = self.ring_last[slot]
            self.ring_last[slot] = o
            if prev is not None:
                o.deps[key] = prev
            self.all_dma.append(o)
        for d in self.pending[eng]:
            self._add_dep(o, d)
        self.pending[eng] = []
        for t in reads:
            for k, d in t.w.items():
                self._add_dep(o, d)
        for t in writes:
            for k, d in t.w.items():
                if k == eng and not dma and eng != "pool":
                    continue
                self._add_dep(o, d)
            for k, d in t.r.items():
                if k == eng and not dma and eng != "pool":
                    continue
                self._add_dep(o, d)
        for t in reads:
            t.r[key] = o
        for t in writes:
            t.w[key] = o
        self.progs[eng].append(o)
        self.last_op[key] = o
        return o

    def barrier(self):
        lasts = list(self.last_op.values())
        for k in self.ENGS:
            self.pending[k] = list(lasts)

    def finalize(self):
        for prog in self.progs.values():
            for o in prog:
                for d in o.deps.values():
                    d.needed = True
        counters = {}
        for prog in self.progs.values():
            for o in prog:
                if o.needed and not o.is_dma:
                    counters[o.key] = counters.get(o.key, 0) + 1
                    o.ticket = counters[o.key]
        for o in self.all_dma:
            o.needed = True
            counters[o.key] = counters.get(o.key, 0) + 1
            o.ticket = counters[o.key]

    EPOCH = 16000

    def sem_of(self, sems, op):
        if op.is_dma:
            return sems[op.key], op.ticket * 16, (op.key, 0)
        ep, loc = divmod(op.ticket - 1, self.EPOCH)
        return sems["%s_%d" % (op.key, ep)], loc + 1, (op.key, ep)

    def n_epochs(self, key):
        mx = 0
        for o in self.progs[key]:
            if o.ticket is not None and not o.is_dma:
                mx = max(mx, o.ticket)
        return (mx - 1) // self.EPOCH + 1 if mx else 1

    def emit_engine(self, eng, engine_obj, sems, final_wait=False):
        waited = {}
        for o in self.progs[eng]:
            for k, d in o.deps.items():
                sem, val, wk = self.sem_of(sems, d)
                if waited.get(wk, 0) >= val:
                    continue
                engine_obj.wait_ge(sem, val)
                waited[wk] = val
            ins = o.fn(engine_obj)
            if o.needed:
                sem, _, _ = self.sem_of(sems, o)
                ins.then_inc(sem, 16 if o.is_dma else 1)
        if final_wait:
            for slot in range(self.NRING):
                last = self.ring_last[slot]
                if last is not None:
                    val = last.ticket * 16
                    if waited.get((last.key, 0), 0) < val:
                        engine_obj.wait_ge(sems[last.key], val)
                        waited[(last.key, 0)] = val


class Arena:
    def __init__(self, ap, nelem):
        self.ap = ap
        self.n = nelem
        self.off = 0

    def reset(self):
        self.off = 0

    def alloc(self, shape, dt):
        esz = 4 if dt in (F32, U32, I32) else 2
        n = 1
        for s in shape[1:]:
            n *= s
        nb = n * esz
        nb = (nb + 31) // 32 * 32
        assert self.off * 2 + nb <= self.n * 2, "arena overflow: need %d have %d" % (self.off * 2 + nb, self.n * 2)
        v = self.ap[:, self.off:self.off + nb // 2]
        self.off += nb // 2
        if esz == 4:
            v = v.bitcast(dt)
        elif dt != BF16:
            v = v.bitcast(dt)
        v = v[:, 0:n]
        if len(shape) > 2:
            names = " ".join("d%d" % i for i in range(len(shape) - 1))
            kw = {"d%d" % i: shape[i + 1] for i in range(len(shape) - 2)}
            v = v.rearrange("p (%s) -> p %s" % (names, names), **kw)
        if shape[0] < 128:
            v = v[0:shape[0]]
        return v


class Rot:
    def __init__(self, arena, n, shape, dt, name):
        self.bufs = [(arena.alloc(shape, dt), T("%s%d" % (name, i))) for i in range(n)]
        self.i = 0

    def next(self):
        b = self.bufs[self.i % len(self.bufs)]
        self.i += 1
        return b


class K:
    def __init__(self, seqs, debug=False, phases="0AGBMPE"):
        self.seqs = tuple(seqs)
        self.NT = sum(seqs)
        self.NB = self.NT // 128
        self.debug = debug
        self.phases = phases
        self.nc = bass.Bass("TRN2", target_bir_lowering=False)
        self.em = Emitter()
        self.dbg_names = []

    def act(self, out, in_, func, R, W, **kw):
        self.em.op("act", lambda e: e.activation(out=out, in_=in_, func=func, **kw), R, W)

    def dve(self, fn, R, W):
        self.em.op("dve", fn, R, W)

    def pool(self, fn, R, W):
        self.em.op("pool", fn, R, W)

    def mm(self, out, lhsT, rhs, start, stop, R, W):
        self.em.op("pe", lambda e: e.matmul(out, lhsT=lhsT, rhs=rhs, start=start, stop=stop), R, W)

    def tr(self, out, in_, ident, R, W):
        self.em.op("pe", lambda e: e.transpose(out=out, in_=in_, identity=ident), R, W)

    def dma(self, out, in_, R, W, q="sp", slow=False):
        q = "sp"
        if slow:
            self.em.op(q, lambda e: e.dma_start(out=out, in_=in_, allow_slow_non_contiguous=True), R, W, dma=True)
        else:
            self.em.op(q, lambda e: e.dma_start(out=out, in_=in_), R, W, dma=True)

    def tt(self, out, in0, in1, op, R, W, eng="dve"):
        self.em.op(eng, lambda e: e.tensor_tensor(out=out, in0=in0, in1=in1, op=op), R, W)

    def ts(self, out, in0, s1, s2, op0, R, W, op1=None, eng="dve", **kw):
        if op1 is None:
            self.em.op(eng, lambda e: e.tensor_scalar(out=out, in0=in0, scalar1=s1, scalar2=s2, op0=op0, **kw), R, W)
        else:
            self.em.op(eng, lambda e: e.tensor_scalar(out=out, in0=in0, scalar1=s1, scalar2=s2, op0=op0, op1=op1, **kw), R, W)

    def cp(self, out, in_, R, W, eng="dve"):
        self.em.op(eng, lambda e: e.tensor_copy(out=out, in_=in_), R, W)

    def dram_in(self, name, shape, dt=F32):
        return self.nc.dram_tensor(name, list(shape), dt, kind="ExternalInput").ap()

    def scratch(self, name, shape, dt):
        kind = "ExternalOutput" if (self.debug and name in self.debug) else "Internal"
        if kind == "ExternalOutput":
            self.dbg_names.append(name)
        return self.nc.dram_tensor(name, list(shape), dt, kind=kind).ap()

    def build(self):
        nc, em = self.nc, self.em
        NT, NB = self.NT, self.NB
        I = {}
        I["x"] = self.dram_in("x", [NT, D])
        I["p"] = self.dram_in("p", [NT, 256])
        I["rel_bias"] = self.dram_in("rel_bias", [32, 4])
        I["norm1_g"] = self.dram_in("norm1_g", [D])
        I["w_in"] = self.dram_in("w_in", [D, IN_DIM])
        I["lambda_qk"] = self.dram_in("lambda_qk", [4, 64])
        I["da_norm_g"] = self.dram_in("da_norm_g", [128])
        I["gla_alpha_w"] = self.dram_in("gla_alpha_w", [2, 16, 256])
        I["gla_alpha_b"] = self.dram_in("gla_alpha_b", [2, 256])
        I["gla_norm_g"] = self.dram_in("gla_norm_g", [128])
        I["w_up_a"] = self.dram_in("w_up_a", [512, D])
        I["w_up_b"] = self.dram_in("w_up_b", [512, D])
        I["w_out"] = self.dram_in("w_out", [D, D])
        I["norm2_g"] = self.dram_in("norm2_g", [D])
        I["peer_w_q"] = self.dram_in("peer_w_q", [D, 2048])
        I["peer_sub_keys"] = self.dram_in("peer_sub_keys", [16, 128, 128])
        I["peer_u"] = self.dram_in("peer_u", [NE, D])
        I["peer_v"] = self.dram_in("peer_v", [NE, D])
        I["norm3_g"] = self.dram_in("norm3_g", [D])
        I["ple_w"] = self.dram_in("ple_w", [256, D])
        I["ple_gate_w"] = self.dram_in("ple_gate_w", [D, D])
        I["final_norm_g"] = self.dram_in("final_norm_g", [D])
        I["t5oh"] = self.dram_in("t5oh", [32, 1280])
        self.I = I
        self.y = nc.dram_tensor("y", [NT, D], F32, kind="ExternalOutput").ap()

        S = {}
        S["QT"] = self.scratch("QT", [4, 128, NT], BF16)
        S["KT"] = self.scratch("KT", [4, 128, NT], BF16)
        S["VA"] = self.scratch("VA", [NT, 512], BF16)
        S["GQ"] = self.scratch("GQ", [NB, 128, 512], BF16)
        S["GK"] = self.scratch("GK", [NB, 128, 512], BF16)
        S["GD"] = self.scratch("GD", [NB, 128, 512], BF16)
        S["GV"] = self.scratch("GV", [NT, 512], BF16)
        S["SR"] = self.scratch("SR", [NT, 512], BF16)
        S["SGT"] = self.scratch("SGT", [16, 128, NT], BF16)
        S["OAT"] = self.scratch("OAT", [4, 128, NT], BF16)
        S["OGT"] = self.scratch("OGT", [4, 128, NT], BF16)
        S["X1"] = self.scratch("X1", [NT, D], F32)
        S["H2T"] = self.scratch("H2T", [8, 128, NT], BF16)
        S["SL"] = self.scratch("SL", [3, 128, NT], F32)
        S["X2"] = self.scratch("X2", [NT, D], F32)
        S["UT"] = self.scratch("UT", [128, 128, 1024], BF16)
        S["VB"] = self.scratch("VB", [128, 128, 1024], BF16)
        S["BV"] = self.scratch("BV", [4, 1280], F32)
        self.S = S
        self.TS = {k: T("S_" + k) for k in S}

        with ExitStack() as es:
            self.es = es
            arena_t = es.enter_context(nc.sbuf_tensor("arena", [128, 94000], BF16))
            self.arena = Arena(arena_t[:, :], 94000)
            cons_t = es.enter_context(nc.sbuf_tensor("cons", [128, 5000], F32))
            self.cons = cons_t
            self.cons_off = 0
            self.banks = [es.enter_context(nc.psum_tensor("bank%d" % i, [128, 512], F32)) for i in range(8)]
            self.Tbank = [T("bank%d" % i) for i in range(8)]

            self.phase0()
            if "A" in self.phases:
                em.barrier()
                self.phaseA()
            if "G" in self.phases:
                em.barrier()
                self.phaseG()
            if "B" in self.phases:
                em.barrier()
                self.phaseB()
            if "M" in self.phases:
                em.barrier()
                self.phaseM()
            if "P" in self.phases:
                em.barrier()
                self.phaseP()
            if "E" in self.phases:
                em.barrier()
                self.phaseE()

            em.finalize()
            sems = {}
            for k in ["dma%d" % i for i in range(em.NRING)]:
                sems[k] = es.enter_context(nc.semaphore("s_" + k))
            for k in ["pe", "act", "dve", "pool"]:
                for ep in range(em.n_epochs(k)):
                    sems["%s_%d" % (k, ep)] = es.enter_context(nc.semaphore("s_%s_%d" % (k, ep)))
            with nc.Block() as block:
                @block.sync
                def _(e):
                    em.emit_engine("sp", e, sems, final_wait=True)

                @block.scalar
                def _(e):
                    em.emit_engine("act", e, sems)

                @block.vector
                def _(e):
                    em.emit_engine("dve", e, sems)

                @block.gpsimd
                def _(e):
                    em.emit_engine("pool", e, sems, final_wait=True)

                @block.tensor
                def _(e):
                    em.emit_engine("pe", e, sems)
        return nc

    def calloc(self, n):
        v = self.cons[:, self.cons_off:self.cons_off + n]
        self.cons_off += n
        assert self.cons_off <= 5000
        return v

    def phase0(self):
        nc, I = self.nc, self.I
        Tc = self.Tc = T("consts")
        self.identf = self.calloc(128)
        self.ident = self.calloc(64).bitcast(BF16)
        self.triF = self.calloc(128)
        self.triB = self.calloc(128)
        self.triSU = self.calloc(128)
        self.triSL = self.calloc(128)
        self.maskF = self.calloc(128)
        self.maskB = self.calloc(128)
        self.epsc = self.calloc(1)
        self.onec = self.calloc(1)
        po = lambda fn: self.pool(fn, [Tc], [Tc])
        po(lambda e: e.memset(self.identf, 0.0))
        po(lambda e: e.affine_select(out=self.identf, in_=self.identf, compare_op=ALU.not_equal, fill=1.0,
                                     base=0, pattern=[[-1, 128]], channel_multiplier=1))
        po(lambda e: e.tensor_copy(out=self.ident, in_=self.identf))

        def tri(dst, val, cmul, step, base, cmp):
            po(lambda e: e.memset(dst, val))
            po(lambda e: e.affine_select(out=dst, in_=dst, compare_op=cmp, fill=0.0, base=base,
                                         pattern=[[step, 128]], channel_multiplier=cmul))
        tri(self.triF, -1.0 / 16, -1, 1, 0, ALU.is_ge)
        tri(self.triB, -1.0 / 16, 1, -1, 0, ALU.is_ge)
        tri(self.triSU, -1.0 / 16, 1, -1, 0, ALU.is_gt)
        tri(self.triSL, -1.0 / 16, -1, 1, 0, ALU.is_gt)
        tri(self.maskF, 1.0, -1, 1, 0, ALU.is_ge)
        tri(self.maskB, 1.0, 1, -1, 0, ALU.is_gt)
        po(lambda e: e.memset(self.epsc, EPS))
        po(lambda e: e.memset(self.onec, 1.0))

    def load_cast_weight(self, dst, src_ap, nchunk, ncols, gvec, stage_rot, Tdst, col_piece=512):
        for c0 in range(0, ncols, col_piece):
            cw = min(col_piece, ncols - c0)
            st, Tst = stage_rot.next()
            stv = st[:, 0:nchunk, 0:cw]
            self.dma(stv, src_ap[:, c0:c0 + cw].rearrange("(c p) n -> p c n", p=128), [], [Tst])
            for c in range(nchunk):
                if gvec is not None:
                    self.ts(dst[:, c, c0:c0 + cw], stv[:, c, :], gvec[:, c:c + 1], None, ALU.mult, [Tst, self.Tc], [Tdst],
                            eng=("dve" if c % 2 == 0 else "pool"))
                else:
                    self.cp(dst[:, c, c0:c0 + cw], stv[:, c, :], [Tst], [Tdst], eng=("dve" if c % 2 == 0 else "pool"))

    def rmsnorm_T(self, xin, Txin, gain_none_out_bf, hT_dst, ThT, tok0, ntok_tile, scr):
        junk, Tjunk = scr["junk"]
        ss, Tss = scr["ss"].next()
        xn, Txn = scr["xn"].next()
        tpb, Ttp = scr["tp"]
        self.act(junk, xin, AF.Square, [Txin], [Tjunk, Tss], accum_out=ss)
        self.act(ss, ss, AF.Sqrt, [Tss], [Tss], scale=1.0 / D, bias=self.epsc)
        self.dve(lambda e: e.reciprocal(out=ss, in_=ss), [Tss], [Tss])
        self.ts(xn, xin, ss, None, ALU.mult, [Txin, Tss], [Txn])
        for c in range(8):
            self.tr(tpb[:, c * 128:(c + 1) * 128], xn[:, c * 128:(c + 1) * 128], self.ident, [Txn, self.Tc], [Ttp])
        self.act(hT_dst[:, :, tok0:tok0 + 128], tpb.rearrange("p (c t) -> p c t", c=8), AF.Copy, [Ttp], [ThT])

    def phaseA(self):
        nc, I, S, TS = self.nc, self.I, self.S, self.TS
        A = self.arena
        A.reset()
        NT = self.NT
        Tc = self.Tc
        win = A.alloc([128, 8, IN_DIM], BF16)
        Twin = T("win")
        g1 = A.alloc([128, 8], F32)
        Tg1 = T("g1")
        self.dma(g1, I["norm1_g"].rearrange("(c p) -> p c", p=128), [], [Tc], slow=True)
        stage = Rot(A, 2, [128, 8, 512], F32, "wstage")
        self.load_cast_weight(win, I["w_in"], 8, IN_DIM, g1, stage, Twin)
        awf = A.alloc([33, 512], F32)
        aw = A.alloc([33, 512], BF16)
        Taw = T("aw")
        self.pool(lambda e: e.memset(awf, 0.0), [], [Taw])
        self.dma(awf[0:16, 0:256], I["gla_alpha_w"][0], [], [Taw])
        self.dma(awf[16:32, 256:512], I["gla_alpha_w"][1], [], [Taw])
        self.dma(awf[32:33, :], I["gla_alpha_b"].rearrange("(o j) k -> o (j k)", o=1), [], [Taw])
        self.cp(aw, awf, [Taw], [Taw])

        xrot = Rot(A, 2, [128, D], F32, "xin")
        scr = {
            "junk": (A.alloc([128, D], F32), T("junk")),
            "ss": Rot(A, 2, [128, 1], F32, "ss"),
            "xn": Rot(A, 2, [128, D], BF16, "xn"),
            "tp": (self.banks[0][:, :].bitcast(BF16), self.Tbank[0]),
        }
        hTrot = Rot(A, 2, [128, 8, 512], BF16, "hT")
        fm_ps = [(self.banks[1][:, :], self.Tbank[1]), (self.banks[2][:, :], self.Tbank[2])]
        tm_ps = [(self.banks[3][:, :], self.Tbank[3]), (self.banks[4][:, :], self.Tbank[4])]
        z_ps = (self.banks[5][:, :], self.Tbank[5])
        d_ps = (self.banks[6][:, :], self.Tbank[6])
        bt_ps = (self.banks[7][:, :], self.Tbank[7])
        fmrot = Rot(A, 3, [128, 512], BF16, "fmout")
        qg_sb = A.alloc([128, 2, 512], F32)
        Tqg = T("qg_sb")
        kg_sb = A.alloc([128, 2, 512], F32)
        Tkg = T("kg_sb")
        lrT = A.alloc([33, 512], BF16)
        TlrT = T("lrT")
        self.pool(lambda e: e.memset(lrT[32:33, :], 1.0), [], [TlrT])
        tmrot = Rot(A, 3, [128, 512], BF16, "tmout")
        e1 = A.alloc([128, 512], F32)
        Te1 = T("e1")
        Lt = A.alloc([128, 512], F32)
        TL = T("L")
        kde = A.alloc([128, 512], F32)
        Tkde = T("kde")
        epos = A.alloc([128, 512], F32)
        Tep = T("epos")
        eneg = A.alloc([128, 512], F32)
        Ten = T("eneg")
        gqrot = Rot(A, 2, [128, 512], BF16, "gq")
        gkrot = Rot(A, 2, [128, 512], BF16, "gk")
        gdrot = Rot(A, 2, [128, 512], BF16, "gd")
        self.DEC = self.calloc(self.NB * 4)
        self.TDEC = T("DEC")

        fmi = [0]
        tmi = [0]

        def fm_proj(hT, ThT, col0, M, ntok):
            ps, Tps = fm_ps[fmi[0] % 2]
            fmi[0] += 1
            for c in range(8):
                self.mm(ps[0:M, 0:ntok], win[:, c, col0:col0 + M], hT[:, c, 0:ntok], c == 0, c == 7, [Twin, ThT], [Tps])
            return ps, Tps

        def tm_proj(hT, ThT, col0, ncols, t0):
            ps, Tps = tm_ps[tmi[0] % 2]
            tmi[0] += 1
            for c in range(8):
                self.mm(ps[:, 0:ncols], hT[:, c, t0:t0 + 128], win[:, c, col0:col0 + ncols], c == 0, c == 7, [ThT, Twin], [Tps])
            return ps, Tps

        tok = 0
        for S_len in self.seqs:
            for tile0 in range(0, S_len, 512):
                ntok = min(512, S_len - tile0)
                g0 = tok + tile0
                hT, ThT = hTrot.next()
                for b in range(ntok // 128):
                    xin, Txin = xrot.next()
                    self.dma(xin, I["x"][g0 + b * 128:g0 + (b + 1) * 128, :], [], [Txin])
                    self.rmsnorm_T(xin, Txin, None, hT, ThT, b * 128, ntok, scr)
                for h in range(4):
                    ps, Tps = fm_proj(hT, ThT, C_QA + h * 128, 128, ntok)
                    o, To = fmrot.next()
                    self.act(o[:, 0:ntok], ps[:, 0:ntok], AF.Copy, [Tps], [To], scale=0.125)
                    self.dma(S["QT"][h, :, g0:g0 + ntok], o[:, 0:ntok], [To], [TS["QT"]], q="pool")
                    ps, Tps = fm_proj(hT, ThT, C_KA + h * 128, 128, ntok)
                    o, To = fmrot.next()
                    self.act(o[:, 0:ntok], ps[:, 0:ntok], AF.Copy, [Tps], [To])
                    self.dma(S["KT"][h, :, g0:g0 + ntok], o[:, 0:ntok], [To], [TS["KT"]], q="pool")
                for c2 in range(2):
                    ps, Tps = fm_proj(hT, ThT, C_QG + c2 * 128, 128, ntok)
                    self.act(qg_sb[:, c2, 0:ntok], ps[:, 0:ntok], AF.Copy, [Tps], [Tqg], scale=0.125)
                    ps, Tps = fm_proj(hT, ThT, C_KG + c2 * 128, 128, ntok)
                    self.act(kg_sb[:, c2, 0:ntok], ps[:, 0:ntok], AF.Copy, [Tps], [Tkg])
                ps, Tps = fm_proj(hT, ThT, C_LR, 32, ntok)
                self.act(lrT[0:32, 0:ntok], ps[0:32, 0:ntok], AF.Copy, [Tps], [TlrT])
                for j in range(16):
                    ps, Tps = fm_proj(hT, ThT, C_GL + j * 128, 128, ntok)
                    o, To = fmrot.next()
                    self.act(o[:, 0:ntok], ps[:, 0:ntok], AF.Sigmoid, [Tps], [To])
                    self.dma(S["SGT"][j, :, g0:g0 + ntok], o[:, 0:ntok], [To], [TS["SGT"]], q="pool")
                for b in range(ntok // 128):
                    gb = (g0 + b * 128) // 128
                    r0 = g0 + b * 128
                    t0 = b * 128
                    ps, Tps = tm_proj(hT, ThT, C_VA, 512, t0)
                    o, To = tmrot.next()
                    self.cp(o, ps, [Tps], [To])
                    self.dma(S["VA"][r0:r0 + 128, :], o, [To], [TS["VA"]], q="pool")
                    ps, Tps = tm_proj(hT, ThT, C_VG, 512, t0)
                    o, To = tmrot.next()
                    self.cp(o, ps, [Tps], [To])
                    self.dma(S["GV"][r0:r0 + 128, :], o, [To], [TS["GV"]], q="pool")
                    ps, Tps = tm_proj(hT, ThT, C_RG, 512, t0)
                    o, To = tmrot.next()
                    self.act(o, ps, AF.Silu, [Tps], [To])
                    self.dma(S["SR"][r0:r0 + 128, :], o, [To], [TS["SR"]], q="pool")
                    zp, Tzp = z_ps
                    self.mm(zp, lrT[0:33, t0:t0 + 128], aw[0:33, :], True, True, [TlrT, Taw], [Tzp])
                    self.act(e1, zp, AF.Exp, [Tzp], [Te1], scale=-1.0)
                    self.act(Lt, e1, AF.Ln, [Te1], [TL], bias=self.onec)
                    dp, Tdp = d_ps
                    self.mm(dp[:, 0:256], self.triSU, Lt[:, 0:256], True, True, [Tc, TL], [Tdp])
                    self.mm(dp[:, 256:512], self.triSL, Lt[:, 256:512], True, True, [Tc, TL], [Tdp])
                    self.act(kde, dp, AF.Exp, [Tdp], [Tkde])
                    kps, Tkps = tm_proj(hT, ThT, C_KG, 256, t0)
                    gd, Tgd = gdrot.next()
                    self.tt(gd.rearrange("p (j k) -> p j k", j=2), kde.rearrange("p (j k) -> p j k", j=2),
                            kps[:, 0:256].unsqueeze(1).to_broadcast([128, 2, 256]), ALU.mult, [Tkde, Tkps], [Tgd])
                    self.dma(S["GD"][gb], gd, [Tgd], [TS["GD"]], q="pool")
                    bp, Tbp = bt_ps
                    for dr in range(2):
                        for c2 in range(2):
                            idx = dr * 2 + c2
                            self.mm(bp[:, idx * 128:(idx + 1) * 128], Lt[:, dr * 256 + c2 * 128: dr * 256 + (c2 + 1) * 128],
                                    self.triF if dr == 0 else self.triB, True, True, [TL, Tc], [Tbp])
                    self.act(epos, bp, AF.Exp, [Tbp], [Tep])
                    self.act(eneg, bp, AF.Exp, [Tbp], [Ten], scale=-1.0)
                    gq, Tgq = gqrot.next()
                    gk, Tgk = gkrot.next()
                    self.tt(gq.rearrange("p (j c t) -> p j c t", j=2, c=2), epos.rearrange("p (j c t) -> p j c t", j=2, c=2),
                            qg_sb[:, :, t0:t0 + 128].unsqueeze(1).to_broadcast([128, 2, 2, 128]), ALU.mult, [Tep, Tqg], [Tgq])
                    self.tt(gk.rearrange("p (j c t) -> p j c t", j=2, c=2), eneg.rearrange("p (j c t) -> p j c t", j=2, c=2),
                            kg_sb[:, :, t0:t0 + 128].unsqueeze(1).to_broadcast([128, 2, 2, 128]), ALU.mult, [Ten, Tkg], [Tgk],
                            eng="pool")
                    self.dma(S["GQ"][gb], gq, [Tgq], [TS["GQ"]], q="pool")
                    self.dma(S["GK"][gb], gk, [Tgk], [TS["GK"]], q="pool")
                    ev = epos.rearrange("p (i t) -> p i t", i=4)
                    self.cp(self.DEC[:, gb * 4:gb * 4 + 2], ev[:, 0:2, 127], [Tep], [self.TDEC])
                    self.cp(self.DEC[:, gb * 4 + 2:gb * 4 + 4], ev[:, 2:4, 0], [Tep], [self.TDEC])
            tok += S_len

    def phaseG(self):
        nc, I, S, TS = self.nc, self.I, self.S, self.TS
        A = self.arena
        A.reset()
        Tc = self.Tc
        maxS = max(self.seqs)
        OF = A.alloc([128, maxS // 128, 512], F32)
        TOF = T("OF")
        gnb = A.alloc([128, 128], F32)
        Tgnb = T("gnb")
        self.dma(gnb, I["gla_norm_g"].partition_broadcast(128), [], [Tgnb])
        qrot = Rot(A, 2, [128, 2, 128], BF16, "gq_in")
        krot = Rot(A, 2, [128, 2, 128], BF16, "gk_in")
        drot = Rot(A, 2, [128, 256], BF16, "gd_in")
        vrot = Rot(A, 2, [128, 512], BF16, "gv_in")
        srrot = Rot(A, 2, [128, 512], BF16, "sr_in")
        attrot = Rot(A, 2, [128, 4, 128], BF16, "att_sb")
        st_f = A.alloc([128, 2, 128], F32)
        st_b = A.alloc([128, 2, 128], BF16)
        Tst = T("state")
        Tstb = T("state_bf")
        tot = A.alloc([128, 512], F32)
        Ttot = T("tot")
        sq = A.alloc([128, 512], F32)
        Tsq = T("sq")
        ssq = A.alloc([128, 4], F32)
        Tssq = T("ssq")
        srg = A.alloc([128, 512], F32)
        Tsrg = T("srg")
        ogrot = Rot(A, 2, [128, 512], BF16, "og")
        ogTrot = Rot(A, 2, [128, 4, 128], BF16, "ogT")
        tpb, Ttp = self.banks[0][:, :].bitcast(BF16), self.Tbank[0]
        att_ps = [(self.banks[1][:, :], self.Tbank[1]), (self.banks[2][:, :], self.Tbank[2])]
        o_ps = [(self.banks[3][:, :], self.Tbank[3]), (self.banks[4][:, :], self.Tbank[4])]
        kv_ps = [(self.banks[5][:, :], self.Tbank[5]), (self.banks[6][:, :], self.Tbank[6])]
        it = 0
        tok = 0
        for S_len in self.seqs:
            nbs = S_len // 128
            gb0 = tok // 128
            for dr in range(2):
                self.dve(lambda e: e.memset(st_f, 0.0), [], [Tst])
                self.pool(lambda e: e.memset(st_b, 0.0), [], [Tstb])
                order = range(nbs) if dr == 0 else range(nbs - 1, -1, -1)
                for lb in order:
                    gb = gb0 + lb
                    r0 = gb * 128
                    q_in, Tq = qrot.next()
                    k_in, Tk = krot.next()
                    d_in, Td = drot.next()
                    v_in, Tv = vrot.next()
                    self.dma(q_in, S["GQ"][gb][:, dr * 256:(dr + 1) * 256].rearrange("p (c t) -> p c t", c=2), [TS["GQ"]], [Tq])
                    self.dma(k_in, S["GK"][gb][:, dr * 256:(dr + 1) * 256].rearrange("p (c t) -> p c t", c=2), [TS["GK"]], [Tk])
                    self.dma(d_in, S["GD"][gb][:, dr * 256:(dr + 1) * 256], [TS["GD"]], [Td])
                    self.dma(v_in, S["GV"][r0:r0 + 128, :], [TS["GV"]], [Tv])
                    if dr == 1:
                        sr_in, Tsr = srrot.next()
                        self.dma(sr_in, S["SR"][r0:r0 + 128, :], [TS["SR"]], [Tsr])
                    kps, Tkps = kv_ps[it % 2]
                    it += 1
                    for h in range(4):
                        c2, par = h // 2, h % 2
                        ph = par * 64
                        aps, Taps = att_ps[par]
                        self.mm(aps[:, c2 * 128:(c2 + 1) * 128], k_in[ph:ph + 64, c2, :], q_in[ph:ph + 64, c2, :], True, True, [Tk, Tq], [Taps])
                    att, Tatt = attrot.next()
                    mask = self.maskF if dr == 0 else self.maskB
                    attv = att.rearrange("p (c par) t -> p c par t", par=2)
                    for par in range(2):
                        aps, Taps = att_ps[par]
                        self.tt(attv[:, :, par, :], aps[:, 0:256].rearrange("p (c t) -> p c t", c=2),
                                mask.unsqueeze(1).to_broadcast([128, 2, 128]), ALU.mult, [Taps, Tc], [Tatt])
                    for h in range(4):
                        c2, par = h // 2, h % 2
                        ph = par * 64
                        ops, Tops = o_ps[par]
                        self.mm(ops[:, c2 * 128:(c2 + 1) * 128], att[:, h, :], v_in[:, h * 128:(h + 1) * 128], True, False, [Tatt, Tv], [Tops])
                        self.mm(ops[:, c2 * 128:(c2 + 1) * 128], q_in[ph:ph + 64, c2, :], st_b[ph:ph + 64, c2, :], False, True, [Tq, Tstb], [Tops])
                    for c2 in range(2):
                        self.mm(kps[:, c2 * 256:(c2 + 1) * 256], d_in[:, c2 * 128:(c2 + 1) * 128], v_in[:, c2 * 256:(c2 + 1) * 256], True, True, [Td, Tv], [Tkps])
                    for h in range(4):
                        c2, hh = h // 2, h % 2
                        rows = slice(hh * 64, (hh + 1) * 64)
                        col = gb * 4 + dr * 2 + c2
                        self._stt(st_f[rows, c2, :], st_f[rows, c2, :], self.DEC[rows, col:col + 1],
                                  kps[rows, c2 * 256 + hh * 128: c2 * 256 + (hh + 1) * 128], ALU.mult, ALU.add, [Tst, Tkps, self.TDEC], [Tst])
                    self.cp(st_b, st_f, [Tst], [Tstb], eng="pool")
                    OFv = OF[:, lb, :].rearrange("p (c par v) -> p c par v", c=2, par=2)
                    totv = tot.rearrange("p (c par v) -> p c par v", c=2, par=2)
                    for par in range(2):
                        ops, Tops = o_ps[par]
                        opv = ops[:, 0:256].rearrange("p (c v) -> p c v", c=2)
                        if dr == 0:
                            self.act(OFv[:, :, par, :], opv, AF.Copy, [Tops], [TOF])
                        else:
                            self.tt(totv[:, :, par, :], opv, OFv[:, :, par, :], ALU.add, [Tops, TOF], [Ttot])
                    if dr == 1:
                        self.act(sq, tot, AF.Square, [Ttot], [Tsq])
                        self.dve(lambda e, sq=sq, ssq=ssq: e.tensor_reduce(out=ssq, in_=sq.rearrange("p (h v) -> p h v", h=4), axis=AX.X, op=ALU.add), [Tsq], [Tssq])
                        self.act(ssq, ssq, AF.Sqrt, [Tssq], [Tssq], scale=1.0 / 128, bias=self.epsc)
                        self.dve(lambda e: e.reciprocal(out=ssq, in_=ssq), [Tssq], [Tssq])
                        self.tt(srg.rearrange("p (h v) -> p h v", h=4), sr_in.rearrange("p (h v) -> p h v", h=4),
                                gnb.unsqueeze(1).to_broadcast([128, 4, 128]), ALU.mult, [Tsr, Tgnb], [Tsrg], eng="pool")
                        self.tt(tot.rearrange("p (h v) -> p h v", h=4), tot.rearrange("p (h v) -> p h v", h=4),
                                ssq.unsqueeze(2).to_broadcast([128, 4, 128]), ALU.mult, [Ttot, Tssq], [Ttot])
                        og, Tog = ogrot.next()
                        self.tt(og, tot, srg, ALU.mult, [Ttot, Tsrg], [Tog])
                        for h in range(4):
                            self.tr(tpb[:, h * 128:(h + 1) * 128], og[:, h * 128:(h + 1) * 128], self.ident, [Tog, Tc], [Ttp])
                        ogT, TogT = ogTrot.next()
                        self.act(ogT, tpb[:, 0:512].rearrange("p (h t) -> p h t", h=4), AF.Copy, [Ttp], [TogT])
                        self.dma(S["OGT"][:, :, r0:r0 + 128].rearrange("h p t -> p h t"), ogT, [TogT], [TS["OGT"]])
            tok += S_len

    def _stt(self, out, in0, scalar, in1, op0, op1, R, W, eng="dve"):
        self.em.op(eng, lambda e: e.scalar_tensor_tensor(out=out, in0=in0, scalar=scalar, in1=in1, op0=op0, op1=op1), R, W)

    def phaseB(self):
        nc, I, S, TS = self.nc, self.I, self.S, self.TS
        A = self.arena
        A.reset()
        Tc = self.Tc
        maxS = max(self.seqs)
        misc_ps, Tmisc = self.banks[7][:, :], self.Tbank[7]
        rb = A.alloc([32, 4], F32)
        oh = A.alloc([32, 1280], F32)
        Tset = T("setupB")
        self.dma(rb, I["rel_bias"], [], [Tset])
        self.dma(oh, I["t5oh"], [], [Tset])
        bv_sb = A.alloc([4, 1280], F32)
        for (a, b) in ((0, 512), (512, 1024), (1024, 1280)):
            self.mm(misc_ps[0:4, 0:b - a], rb, oh[:, a:b], True, True, [Tset], [Tmisc])
            self.cp(bv_sb[:, a:b], misc_ps[0:4, 0:b - a], [Tmisc], [Tset])
        self.dma(S["BV"], bv_sb, [Tset], [TS["BV"]])
        TBf = A.alloc([128, 4, 1154], F32)
        TB = A.alloc([128, 4, 1154], BF16)
        TTB = T("TB")
        for h in range(4):
            src = bass.AP(S["BV"].tensor, h * 1280, [[1, 128], [1, 1153]])
            self.dma(TBf[:, h, 0:1153], src, [TS["BV"]], [TTB])
        self.cp(TB[:, :, 0:1153], TBf[:, :, 0:1153], [TTB], [TTB])
        Jf = A.alloc([128, 128], F32)
        Jb = A.alloc([128, 128], BF16)
        self.pool(lambda e: e.memset(Jf, 0.0), [], [TTB])
        self.pool(lambda e: e.affine_select(out=Jf, in_=Jf, compare_op=ALU.not_equal, fill=1.0, base=-127,
                                            pattern=[[1, 128]], channel_multiplier=1), [TTB], [TTB])
        self.pool(lambda e: e.tensor_copy(out=Jb, in_=Jf), [TTB], [TTB])
        cfar = A.alloc([128, 4, 2], F32)
        Tcf = T("cfar")
        for h in range(4):
            self.dma(cfar[:, h, 0:1], S["BV"][h:h + 1, 1279:1280].partition_broadcast(128), [TS["BV"]], [Tcf])
            self.dma(cfar[:, h, 1:2], S["BV"][h:h + 1, 1:2].partition_broadcast(128), [TS["BV"]], [Tcf])
        lq = A.alloc([1, 256], F32)
        lp = A.alloc([1, 128], F32)
        l2 = A.alloc([1, 2], F32)
        Tl = T("lam")
        self.dma(lq, I["lambda_qk"].rearrange("(o a) d -> o (a d)", o=1), [], [Tl])
        lqv = lq.rearrange("p (a b d) -> p a b d", a=2, b=2)
        self.tt(lp.rearrange("p (a d) -> p a d", a=2), lqv[:, :, 0, :], lqv[:, :, 1, :], ALU.mult, [Tl], [Tl])
        self.dve(lambda e: e.tensor_reduce(out=l2, in_=lp.rearrange("p (a d) -> p a d", a=2), axis=AX.X, op=ALU.add), [Tl], [Tl])
        self.act(l2, l2, AF.Exp, [Tl], [Tl])
        lam_init = 0.8 - 0.6 * math.exp(-0.3 * 0)
        self.tt(l2[:, 0:1], l2[:, 1:2], l2[:, 0:1], ALU.subtract, [Tl], [Tl])
        self.ts(l2[:, 0:1], l2[:, 0:1], -lam_init, None, ALU.add, [Tl], [Tl])
        ones1 = A.alloc([1, 128], F32)
        self.dve(lambda e: e.memset(ones1, 1.0), [], [Tl])
        neglam = A.alloc([128, 1], F32)
        self.mm(misc_ps[:, 0:2], ones1, l2[:, 0:2], True, True, [Tl], [Tmisc])
        self.cp(neglam, misc_ps[:, 0:1], [Tmisc], [Tl])
        gab = A.alloc([128, 128], F32)
        self.dma(gab, I["da_norm_g"].partition_broadcast(128), [], [Tl])
        self.ts(gab, gab, 1.0 - lam_init, None, ALU.mult, [Tl], [Tl])

        ktrot = Rot(A, 2, [128, maxS], BF16, "kt")
        vxrot = Rot(A, 2, [128, maxS // 128, 132], BF16, "vx")
        for (vb, Tv) in vxrot.bufs:
            self.pool(lambda e, vb=vb: e.memset(vb, 1.0), [], [Tv])
        qrot = Rot(A, 2, [128, 512], BF16, "q")
        ptrot = Rot(A, 3, [128, 512], BF16, "pt")
        osb = [(A.alloc([128, 128], F32), T("osb%d" % i)) for i in range(4)]
        o2rot = Rot(A, 2, [128, 128], F32, "o2")
        obrot = Rot(A, 2, [128, 128], BF16, "ob")
        rrot = Rot(A, 4, [128, 2], F32, "rz")
        junk = A.alloc([128, 128], F32)
        Tjunk = T("junkB")
        oaTrot = Rot(A, 2, [128, 512], BF16, "oaT")
        tpb, Ttp = self.banks[0][:, :].bitcast(BF16), self.Tbank[0]
        lg_ps = [(self.banks[1][:, :], self.Tbank[1]), (self.banks[2][:, :], self.Tbank[2])]
        acc_ps = [(self.banks[3 + i][:, :], self.Tbank[3 + i]) for i in range(4)]
        li = 0
        tok = 0
        for S_len in self.seqs:
            nkb = S_len // 128
            for h in range(4):
                kt, Tkt = ktrot.next()
                vx, Tvx = vxrot.next()
                self.dma(kt[:, 0:S_len], S["KT"][h, :, tok:tok + S_len], [TS["KT"]], [Tkt])
                self.dma(vx[:, 0:nkb, 0:128], S["VA"][tok:tok + S_len, h * 128:(h + 1) * 128].rearrange("(kb p) v -> p kb v", p=128),
                         [TS["VA"]], [Tvx])
                for q0 in range(0, S_len, 512):
                    nq = min(512, S_len - q0)
                    nqb = nq // 128
                    qt, Tq = qrot.next()
                    self.dma(qt[:, 0:nq], S["QT"][h, :, tok + q0:tok + q0 + nq], [TS["QT"]], [Tq])
                    oaT, ToaT = oaTrot.next()
                    for c in range(2):
                        pr = slice(c * 64, (c + 1) * 64)
                        for kb in range(nkb):
                            k0 = kb * 128
                            delta = k0 - q0
                            near = (-256 < delta < 640)
                            lg, Tlg = lg_ps[li % 2]
                            li += 1
                            self.mm(lg[:, 0:nq], kt[pr, k0:k0 + 128], qt[pr, 0:nq], True, not near, [Tkt, Tq], [Tlg])
                            pt, Tpt = ptrot.next()
                            if near:
                                st = 512 - delta + 1
                                self.mm(lg[:, 0:nq], Jb, TB[:, h, st:st + nq], False, True, [Tc, TTB], [Tlg])
                                self.act(pt[:, 0:nq], lg[:, 0:nq], AF.Exp, [Tlg], [Tpt])
                            else:
                                self.act(pt[:, 0:nq], lg[:, 0:nq], AF.Exp, [Tlg, Tcf], [Tpt],
                                         bias=cfar[:, h, (0 if delta < 0 else 1):(1 if delta < 0 else 2)])
                            for qb in range(nqb):
                                ac, Tac = acc_ps[qb]
                                self.mm(ac[:, 0:129], pt[:, qb * 128:(qb + 1) * 128], vx[:, kb, 0:129], kb == 0, kb == nkb - 1, [Tpt, Tvx], [Tac])
                        for qb in range(nqb):
                            ac, Tac = acc_ps[qb]
                            ob_, Tob_ = osb[qb]
                            rz, Trz = rrot.next()
                            self.dve(lambda e, rz=rz, ac=ac: e.reciprocal(out=rz[:, 0:1], in_=ac[:, 128:129]), [Tac], [Trz])
                            if c == 0:
                                self.ts(ob_, ac[:, 0:128], rz[:, 0:1], None, ALU.mult, [Tac, Trz], [Tob_])
                            else:
                                self.ts(rz[:, 0:1], rz[:, 0:1], neglam[:, 0:1], None, ALU.mult, [Trz, Tl], [Trz])
                                o2, To2 = o2rot.next()
                                self._stt(o2, ac[:, 0:128], rz[:, 0:1], ob_, ALU.mult, ALU.add, [Tac, Trz, Tob_], [To2])
                                self.act(junk, o2, AF.Square, [To2], [Tjunk, Trz], accum_out=rz[:, 1:2])
                                self.act(rz[:, 1:2], rz[:, 1:2], AF.Sqrt, [Trz], [Trz], scale=1.0 / 128, bias=self.epsc)
                                self.dve(lambda e, rz=rz: e.reciprocal(out=rz[:, 1:2], in_=rz[:, 1:2]), [Trz], [Trz])
                                ob, Tob = obrot.next()
                                self._stt(ob, o2, rz[:, 1:2], gab, ALU.mult, ALU.mult, [To2, Trz, Tl], [Tob])
                                self.tr(tpb[:, qb * 128:(qb + 1) * 128], ob, self.ident, [Tob, Tc], [Ttp])
                    self.act(oaT[:, 0:nq], tpb[:, 0:nq], AF.Copy, [Ttp], [ToaT])
                    self.dma(S["OAT"][h, :, tok + q0:tok + q0 + nq], oaT[:, 0:nq], [ToaT], [TS["OAT"]])
            tok += S_len

    def phaseM(self):
        nc, I, S, TS = self.nc, self.I, self.S, self.TS
        A = self.arena
        A.reset()
        Tc = self.Tc
        stage = Rot(A, 2, [128, 8, 512], F32, "wstage")
        wua = A.alloc([128, 4, D], BF16)
        wub = A.alloc([128, 4, D], BF16)
        wout = A.alloc([128, 8, D], BF16)
        Tw = T("wM")
        self.load_cast_weight(wua, I["w_up_a"], 4, D, None, stage, Tw)
        self.load_cast_weight(wub, I["w_up_b"], 4, D, None, stage, Tw)
        self.load_cast_weight(wout, I["w_out"], 8, D, None, stage, Tw)
        oarot = Rot(A, 2, [128, 4, 512], BF16, "oat")
        ogrot = Rot(A, 2, [128, 4, 512], BF16, "ogt")
        sgrot = Rot(A, 2, [128, 16, 512], BF16, "sgt")
        m1 = A.alloc([128, 512], F32)
        m2 = A.alloc([128, 512], F32)
        Tm1, Tm2 = T("m1"), T("m2")
        mTrot = Rot(A, 2, [128, 8, 512], BF16, "mT")
        xrot = Rot(A, 2, [128, D], F32, "xM")
        orot = Rot(A, 2, [128, D], F32, "oM")
        ya_ps = [(self.banks[0][:, :], self.Tbank[0]), (self.banks[1][:, :], self.Tbank[1])]
        yb_ps = [(self.banks[2][:, :], self.Tbank[2]), (self.banks[3][:, :], self.Tbank[3])]
        out_ps = [((self.banks[4][:, :], self.banks[5][:, :]), self.Tbank[4]), ((self.banks[6][:, :], self.banks[7][:, :]), self.Tbank[6])]
        yi = 0
        oi = 0
        for g0 in range(0, self.NT, 512):
            ntok = min(512, self.NT - g0)
            oat, Toat = oarot.next()
            ogt, Togt = ogrot.next()
            sgt, Tsgt = sgrot.next()
            self.dma(oat[:, :, 0:ntok], S["OAT"][:, :, g0:g0 + ntok].rearrange("h p t -> p h t"), [TS["OAT"]], [Toat])
            self.dma(ogt[:, :, 0:ntok], S["OGT"][:, :, g0:g0 + ntok].rearrange("h p t -> p h t"), [TS["OGT"]], [Togt])
            self.dma(sgt[:, :, 0:ntok], S["SGT"][:, :, g0:g0 + ntok].rearrange("j p t -> p j t"), [TS["SGT"]], [Tsgt])
            mT, TmT = mTrot.next()
            for j in range(8):
                pa, Tpa = ya_ps[yi % 2]
                pb, Tpb = yb_ps[yi % 2]
                yi += 1
                for f in range(4):
                    self.mm(pa[:, 0:ntok], wua[:, f, j * 128:(j + 1) * 128], oat[:, f, 0:ntok], f == 0, f == 3, [Tw, Toat], [Tpa])
                for f in range(4):
                    self.mm(pb[:, 0:ntok], wub[:, f, j * 128:(j + 1) * 128], ogt[:, f, 0:ntok], f == 0, f == 3, [Tw, Togt], [Tpb])
                self.tt(m1[:, 0:ntok], pa[:, 0:ntok], sgt[:, j, 0:ntok], ALU.mult, [Tpa, Tsgt], [Tm1])
                self.tt(m2[:, 0:ntok], pb[:, 0:ntok], sgt[:, 8 + j, 0:ntok], ALU.mult, [Tpb, Tsgt], [Tm2])
                self.tt(mT[:, j, 0:ntok], m1[:, 0:ntok], m2[:, 0:ntok], ALU.add, [Tm1, Tm2], [TmT], eng="pool")
            for b in range(ntok // 128):
                r0 = g0 + b * 128
                xin, Txin = xrot.next()
                self.dma(xin, I["x"][r0:r0 + 128, :], [], [Txin])
                (p0, p1), Tp = out_ps[oi % 2]
                oi += 1
                for half, pp in enumerate((p0, p1)):
                    for j in range(8):
                        self.mm(pp, mT[:, j, b * 128:(b + 1) * 128], wout[:, j, half * 512:(half + 1) * 512], j == 0, j == 7, [TmT, Tw], [Tp])
                xo, Txo = orot.next()
                self.tt(xo[:, 0:512], p0, xin[:, 0:512], ALU.add, [Tp, Txin], [Txo])
                self.tt(xo[:, 512:1024], p1, xin[:, 512:1024], ALU.add, [Tp, Txin], [Txo])
                self.dma(S["X1"][r0:r0 + 128, :], xo, [Txo], [TS["X1"]])

    def phaseP(self):
        self.phaseP0()
        self.em.barrier()
        self.phaseP1()
        self.em.barrier()
        self.phaseP2()

    def phaseP0(self):
        nc, I, S, TS = self.nc, self.I, self.S, self.TS
        A = self.arena
        A.reset()
        Tc = self.Tc
        urot = Rot(A, 2, [128, D], F32, "u_in")
        ubrot = Rot(A, 2, [128, D], BF16, "u_bf")
        utrot = Rot(A, 2, [128, D], BF16, "uT")
        vrot = Rot(A, 2, [128, D], F32, "v_in")
        vbrot = Rot(A, 2, [128, D], BF16, "v_bf")
        tps = [(self.banks[0][:, :].bitcast(BF16), self.Tbank[0]), (self.banks[1][:, :].bitcast(BF16), self.Tbank[1])]
        for i in range(128):
            u_in, Tu = urot.next()
            self.dma(u_in, I["peer_u"][i * 128:(i + 1) * 128, :], [], [Tu])
            u_bf, Tub = ubrot.next()
            self.cp(u_bf, u_in, [Tu], [Tub])
            tp, Ttp = tps[i % 2]
            for c in range(8):
                self.tr(tp[:, c * 128:(c + 1) * 128], u_bf[:, c * 128:(c + 1) * 128], self.ident, [Tub, Tc], [Ttp])
            uT, TuT = utrot.next()
            self.act(uT, tp, AF.Copy, [Ttp], [TuT])
            self.dma(S["UT"][i], uT, [TuT], [TS["UT"]])
            v_in, Tv = vrot.next()
            self.dma(v_in, I["peer_v"][i * 128:(i + 1) * 128, :], [], [Tv])
            v_bf, Tvb = vbrot.next()
            self.cp(v_bf, v_in, [Tv], [Tvb], eng="pool")
            self.dma(S["VB"][i], v_bf, [Tvb], [TS["VB"]])

    def phaseP1(self):
        nc, I, S, TS = self.nc, self.I, self.S, self.TS
        A = self.arena
        A.reset()
        Tc = self.Tc
        NT = self.NT
        stage = Rot(A, 2, [128, 8, 512], F32, "wstage")
        g2 = A.alloc([128, 8], F32)
        Tw = T("wP1")
        self.dma(g2, I["norm2_g"].rearrange("(c p) -> p c", p=128), [], [Tc], slow=True)
        wq = A.alloc([128, 8, 2048], BF16)
        self.load_cast_weight(wq, I["peer_w_q"], 8, 2048, g2, stage, Tw)
        skf = A.alloc([128, 16, 128], F32)
        skb = A.alloc([128, 16, 128], BF16)
        skT = A.alloc([128, 16, 128], BF16)
        self.dma(skf, I["peer_sub_keys"].rearrange("g n d -> n g d"), [], [Tw])
        self.cp(skb, skf, [Tw], [Tw])
        tpb0, Ttp0 = self.banks[0][:, :].bitcast(BF16), self.Tbank[0]
        for half in range(2):
            for g in range(8):
                self.tr(tpb0[:, g * 128:(g + 1) * 128], skb[:, half * 8 + g, :], self.ident, [Tw, Tc], [Ttp0])
            self.cp(skT[:, half * 8:(half + 1) * 8, :], tpb0.rearrange("p (g n) -> p g n", g=8), [Ttp0], [Tw])
        io_i = A.alloc([128, 128], I32)
        self.iota128 = self.calloc(128)
        self.pool(lambda e: e.iota(io_i, pattern=[[1, 128]], base=0, channel_multiplier=0), [], [Tc])
        self.pool(lambda e: e.tensor_copy(out=self.iota128, in_=io_i), [Tc], [Tc])
        iota16 = self.iota128[:, 0:16]
        xrot = Rot(A, 2, [128, D], F32, "x1in")
        scr = {
            "junk": (A.alloc([128, D], F32), T("junkP")),
            "ss": Rot(A, 2, [128, 1], F32, "ssP"),
            "xn": Rot(A, 2, [128, D], BF16, "xnP"),
            "tp": (self.banks[0][:, :].bitcast(BF16), self.Tbank[0]),
        }
        hTrot = Rot(A, 2, [128, 8, 512], BF16, "h2T")
        qT_sb = A.alloc([128, 16, 512], BF16)
        TqT = T("qT_sb")
        q_ps = [(self.banks[1][:, :], self.Tbank[1]), (self.banks[2][:, :], self.Tbank[2])]
        s_ps = [(self.banks[3 + i][:, :], self.Tbank[3 + i]) for i in range(4)]
        t_ps, Tt_ps = self.banks[7][:, :], self.Tbank[7]
        ssb_rot = Rot(A, 2, [128, 2048], F32, "s_sb")
        v16 = A.alloc([128, 16, 16], F32)
        i16 = A.alloc([128, 16, 16], U32)
        Tv16, Ti16 = T("v16"), T("i16")
        s2 = A.alloc([128, 128], F32)
        Ts2 = T("s2")
        cand = A.alloc([128, 8, 256], F32)
        Tcand = T("cand")
        c2b = A.alloc([128, 256], F32)
        Tc2 = T("c2b")
        tv = A.alloc([128, 8, 16], F32)
        pos = A.alloc([128, 8, 16], U32)
        Ttv, Tpos = T("tv"), T("pos")
        ex = A.alloc([128, 8, 16], F32)
        Tex = T("ex")
        zz = A.alloc([128, 8], F32)
        Tzz = T("zz")
        au = A.alloc([128, 8, 16], U32)
        bu = A.alloc([128, 8, 16], U32)
        af = A.alloc([128, 8, 16], F32)
        bf = A.alloc([128, 8, 16], F32)
        I1f = A.alloc([128, 8, 16], F32)
        I2f = A.alloc([128, 8, 16], F32)
        Tdec = T("decode")
        eq = A.alloc([128, 8, 16, 16], F32)
        Teq = T("eq")
        tabs = A.alloc([128, 3, 128], F32)
        Ttabs = T("tabs")
        slTrot = Rot(A, 2, [128, 3, 128], F32, "slT")
        qi = 0
        for g0 in range(0, NT, 512):
            ntok = min(512, NT - g0)
            hT, ThT = hTrot.next()
            for b in range(ntok // 128):
                xin, Txin = xrot.next()
                self.dma(xin, S["X1"][g0 + b * 128:g0 + (b + 1) * 128, :], [TS["X1"]], [Txin])
                self.rmsnorm_T(xin, Txin, None, hT, ThT, b * 128, ntok, scr)
            self.dma(S["H2T"][:, :, g0:g0 + ntok].rearrange("c p t -> p c t"), hT[:, :, 0:ntok], [ThT], [TS["H2T"]])
            for hc in range(16):
                ps, Tps = q_ps[qi % 2]
                qi += 1
                for c in range(8):
                    self.mm(ps[:, 0:ntok], wq[:, c, hc * 128:(hc + 1) * 128], hT[:, c, 0:ntok], c == 0, c == 7, [Tw, ThT], [Tps])
                self.act(qT_sb[:, hc, 0:ntok], ps[:, 0:ntok], AF.Copy, [Tps], [TqT])
            for b in range(ntok // 128):
                r0 = g0 + b * 128
                for hc in range(16):
                    bk, Tbk = s_ps[hc // 4]
                    self.mm(bk[:, (hc % 4) * 128:(hc % 4 + 1) * 128], qT_sb[:, hc, b * 128:(b + 1) * 128], skT[:, hc, :], True, True, [TqT, Tw], [Tbk])
                s_sb, Tssb = ssb_rot.next()
                for q in range(4):
                    bk, Tbk = s_ps[q]
                    self.act(s_sb[:, q * 512:(q + 1) * 512], bk, AF.Copy, [Tbk], [Tssb])
                for g in range(16):
                    sc = s_sb[:, g * 128:(g + 1) * 128]
                    self._top16(sc, Tssb, s2, Ts2, v16[:, g, :], Tv16, i16[:, g, :], Ti16)
                v16v = v16.rearrange("p (h c) k -> p h c k", c=2)
                self.tt(cand.rearrange("p h (a b) -> p h a b", a=16),
                        v16v[:, :, 0, :].unsqueeze(3).to_broadcast([128, 8, 16, 16]),
                        v16v[:, :, 1, :].unsqueeze(2).to_broadcast([128, 8, 16, 16]), ALU.add, [Tv16], [Tcand])
                for h in range(8):
                    self._top16(cand[:, h, :], Tcand, c2b, Tc2, tv[:, h, :], Ttv, pos[:, h, :], Tpos)
                self.tt(ex, tv, tv[:, :, 0:1].to_broadcast([128, 8, 16]), ALU.subtract, [Ttv], [Tex])
                self.act(ex, ex, AF.Exp, [Tex], [Tex])
                self.dve(lambda e: e.tensor_reduce(out=zz, in_=ex, axis=AX.X, op=ALU.add), [Tex], [Tzz])
                self.dve(lambda e: e.reciprocal(out=zz, in_=zz), [Tzz], [Tzz])
                self.tt(tabs[:, 2, :].rearrange("p (h k) -> p h k", h=8), ex, zz.unsqueeze(2).to_broadcast([128, 8, 16]), ALU.mult,
                        [Tex, Tzz], [Ttabs])
                self.dve(lambda e: e.tensor_single_scalar(out=au, in_=pos, scalar=4, op=ALU.logical_shift_right), [Tpos], [Tdec])
                self.dve(lambda e: e.tensor_single_scalar(out=bu, in_=pos, scalar=15, op=ALU.bitwise_and), [Tpos], [Tdec])
                self.cp(af, au, [Tdec], [Tdec])
                self.cp(bf, bu, [Tdec], [Tdec])
                i16v = i16.rearrange("p (h c) k -> p h c k", c=2)
                self.cp(I1f, i16v[:, :, 0, :], [Ti16], [Tdec])
                self.cp(I2f, i16v[:, :, 1, :], [Ti16], [Tdec])
                for (sel, tabf, j) in ((af, I1f, 0), (bf, I2f, 1)):
                    self.tt(eq, iota16.unsqueeze(1).unsqueeze(1).to_broadcast([128, 8, 16, 16]),
                            sel.unsqueeze(3).to_broadcast([128, 8, 16, 16]), ALU.is_equal, [Tc, Tdec], [Teq])
                    self.tt(eq, eq, tabf.unsqueeze(2).to_broadcast([128, 8, 16, 16]), ALU.mult, [Teq, Tdec], [Teq])
                    self.dve(lambda e, j=j: e.tensor_reduce(out=tabs[:, j, :].rearrange("p (h k) -> p h k", h=8), in_=eq, axis=AX.X, op=ALU.add),
                             [Teq], [Ttabs])
                for j in range(3):
                    self.tr(t_ps[:, j * 128:(j + 1) * 128], tabs[:, j, :], self.identf, [Ttabs, Tc], [Tt_ps])
                slT, TslT = slTrot.next()
                self.cp(slT, t_ps[:, 0:384].rearrange("p (j t) -> p j t", j=3), [Tt_ps], [TslT])
                self.dma(S["SL"][:, :, r0:r0 + 128].rearrange("j p t -> p j t"), slT, [TslT], [TS["SL"]])

    def _top16(self, src, Tsrc, tmp, Ttmp, vout, Tvout, iout, Tiout):
        self.dve(lambda e: e.max(out=vout[:, 0:8], in_=src), [Tsrc], [Tvout])
        self.dve(lambda e: e.max_index(out=iout[:, 0:8], in_max=vout[:, 0:8], in_values=src), [Tsrc, Tvout], [Tiout])
        self.dve(lambda e: e.match_replace(out=tmp, in_to_replace=vout[:, 0:8], in_values=src, imm_value=-1e30), [Tsrc, Tvout], [Ttmp])
        self.dve(lambda e: e.max(out=vout[:, 8:16], in_=tmp), [Ttmp], [Tvout])
        self.dve(lambda e: e.max_index(out=iout[:, 8:16], in_max=vout[:, 8:16], in_values=tmp), [Ttmp, Tvout], [Tiout])

    def phaseP2(self):
        nc, I, S, TS = self.nc, self.I, self.S, self.TS
        A = self.arena
        A.reset()
        Tc = self.Tc
        NT = self.NT
        TT = 256
        Gsb = A.alloc([128, 128, TT], BF16)
        TG = T("Gsb")
        utrot = Rot(A, 2, [128, 8, 1024], BF16, "ut")
        vbrot = Rot(A, 2, [128, 8, 1024], BF16, "vb")
        wrot = Rot(A, 2, [128, 16, 128], BF16, "Wt")
        yrot = Rot(A, 2, [128, 16, 128], BF16, "Yt")
        hrot = Rot(A, 2, [128, 8, TT], BF16, "h2t")
        slrot = Rot(A, 2, [128, 3, TT], F32, "sl")
        agrot = Rot(A, 3, [128, TT], BF16, "ag")
        garot = Rot(A, 3, [128, TT], BF16, "ga")
        xrot = Rot(A, 2, [128, D], F32, "x1p")
        orot = Rot(A, 2, [128, D], F32, "x2p")
        out_ps = [(self.banks[i][:, :], self.Tbank[i]) for i in range(4)]
        a_ps = [(self.banks[4][:, :], self.Tbank[4]), (self.banks[5][:, :], self.Tbank[5])]
        g_ps = [(self.banks[6][:, :], self.Tbank[6]), (self.banks[7][:, :], self.Tbank[7])]
        iota_b = self.iota128.unsqueeze(1).to_broadcast([128, 16, 128])
        ai = 0
        gi = 0
        for g0 in range(0, NT, TT):
            nt = min(TT, NT - g0)
            h2t, Th = hrot.next()
            sl, Tsl = slrot.next()
            self.dma(h2t[:, :, 0:nt], S["H2T"][:, :, g0:g0 + nt].rearrange("c p t -> p c t"), [TS["H2T"]], [Th])
            self.dma(sl[:, :, 0:nt], S["SL"][:, :, g0:g0 + nt].rearrange("j p t -> p j t"), [TS["SL"]], [Tsl])
            for t0 in range(0, nt, 16):
                Wt, TW = wrot.next()
                Yt, TY = yrot.next()
                self.tt(Yt, iota_b, sl[:, 1, t0:t0 + 16].unsqueeze(2).to_broadcast([128, 16, 128]), ALU.is_equal, [Tc, Tsl], [TY])
                self.tt(Wt, iota_b, sl[:, 0, t0:t0 + 16].unsqueeze(2).to_broadcast([128, 16, 128]), ALU.is_equal, [Tc, Tsl], [TW])
                self.tt(Wt, Wt, sl[:, 2, t0:t0 + 16].unsqueeze(2).to_broadcast([128, 16, 128]), ALU.mult, [TW, Tsl], [TW])
                for q4 in range(4):
                    gp, Tgp = g_ps[gi % 2]
                    gi += 1
                    for tl in range(4):
                        t = q4 * 4 + tl
                        self.mm(gp[:, tl * 128:(tl + 1) * 128], Yt[:, t, :], Wt[:, t, :], True, True, [TY, TW], [Tgp])
                    ta = t0 + q4 * 4
                    self.act(Gsb[:, :, ta:ta + 4].rearrange("p i t -> p t i"), gp.rearrange("p (t i) -> p t i", t=4), AF.Copy, [Tgp], [TG])
            for grp in range(16):
                ut, Tut = utrot.next()
                vb, Tvb = vbrot.next()
                self.dma(ut, S["UT"][grp * 8:(grp + 1) * 8].rearrange("j p f -> p j f"), [TS["UT"]], [Tut])
                self.dma(vb, S["VB"][grp * 8:(grp + 1) * 8].rearrange("j p f -> p j f"), [TS["VB"]], [Tvb])
                for j in range(8):
                    i1 = grp * 8 + j
                    ap_, Tap = a_ps[ai % 2]
                    ai += 1
                    for c in range(8):
                        self.mm(ap_[:, 0:nt], ut[:, j, c * 128:(c + 1) * 128], h2t[:, c, 0:nt], c == 0, c == 7, [Tut, Th], [Tap])
                    ag, Tag = agrot.next()
                    self.act(ag[:, 0:nt], ap_[:, 0:nt], AF.Gelu_apprx_tanh, [Tap], [Tag])
                    ga, Tga = garot.next()
                    self.tt(ga[:, 0:nt], ag[:, 0:nt], Gsb[:, i1, 0:nt], ALU.mult, [Tag, TG], [Tga])
                    for tb in range(nt // 128):
                        for half in range(2):
                            op_, Top = out_ps[tb * 2 + half]
                            self.mm(op_, ga[:, tb * 128:(tb + 1) * 128], vb[:, j, half * 512:(half + 1) * 512], i1 == 0, i1 == 127, [Tga, Tvb], [Top])
            for tb in range(nt // 128):
                r0 = g0 + tb * 128
                xin, Txin = xrot.next()
                self.dma(xin, S["X1"][r0:r0 + 128, :], [TS["X1"]], [Txin])
                xo, Txo = orot.next()
                for half in range(2):
                    op_, Top = out_ps[tb * 2 + half]
                    self.tt(xo[:, half * 512:(half + 1) * 512], op_, xin[:, half * 512:(half + 1) * 512], ALU.add, [Top, Txin], [Txo])
                self.dma(S["X2"][r0:r0 + 128, :], xo, [Txo], [TS["X2"]])

    def phaseE(self):
        nc, I, S, TS = self.nc, self.I, self.S, self.TS
        A = self.arena
        A.reset()
        Tc = self.Tc
        stage = Rot(A, 2, [128, 8, 512], F32, "wstage")
        g3 = A.alloc([128, 8], F32)
        Tw = T("wE")
        self.dma(g3, I["norm3_g"].rearrange("(c p) -> p c", p=128), [], [Tc], slow=True)
        wg = A.alloc([128, 8, D], BF16)
        wp = A.alloc([128, 2, D], BF16)
        self.load_cast_weight(wg, I["ple_gate_w"], 8, D, g3, stage, Tw)
        self.load_cast_weight(wp, I["ple_w"], 2, D, None, stage, Tw)
        gfb = A.alloc([128, D], F32)
        self.dma(gfb, I["final_norm_g"].partition_broadcast(128), [], [Tw])
        xrot = Rot(A, 2, [128, D], F32, "xE")
        prot = Rot(A, 2, [128, 256], F32, "pE")
        pbrot = Rot(A, 2, [128, 256], BF16, "pbE")
        pTrot = Rot(A, 2, [128, 2, 128], BF16, "pT")
        scr = {
            "junk": (A.alloc([128, D], F32), T("junkE")),
            "ss": Rot(A, 2, [128, 1], F32, "ssE"),
            "xn": Rot(A, 2, [128, D], BF16, "xnE"),
            "tp": (self.banks[0][:, :].bitcast(BF16), self.Tbank[0]),
        }
        hTrot = Rot(A, 2, [128, 8, 128], BF16, "h3T")
        gate = A.alloc([128, D], F32)
        Tgate = T("gate")
        x3rot = Rot(A, 2, [128, D], F32, "x3")
        yrot = Rot(A, 2, [128, D], F32, "yE")
        ssf = Rot(A, 2, [128, 1], F32, "ssf")
        junk2 = A.alloc([128, D], F32)
        Tj2 = T("junk2")
        g_ps = ((self.banks[1][:, :], self.banks[2][:, :]), self.Tbank[1])
        p_ps = ((self.banks[3][:, :], self.banks[4][:, :]), self.Tbank[3])
        tp2, Ttp2 = self.banks[5][:, :].bitcast(BF16), self.Tbank[5]
        for gb in range(self.NB):
            r0 = gb * 128
            xin, Txin = xrot.next()
            self.dma(xin, S["X2"][r0:r0 + 128, :], [TS["X2"]], [Txin])
            pin, Tpin = prot.next()
            self.dma(pin, I["p"][r0:r0 + 128, :], [], [Tpin])
            pb, Tpb = pbrot.next()
            self.cp(pb, pin, [Tpin], [Tpb], eng="pool")
            for c2 in range(2):
                self.tr(tp2[:, c2 * 128:(c2 + 1) * 128], pb[:, c2 * 128:(c2 + 1) * 128], self.ident, [Tpb, Tc], [Ttp2])
            pT, TpT = pTrot.next()
            self.cp(pT, tp2[:, 0:256].rearrange("p (c t) -> p c t", c=2), [Ttp2], [TpT])
            hT, ThT = hTrot.next()
            self.rmsnorm_T(xin, Txin, None, hT, ThT, 0, 128, scr)
            (g0_, g1_), Tg = g_ps
            (q0_, q1_), Tq = p_ps
            for half, pp in enumerate((g0_, g1_)):
                for c in range(8):
                    self.mm(pp, hT[:, c, :], wg[:, c, half * 512:(half + 1) * 512], c == 0, c == 7, [ThT, Tw], [Tg])
            for half, pp in enumerate((q0_, q1_)):
                for c2 in range(2):
                    self.mm(pp, pT[:, c2, :], wp[:, c2, half * 512:(half + 1) * 512], c2 == 0, c2 == 1, [TpT, Tw], [Tq])
            self.act(gate[:, 0:512], g0_, AF.Sigmoid, [Tg], [Tgate])
            self.act(gate[:, 512:1024], g1_, AF.Sigmoid, [Tg], [Tgate])
            x3, Tx3 = x3rot.next()
            self.tt(x3[:, 0:512], q0_, gate[:, 0:512], ALU.mult, [Tq, Tgate], [Tx3])
            self.tt(x3[:, 512:1024], q1_, gate[:, 512:1024], ALU.mult, [Tq, Tgate], [Tx3])
            self.tt(x3, x3, xin, ALU.add, [Tx3, Txin], [Tx3], eng="pool")
            sf, Tsf = ssf.next()
            self.act(junk2, x3, AF.Square, [Tx3], [Tj2, Tsf], accum_out=sf)
            self.act(sf, sf, AF.Sqrt, [Tsf], [Tsf], scale=1.0 / D, bias=self.epsc)
            self.dve(lambda e, sf=sf: e.reciprocal(out=sf, in_=sf), [Tsf], [Tsf])
            yo, Tyo = yrot.next()
            self._stt(yo, x3, sf, gfb, ALU.mult, ALU.mult, [Tx3, Tsf, Tw], [Tyo])
            self.dma(self.y[r0:r0 + 128, :], yo, [Tyo], [])


def t5_onehot():
    rel = 640 - np.arange(1280)
    nb, max_exact = 16, 8
    ret = (rel > 0).astype(np.int64) * nb
    n = np.abs(rel)
    nf = np.maximum(n, 1).astype(np.float32)
    large = max_exact + (np.log(nf / max_exact) / math.log(128 / max_exact) * (nb - max_exact)).astype(np.int32)
    large = np.minimum(large, nb - 1)
    bucket = ret + np.where(n < max_exact, n, large)
    oh = np.zeros((32, 1280), np.float32)
    oh[bucket, np.arange(1280)] = 1.0
    return oh


def core_inputs(inputs, c, seqs=SEQS_FULL):
    f = lambda a: np.ascontiguousarray(np.asarray(a, dtype=np.float32))
    xp = inputs["x_prompt"][c]
    xs0 = inputs["x_sample"][2 * c]
    xs1 = inputs["x_sample"][2 * c + 1]
    pp = inputs["p_prompt"][0, c]
    ps0 = inputs["p_sample"][0, 2 * c]
    ps1 = inputs["p_sample"][0, 2 * c + 1]
    m = {
        "x": f(np.concatenate([xp, xs0, xs1], axis=0)),
        "p": f(np.concatenate([pp, ps0, ps1], axis=0)),
    }
    m.update(shared_inputs(inputs))
    return m


def shared_inputs(inputs):
    f = lambda a: np.ascontiguousarray(np.asarray(a, dtype=np.float32))
    return {
        "rel_bias": f(inputs["rel_bias"]),
        "norm1_g": f(inputs["norm1_g"][0]),
        "w_in": f(inputs["w_in"][0]),
        "lambda_qk": f(inputs["lambda_qk"][0]),
        "da_norm_g": f(inputs["da_norm_g"][0]),
        "gla_alpha_w": f(inputs["gla_alpha_w"][0]),
        "gla_alpha_b": f(inputs["gla_alpha_b"][0]),
        "gla_norm_g": f(inputs["gla_norm_g"][0]),
        "w_up_a": f(inputs["w_up_a"][0]),
        "w_up_b": f(inputs["w_up_b"][0]),
        "w_out": f(inputs["w_out"][0]),
        "norm2_g": f(inputs["norm2_g"][0]),
        "peer_w_q": f(inputs["peer_w_q"][0]),
        "peer_sub_keys": f(np.asarray(inputs["peer_sub_keys"][0]).reshape(16, 128, 128)),
        "peer_u": f(inputs["peer_u"][0]),
        "peer_v": f(inputs["peer_v"][0]),
        "norm3_g": f(inputs["norm3_g"][0]),
        "ple_w": f(inputs["ple_w"][0]),
        "ple_gate_w": f(inputs["ple_gate_w"][0]),
        "final_norm_g": f(inputs["final_norm_g"]),
        "t5oh": t5_onehot(),
    }


def kernel(**inputs):
    nc = K(SEQS_FULL).build()
    in_maps = [core_inputs(inputs, c) for c in range(8)]
    res = run_bass_kernel_spmd(nc, in_maps, core_ids=list(range(8)))
    yp = np.zeros((8, 4096, D), np.float32)
    ys = np.zeros((16, 2048, D), np.float32)
    for c in range(8):
        y = res.results[c]["y"]
        yp[c] = y[0:4096]
        ys[2 * c] = y[4096:6144]
        ys[2 * c + 1] = y[6144:8192]
    return (yp, ys)
```

```python
import math
import numpy as np
from contextlib import ExitStack
import concourse.bass as bass
import concourse.mybir as mybir
from concourse.bass_utils import run_bass_kernel_spmd

F32 = mybir.dt.float32
BF16 = mybir.dt.bfloat16
U32 = mybir.dt.uint32
I32 = mybir.dt.int32
AF = mybir.ActivationFunctionType
ALU = mybir.AluOpType
AX = mybir.AxisListType

D = 1024
IN_DIM = 5152
C_QA, C_KA, C_VA, C_QG, C_KG, C_VG, C_RG, C_LR, C_GL = 0, 512, 1024, 1536, 1792, 2048, 2560, 3072, 3104
EPS = 1e-6
NE = 16384
SEQS_FULL = (4096, 2048, 2048)


class T:
    __slots__ = ("name", "w", "r")

    def __init__(self, name=""):
        self.name = name
        self.w = {}
        self.r = {}


class Op:
    __slots__ = ("eng", "key", "fn", "deps", "ticket", "needed", "is_dma", "order")

    def __init__(self, eng, key, fn, is_dma):
        self.eng = eng
        self.key = key
        self.fn = fn
        self.deps = {}
        self.ticket = None
        self.needed = False
        self.is_dma = is_dma
        self.order = 0


class Emitter:
    NRING = 32
    ENGS = ("pe", "act", "dve", "pool", "sp")

    def __init__(self):
        self.progs = {k: [] for k in self.ENGS}
        self.ring_last = [None] * self.NRING
        self.ring_i = 0
        self.all_dma = []
        self.last_op = {}
        self.pending = {k: [] for k in self.ENGS}
        self.nops = 0

    def _add_dep(self, op, d):
        if d is op:
            return
        if d.key == op.key and op.key == "pe":
            return
        cur = op.deps.get(d.key)
        if cur is None or cur.order < d.order:
            op.deps[d.key] = d

    def op(self, eng, fn, reads=(), writes=(), dma=False):
        self.nops += 1
        if dma:
            slot = self.ring_i % self.NRING
            self.ring_i += 1
            key = "dma%d" % slot
        else:
            key = eng
        o = Op(eng, key, fn, dma)
        o.order = self.nops
        if dma:
            prev = self.ring_last[slot]
            self.ring_last[slot] = o
            if prev is not None:
                o.deps[key] = prev
            self.all_dma.append(o)
        for d in self.pending[eng]:
            self._add_dep(o, d)
        self.pending[eng] = []
        for t in reads:
            for k, d in t.w.items():
                self._add_dep(o, d)
        for t in writes:
            for k, d in t.w.items():
                if k == eng and not dma and eng != "pool":
                    continue
                self._add_dep(o, d)
            for k, d in t.r.items():
                if k == eng and not dma and eng != "pool":
                    continue
                self._add_dep(o, d)
        for t in reads:
            t.r[key] = o
        for t in writes:
            t.w[key] = o
        self.progs[eng].append(o)
        self.last_op[key] = o
        return o

    def barrier(self):
        lasts = list(self.last_op.values())
        for k in self.ENGS:
            self.pending[k] = list(lasts)

    def finalize(self):
        for prog in self.progs.values():
            for o in prog:
                for d in o.deps.values():
                    d.needed = True
        counters = {}
        for prog in self.progs.values():
            for o in prog:
                if o.needed and not o.is_dma:
                    counters[o.key] = counters.get(o.key, 0) + 1
                    o.ticket = counters[o.key]
        for o in self.all_dma:
            o.needed = True
            counters[o.key] = counters.get(o.key, 0) + 1
            o.ticket = counters[o.key]

    EPOCH = 16000

    def sem_of(self, sems, op):
        if op.is_dma:
            return sems[op.key], op.ticket * 16, (op.key, 0)
        ep, loc = divmod(op.ticket - 1, self.EPOCH)
        return sems["%s_%d" % (op.key, ep)], loc + 1, (op.key, ep)

    def n_epochs(self, key):
        mx = 0
        for o in self.progs[key]:
            if o.ticket is not None and not o.is_dma:
                mx = max(mx, o.ticket)
        return (mx - 1) // self.EPOCH + 1 if mx else 1

    def emit_engine(self, eng, engine_obj, sems, final_wait=False):
        waited = {}
        for o in self.progs[eng]:
            for k, d in o.deps.items():
                sem, val, wk = self.sem_of(sems, d)
                if waited.get(wk, 0) >= val:
                    continue
                engine_obj.wait_ge(sem, val)
                waited[wk] = val
            ins = o.fn(engine_obj)
            if o.needed:
                sem, _, _ = self.sem_of(sems, o)
                ins.then_inc(sem, 16 if o.is_dma else 1)
        if final_wait:
            for slot in range(self.NRING):
                last = self.ring_last[slot]
                if last is not None:
                    val = last.ticket * 16
                    if waited.get((last.key, 0), 0) < val:
                        engine_obj.wait_ge(sems[last.key], val)
                        waited[(last.key, 0)] = val


class Arena:
    def __init__(self, ap, nelem):
        self.ap = ap
        self.n = nelem
        self.off = 0

    def reset(self):
        self.off = 0

    def alloc(self, shape, dt):
        esz = 4 if dt in (F32, U32, I32) else 2
        n = 1
        for s in shape[1:]:
            n *= s
        nb = n * esz
        nb = (nb + 31) // 32 * 32
        assert self.off * 2 + nb <= self.n * 2, "arena overflow: need %d have %d" % (self.off * 2 + nb, self.n * 2)
        v = self.ap[:, self.off:self.off + nb // 2]
        self.off += nb // 2
        if esz == 4:
            v = v.bitcast(dt)
        elif dt != BF16:
            v = v.bitcast(dt)
        v = v[:, 0:n]
        if len(shape) > 2:
            names = " ".join("d%d" % i for i in range(len(shape) - 1))
            kw = {"d%d" % i: shape[i + 1] for i in range(len(shape) - 2)}
            v = v.rearrange("p (%s) -> p %s" % (names, names), **kw)
        if shape[0] < 128:
            v = v[0:shape[0]]
        return v


class Rot:
    def __init__(self, arena, n, shape, dt, name):
        self.bufs = [(arena.alloc(shape, dt), T("%s%d" % (name, i))) for i in range(n)]
        self.i = 0

    def next(self):
        b = self.bufs[self.i % len(self.bufs)]
        self.i += 1
        return b


class K:
    def __init__(self, seqs, debug=False, phases="0AGBMPE"):
        self.seqs = tuple(seqs)
        self.NT = sum(seqs)
        self.NB = self.NT // 128
        self.debug = debug
        self.phases = phases
        self.nc = bass.Bass("TRN2", target_bir_lowering=False)
        self.em = Emitter()
        self.dbg_names = []

    def act(self, out, in_, func, R, W, **kw):
        self.em.op("act", lambda e: e.activation(out=out, in_=in_, func=func, **kw), R, W)

    def dve(self, fn, R, W):
        self.em.op("dve", fn, R, W)

    def pool(self, fn, R, W):
        self.em.op("pool", fn, R, W)

    def mm(self, out, lhsT, rhs, start, stop, R, W):
        self.em.op("pe", lambda e: e.matmul(out, lhsT=lhsT, rhs=rhs, start=start, stop=stop), R, W)

    def tr(self, out, in_, ident, R, W):
        self.em.op("pe", lambda e: e.transpose(out=out, in_=in_, identity=ident), R, W)

    def dma(self, out, in_, R, W, q="sp", slow=False):
        q = "sp"
        if slow:
            self.em.op(q, lambda e: e.dma_start(out=out, in_=in_, allow_slow_non_contiguous=True), R, W, dma=True)
        else:
            self.em.op(q, lambda e: e.dma_start(out=out, in_=in_), R, W, dma=True)

    def tt(self, out, in0, in1, op, R, W, eng="dve"):
        self.em.op(eng, lambda e: e.tensor_tensor(out=out, in0=in0, in1=in1, op=op), R, W)

    def ts(self, out, in0, s1, s2, op0, R, W, op1=None, eng="dve", **kw):
        if op1 is None:
            self.em.op(eng, lambda e: e.tensor_scalar(out=out, in0=in0, scalar1=s1, scalar2=s2, op0=op0, **kw), R, W)
        else:
            self.em.op(eng, lambda e: e.tensor_scalar(out=out, in0=in0, scalar1=s1, scalar2=s2, op0=op0, op1=op1, **kw), R, W)

    def cp(self, out, in_, R, W, eng="dve"):
        self.em.op(eng, lambda e: e.tensor_copy(out=out, in_=in_), R, W)

    def dram_in(self, name, shape, dt=F32):
        return self.nc.dram_tensor(name, list(shape), dt, kind="ExternalInput").ap()

    def scratch(self, name, shape, dt):
        kind = "ExternalOutput" if (self.debug and name in self.debug) else "Internal"
        if kind == "ExternalOutput":
            self.dbg_names.append(name)
        return self.nc.dram_tensor(name, list(shape), dt, kind=kind).ap()

    def build(self):
        nc, em = self.nc, self.em
        NT, NB = self.NT, self.NB
        I = {}
        I["x"] = self.dram_in("x", [NT, D])
        I["p"] = self.dram_in("p", [NT, 256])
        I["rel_bias"] = self.dram_in("rel_bias", [32, 4])
        I["norm1_g"] = self.dram_in("norm1_g", [D])
        I["w_in"] = self.dram_in("w_in", [D, IN_DIM])
        I["lambda_qk"] = self.dram_in("lambda_qk", [4, 64])
        I["da_norm_g"] = self.dram_in("da_norm_g", [128])
        I["gla_alpha_w"] = self.dram_in("gla_alpha_w", [2, 16, 256])
        I["gla_alpha_b"] = self.dram_in("gla_alpha_b", [2, 256])
        I["gla_norm_g"] = self.dram_in("gla_norm_g", [128])
        I["w_up_a"] = self.dram_in("w_up_a", [512, D])
        I["w_up_b"] = self.dram_in("w_up_b", [512, D])
        I["w_out"] = self.dram_in("w_out", [D, D])
        I["norm2_g"] = self.dram_in("norm2_g", [D])
        I["peer_w_q"] = self.dram_in("peer_w_q", [D, 2048])
        I["peer_sub_keys"] = self.dram_in("peer_sub_keys", [16, 128, 128])
        I["peer_u"] = self.dram_in("peer_u", [NE, D])
        I["peer_v"] = self.dram_in("peer_v", [NE, D])
        I["norm3_g"] = self.dram_in("norm3_g", [D])
        I["ple_w"] = self.dram_in("ple_w", [256, D])
        I["ple_gate_w"] = self.dram_in("ple_gate_w", [D, D])
        I["final_norm_g"] = self.dram_in("final_norm_g", [D])
        I["t5oh"] = self.dram_in("t5oh", [32, 1280])
        self.I = I
        self.y = nc.dram_tensor("y", [NT, D], F32, kind="ExternalOutput").ap()

        S = {}
        S["QT"] = self.scratch("QT", [4, 128, NT], BF16)
        S["KT"] = self.scratch("KT", [4, 128, NT], BF16)
        S["VA"] = self.scratch("VA", [NT, 512], BF16)
        S["GQ"] = self.scratch("GQ", [NB, 128, 512], BF16)
        S["GK"] = self.scratch("GK", [NB, 128, 512], BF16)
        S["GD"] = self.scratch("GD", [NB, 128, 512], BF16)
        S["GV"] = self.scratch("GV", [NT, 512], BF16)
        S["SR"] = self.scratch("SR", [NT, 512], BF16)
        S["SGT"] = self.scratch("SGT", [16, 128, NT], BF16)
        S["OAT"] = self.scratch("OAT", [4, 128, NT], BF16)
        S["OGT"] = self.scratch("OGT", [4, 128, NT], BF16)
        S["X1"] = self.scratch("X1", [NT, D], F32)
        S["H2T"] = self.scratch("H2T", [8, 128, NT], BF16)
        S["SL"] = self.scratch("SL", [3, 128, NT], F32)
        S["X2"] = self.scratch("X2", [NT, D], F32)
        S["UT"] = self.scratch("UT", [128, 128, 1024], BF16)
        S["VB"] = self.scratch("VB", [128, 128, 1024], BF16)
        S["BV"] = self.scratch("BV", [4, 1280], F32)
        self.S = S
        self.TS = {k: T("S_" + k) for k in S}

        with ExitStack() as es:
            self.es = es
            arena_t = es.enter_context(nc.sbuf_tensor("arena", [128, 94000], BF16))
            self.arena = Arena(arena_t[:, :], 94000)
            cons_t = es.enter_context(nc.sbuf_tensor("cons", [128, 5000], F32))
            self.cons = cons_t
            self.cons_off = 0
            self.banks = [es.enter_context(nc.psum_tensor("bank%d" % i, [128, 512], F32)) for i in range(8)]
            self.Tbank = [T("bank%d" % i) for i in range(8)]

            self.phase0()
            if "A" in self.phases:
                em.barrier()
                self.phaseA()
            if "G" in self.phases:
                em.barrier()
                self.phaseG()
            if "B" in self.phases:
                em.barrier()
                self.phaseB()
            if "M" in self.phases:
                em.barrier()
                self.phaseM()
            if "P" in self.phases:
                em.barrier()
                self.phaseP()
            if "E" in self.phases:
                em.barrier()
                self.phaseE()

            em.finalize()
            sems = {}
            for k in ["dma%d" % i for i in range(em.NRING)]:
                sems[k] = es.enter_context(nc.semaphore("s_" + k))
            for k in ["pe", "act", "dve", "pool"]:
                for ep in range(em.n_epochs(k)):
                    sems["%s_%d" % (k, ep)] = es.enter_context(nc.semaphore("s_%s_%d" % (k, ep)))
            with nc.Block() as block:
                @block.sync
                def _(e):
                    em.emit_engine("sp", e, sems, final_wait=True)

                @block.scalar
                def _(e):
                    em.emit_engine("act", e, sems)

                @block.vector
                def _(e):
                    em.emit_engine("dve", e, sems)

                @block.gpsimd
                def _(e):
                    em.emit_engine("pool", e, sems, final_wait=True)

                @block.tensor
                def _(e):
                    em.emit_engine("pe", e, sems)
        return nc

    def calloc(self, n):
        v = self.cons[:, self.cons_off:self.cons_off + n]
        self.cons_off += n
        assert self.cons_off <= 5000
        return v

    def phase0(self):
        nc, I = self.nc, self.I
        Tc = self.Tc = T("consts")
        self.identf = self.calloc(128)
        self.ident = self.calloc(64).bitcast(BF16)
        self.triF = self.calloc(128)
        self.triB = self.calloc(128)
        self.triSU = self.calloc(128)
        self.triSL = self.calloc(128)
        self.maskF = self.calloc(128)
        self.maskB = self.calloc(128)
        self.epsc = self.calloc(1)
        self.onec = self.calloc(1)
        po = lambda fn: self.pool(fn, [Tc], [Tc])
        po(lambda e: e.memset(self.identf, 0.0))
        po(lambda e: e.affine_select(out=self.identf, in_=self.identf, compare_op=ALU.not_equal, fill=1.0,
                                     base=0, pattern=[[-1, 128]], channel_multiplier=1))
        po(lambda e: e.tensor_copy(out=self.ident, in_=self.identf))

        def tri(dst, val, cmul, step, base, cmp):
            po(lambda e: e.memset(dst, val))
            po(lambda e: e.affine_select(out=dst, in_=dst, compare_op=cmp, fill=0.0, base=base,
                                         pattern=[[step, 128]], channel_multiplier=cmul))
        tri(self.triF, -1.0 / 16, -1, 1, 0, ALU.is_ge)
        tri(self.triB, -1.0 / 16, 1, -1, 0, ALU.is_ge)
        tri(self.triSU, -1.0 / 16, 1, -1, 0, ALU.is_gt)
        tri(self.triSL, -1.0 / 16, -1, 1, 0, ALU.is_gt)
        tri(self.maskF, 1.0, -1, 1, 0, ALU.is_ge)
        tri(self.maskB, 1.0, 1, -1, 0, ALU.is_gt)
        po(lambda e: e.memset(self.epsc, EPS))
        po(lambda e: e.memset(self.onec, 1.0))

    def load_cast_weight(self, dst, src_ap, nchunk, ncols, gvec, stage_rot, Tdst, col_piece=512):
        for c0 in range(0, ncols, col_piece):
            cw = min(col_piece, ncols - c0)
            st, Tst = stage_rot.next()
            stv = st[:, 0:nchunk, 0:cw]
            self.dma(stv, src_ap[:, c0:c0 + cw].rearrange("(c p) n -> p c n", p=128), [], [Tst])
            for c in range(nchunk):
                if gvec is not None:
                    self.ts(dst[:, c, c0:c0 + cw], stv[:, c, :], gvec[:, c:c + 1], None, ALU.mult, [Tst, self.Tc], [Tdst],
                            eng=("dve" if c % 2 == 0 else "pool"))
                else:
                    self.cp(dst[:, c, c0:c0 + cw], stv[:, c, :], [Tst], [Tdst], eng=("dve" if c % 2 == 0 else "pool"))

    def rmsnorm_T(self, xin, Txin, gain_none_out_bf, hT_dst, ThT, tok0, ntok_tile, scr):
        junk, Tjunk = scr["junk"]
        ss, Tss = scr["ss"].next()
        xn, Txn = scr["xn"].next()
        tpb, Ttp = scr["tp"]
        self.act(junk, xin, AF.Square, [Txin], [Tjunk, Tss], accum_out=ss)
        self.act(ss, ss, AF.Sqrt, [Tss], [Tss], scale=1.0 / D, bias=self.epsc)
        self.dve(lambda e: e.reciprocal(out=ss, in_=ss), [Tss], [Tss])
        self.ts(xn, xin, ss, None, ALU.mult, [Txin, Tss], [Txn])
        for c in range(8):
            self.tr(tpb[:, c * 128:(c + 1) * 128], xn[:, c * 128:(c + 1) * 128], self.ident, [Txn, self.Tc], [Ttp])
        self.act(hT_dst[:, :, tok0:tok0 + 128], tpb.rearrange("p (c t) -> p c t", c=8), AF.Copy, [Ttp], [ThT])

    def phaseA(self):
        nc, I, S, TS = self.nc, self.I, self.S, self.TS
        A = self.arena
        A.reset()
        NT = self.NT
        Tc = self.Tc
        win = A.alloc([128, 8, IN_DIM], BF16)
        Twin = T("win")
        g1 = A.alloc([128, 8], F32)
        Tg1 = T("g1")
        self.dma(g1, I["norm1_g"].rearrange("(c p) -> p c", p=128), [], [Tc], slow=True)
        stage = Rot(A, 2, [128, 8, 512], F32, "wstage")
        self.load_cast_weight(win, I["w_in"], 8, IN_DIM, g1, stage, Twin)
        awf = A.alloc([33, 512], F32)
        aw = A.alloc([33, 512], BF16)
        Taw = T("aw")
        self.pool(lambda e: e.memset(awf, 0.0), [], [Taw])
        self.dma(awf[0:16, 0:256], I["gla_alpha_w"][0], [], [Taw])
        self.dma(awf[16:32, 256:512], I["gla_alpha_w"][1], [], [Taw])
        self.dma(awf[32:33, :], I["gla_alpha_b"].rearrange("(o j) k -> o (j k)", o=1), [], [Taw])
        self.cp(aw, awf, [Taw], [Taw])

        xrot = Rot(A, 2, [128, D], F32, "xin")
        scr = {
            "junk": (A.alloc([128, D], F32), T("junk")),
            "ss": Rot(A, 2, [128, 1], F32, "ss"),
            "xn": Rot(A, 2, [128, D], BF16, "xn"),
            "tp": (self.banks[0][:, :].bitcast(BF16), self.Tbank[0]),
        }
        hTrot = Rot(A, 2, [128, 8, 512], BF16, "hT")
        fm_ps = [(self.banks[1][:, :], self.Tbank[1]), (self.banks[2][:, :], self.Tbank[2])]
        tm_ps = [(self.banks[3][:, :], self.Tbank[3]), (self.banks[4][:, :], self.Tbank[4])]
        z_ps = (self.banks[5][:, :], self.Tbank[5])
        d_ps = (self.banks[6][:, :], self.Tbank[6])
        bt_ps = (self.banks[7][:, :], self.Tbank[7])
        fmrot = Rot(A, 3, [128, 512], BF16, "fmout")
        qg_sb = A.alloc([128, 2, 512], F32)
        Tqg = T("qg_sb")
        kg_sb = A.alloc([128, 2, 512], F32)
        Tkg = T("kg_sb")
        lrT = A.alloc([33, 512], BF16)
        TlrT = T("lrT")
        self.pool(lambda e: e.memset(lrT[32:33, :], 1.0), [], [TlrT])
        tmrot = Rot(A, 3, [128, 512], BF16, "tmout")
        e1 = A.alloc([128, 512], F32)
        Te1 = T("e1")
        Lt = A.alloc([128, 512], F32)
        TL = T("L")
        kde = A.alloc([128, 512], F32)
        Tkde = T("kde")
        epos = A.alloc([128, 512], F32)
        Tep = T("epos")
        eneg = A.alloc([128, 512], F32)
        Ten = T("eneg")
        gqrot = Rot(A, 2, [128, 512], BF16, "gq")
        gkrot = Rot(A, 2, [128, 512], BF16, "gk")
        gdrot = Rot(A, 2, [128, 512], BF16, "gd")
        self.DEC = self.calloc(self.NB * 4)
        self.TDEC = T("DEC")

        fmi = [0]
        tmi = [0]

        def fm_proj(hT, ThT, col0, M, ntok):
            ps, Tps = fm_ps[fmi[0] % 2]
            fmi[0] += 1
            for c in range(8):
                self.mm(ps[0:M, 0:ntok], win[:, c, col0:col0 + M], hT[:, c, 0:ntok], c == 0, c == 7, [Twin, ThT], [Tps])
            return ps, Tps

        def tm_proj(hT, ThT, col0, ncols, t0):
            ps, Tps = tm_ps[tmi[0] % 2]
            tmi[0] += 1
            for c in range(8):
                self.mm(ps[:, 0:ncols], hT[:, c, t0:t0 + 128], win[:, c, col0:col0 + ncols], c == 0, c == 7, [ThT, Twin], [Tps])
            return ps, Tps

        tok = 0
        for S_len in self.seqs:
            for tile0 in range(0, S_len, 512):
                ntok = min(512, S_len - tile0)
                g0 = tok + tile0
                hT, ThT = hTrot.next()
                for b in range(ntok // 128):
                    xin, Txin = xrot.next()
                    self.dma(xin, I["x"][g0 + b * 128:g0 + (b + 1) * 128, :], [], [Txin])
                    self.rmsnorm_T(xin, Txin, None, hT, ThT, b * 128, ntok, scr)
                for h in range(4):
                    ps, Tps = fm_proj(hT, ThT, C_QA + h * 128, 128, ntok)
                    o, To = fmrot.next()
                    self.act(o[:, 0:ntok], ps[:, 0:ntok], AF.Copy, [Tps], [To], scale=0.125)
                    self.dma(S["QT"][h, :, g0:g0 + ntok], o[:, 0:ntok], [To], [TS["QT"]], q="pool")
                    ps, Tps = fm_proj(hT, ThT, C_KA + h * 128, 128, ntok)
                    o, To = fmrot.next()
                    self.act(o[:, 0:ntok], ps[:, 0:ntok], AF.Copy, [Tps], [To])
                    self.dma(S["KT"][h, :, g0:g0 + ntok], o[:, 0:ntok], [To], [TS["KT"]], q="pool")
                for c2 in range(2):
                    ps, Tps = fm_proj(hT, ThT, C_QG + c2 * 128, 128, ntok)
                    self.act(qg_sb[:, c2, 0:ntok], ps[:, 0:ntok], AF.Copy, [Tps], [Tqg], scale=0.125)
                    ps, Tps = fm_proj(hT, ThT, C_KG + c2 * 128, 128, ntok)
                    self.act(kg_sb[:, c2, 0:ntok], ps[:, 0:ntok], AF.Copy, [Tps], [Tkg])
                ps, Tps = fm_proj(hT, ThT, C_LR, 32, ntok)
                self.act(lrT[0:32, 0:ntok], ps[0:32, 0:ntok], AF.Copy, [Tps], [TlrT])
                for j in range(16):
                    ps, Tps = fm_proj(hT, ThT, C_GL + j * 128, 128, ntok)
                    o, To = fmrot.next()
                    self.act(o[:, 0:ntok], ps[:, 0:ntok], AF.Sigmoid, [Tps], [To])
                    self.dma(S["SGT"][j, :, g0:g0 + ntok], o[:, 0:ntok], [To], [TS["SGT"]], q="pool")
                for b in range(ntok // 128):
                    gb = (g0 + b * 128) // 128
                    r0 = g0 + b * 128
                    t0 = b * 128
                    ps, Tps = tm_proj(hT, ThT, C_VA, 512, t0)
                    o, To = tmrot.next()
                    self.cp(o, ps, [Tps], [To])
                    self.dma(S["VA"][r0:r0 + 128, :], o, [To], [TS["VA"]], q="pool")
                    ps, Tps = tm_proj(hT, ThT, C_VG, 512, t0)
                    o, To = tmrot.next()
                    self.cp(o, ps, [Tps], [To])
                    self.dma(S["GV"][r0:r0 + 128, :], o, [To], [TS["GV"]], q="pool")
                    ps, Tps = tm_proj(hT, ThT, C_RG, 512, t0)
                    o, To = tmrot.next()
                    self.act(o, ps, AF.Silu, [Tps], [To])
                    self.dma(S["SR"][r0:r0 + 128, :], o, [To], [TS["SR"]], q="pool")
                    zp, Tzp = z_ps
                    self.mm(zp, lrT[0:33, t0:t0 + 128], aw[0:33, :], True, True, [TlrT, Taw], [Tzp])
                    self.act(e1, zp, AF.Exp, [Tzp], [Te1], scale=-1.0)
                    self.act(Lt, e1, AF.Ln, [Te1], [TL], bias=self.onec)
                    dp, Tdp = d_ps
                    self.mm(dp[:, 0:256], self.triSU, Lt[:, 0:256], True, True, [Tc, TL], [Tdp])
                    self.mm(dp[:, 256:512], self.triSL, Lt[:, 256:512], True, True, [Tc, TL], [Tdp])
                    self.act(kde, dp, AF.Exp, [Tdp], [Tkde])
                    kps, Tkps = tm_proj(hT, ThT, C_KG, 256, t0)
                    gd, Tgd = gdrot.next()
                    self.tt(gd.rearrange("p (j k) -> p j k", j=2), kde.rearrange("p (j k) -> p j k", j=2),
                            kps[:, 0:256].unsqueeze(1).to_broadcast([128, 2, 256]), ALU.mult, [Tkde, Tkps], [Tgd])
                    self.dma(S["GD"][gb], gd, [Tgd], [TS["GD"]], q="pool")
                    bp, Tbp = bt_ps
                    for dr in range(2):
                        for c2 in range(2):
                            idx = dr * 2 + c2
                            self.mm(bp[:, idx * 128:(idx + 1) * 128], Lt[:, dr * 256 + c2 * 128: dr * 256 + (c2 + 1) * 128],
                                    self.triF if dr == 0 else self.triB, True, True, [TL, Tc], [Tbp])
                    self.act(epos, bp, AF.Exp, [Tbp], [Tep])
                    self.act(eneg, bp, AF.Exp, [Tbp], [Ten], scale=-1.0)
                    gq, Tgq = gqrot.next()
                    gk, Tgk = gkrot.next()
                    self.tt(gq.rearrange("p (j c t) -> p j c t", j=2, c=2), epos.rearrange("p (j c t) -> p j c t", j=2, c=2),
                            qg_sb[:, :, t0:t0 + 128].unsqueeze(1).to_broadcast([128, 2, 2, 128]), ALU.mult, [Tep, Tqg], [Tgq])
                    self.tt(gk.rearrange("p (j c t) -> p j c t", j=2, c=2), eneg.rearrange("p (j c t) -> p j c t", j=2, c=2),
                            kg_sb[:, :, t0:t0 + 128].unsqueeze(1).to_broadcast([128, 2, 2, 128]), ALU.mult, [Ten, Tkg], [Tgk],
                            eng="pool")
                    self.dma(S["GQ"][gb], gq, [Tgq], [TS["GQ"]], q="pool")
                    self.dma(S["GK"][gb], gk, [Tgk], [TS["GK"]], q="pool")
                    ev = epos.rearrange("p (i t) -> p i t", i=4)
                    self.cp(self.DEC[:, gb * 4:gb * 4 + 2], ev[:, 0:2, 127], [Tep], [self.TDEC])
                    self.cp(self.DEC[:, gb * 4 + 2:gb * 4 + 4], ev[:, 2:4, 0], [Tep], [self.TDEC])
            tok += S_len

    def phaseG(self):
        nc, I, S, TS = self.nc, self.I, self.S, self.TS
        A = self.arena
        A.reset()
        Tc = self.Tc
        maxS = max(self.seqs)
        OF = A.alloc([128, maxS // 128, 512], F32)
        TOF = T("OF")
        gnb = A.alloc([128, 128], F32)
        Tgnb = T("gnb")
        self.dma(gnb, I["gla_norm_g"].partition_broadcast(128), [], [Tgnb])
        qrot = Rot(A, 2, [128, 2, 128], BF16, "gq_in")
        krot = Rot(A, 2, [128, 2, 128], BF16, "gk_in")
        drot = Rot(A, 2, [128, 256], BF16, "gd_in")
        vrot = Rot(A, 2, [128, 512], BF16, "gv_in")
        srrot = Rot(A, 2, [128, 512], BF16, "sr_in")
        attrot = Rot(A, 2, [128, 4, 128], BF16, "att_sb")
        st_f = A.alloc([128, 2, 128], F32)
        st_b = A.alloc([128, 2, 128], BF16)
        Tst = T("state")
        Tstb = T("state_bf")
        tot = A.alloc([128, 512], F32)
        Ttot = T("tot")
        sq = A.alloc([128, 512], F32)
        Tsq = T("sq")
        ssq = A.alloc([128, 4], F32)
        Tssq = T("ssq")
        srg = A.alloc([128, 512], F32)
        Tsrg = T("srg")
        ogrot = Rot(A, 2, [128, 512], BF16, "og")
        ogTrot = Rot(A, 2, [128, 4, 128], BF16, "ogT")
        tpb, Ttp = self.banks[0][:, :].bitcast(BF16), self.Tbank[0]
        att_ps = [(self.banks[1][:, :], self.Tbank[1]), (self.banks[2][:, :], self.Tbank[2])]
        o_ps = [(self.banks[3][:, :], self.Tbank[3]), (self.banks[4][:, :], self.Tbank[4])]
        kv_ps = [(self.banks[5][:, :], self.Tbank[5]), (self.banks[6][:, :], self.Tbank[6])]
        it = 0
        tok = 0
        for S_len in self.seqs:
            nbs = S_len // 128
            gb0 = tok // 128
            for dr in range(2):
                self.dve(lambda e: e.memset(st_f, 0.0), [], [Tst])
                self.pool(lambda e: e.memset(st_b, 0.0), [], [Tstb])
                order = range(nbs) if dr == 0 else range(nbs - 1, -1, -1)
                for lb in order:
                    gb = gb0 + lb
                    r0 = gb * 128
                    q_in, Tq = qrot.next()
                    k_in, Tk = krot.next()
                    d_in, Td = drot.next()
                    v_in, Tv = vrot.next()
                    self.dma(q_in, S["GQ"][gb][:, dr * 256:(dr + 1) * 256].rearrange("p (c t) -> p c t", c=2), [TS["GQ"]], [Tq])
                    self.dma(k_in, S["GK"][gb][:, dr * 256:(dr + 1) * 256].rearrange("p (c t) -> p c t", c=2), [TS["GK"]], [Tk])
                    self.dma(d_in, S["GD"][gb][:, dr * 256:(dr + 1) * 256], [TS["GD"]], [Td])
                    self.dma(v_in, S["GV"][r0:r0 + 128, :], [TS["GV"]], [Tv])
                    if dr == 1:
                        sr_in, Tsr = srrot.next()
                        self.dma(sr_in, S["SR"][r0:r0 + 128, :], [TS["SR"]], [Tsr])
                    kps, Tkps = kv_ps[it % 2]
                    it += 1
                    for h in range(4):
                        c2, par = h // 2, h % 2
                        ph = par * 64
                        aps, Taps = att_ps[par]
                        self.mm(aps[:, c2 * 128:(c2 + 1) * 128], k_in[ph:ph + 64, c2, :], q_in[ph:ph + 64, c2, :], True, True, [Tk, Tq], [Taps])
                    att, Tatt = attrot.next()
                    mask = self.maskF if dr == 0 else self.maskB
                    attv = att.rearrange("p (c par) t -> p c par t", par=2)
                    for par in range(2):
                        aps, Taps = att_ps[par]
                        self.tt(attv[:, :, par, :], aps[:, 0:256].rearrange("p (c t) -> p c t", c=2),
                                mask.unsqueeze(1).to_broadcast([128, 2, 128]), ALU.mult, [Taps, Tc], [Tatt])
                    for h in range(4):
                        c2, par = h // 2, h % 2
                        ph = par * 64
                        ops, Tops = o_ps[par]
                        self.mm(ops[:, c2 * 128:(c2 + 1) * 128], att[:, h, :], v_in[:, h * 128:(h + 1) * 128], True, False, [Tatt, Tv], [Tops])
                        self.mm(ops[:, c2 * 128:(c2 + 1) * 128], q_in[ph:ph + 64, c2, :], st_b[ph:ph + 64, c2, :], False, True, [Tq, Tstb], [Tops])
                    for c2 in range(2):
                        self.mm(kps[:, c2 * 256:(c2 + 1) * 256], d_in[:, c2 * 128:(c2 + 1) * 128], v_in[:, c2 * 256:(c2 + 1) * 256], True, True, [Td, Tv], [Tkps])
                    for h in range(4):
                        c2, hh = h // 2, h % 2
                        rows = slice(hh * 64, (hh + 1) * 64)
                        col = gb * 4 + dr * 2 + c2
                        self._stt(st_f[rows, c2, :], st_f[rows, c2, :], self.DEC[rows, col:col + 1],
                                  kps[rows, c2 * 256 + hh * 128: c2 * 256 + (hh + 1) * 128], ALU.mult, ALU.add, [Tst, Tkps, self.TDEC], [Tst])
                    self.cp(st_b, st_f, [Tst], [Tstb], eng="pool")
                    OFv = OF[:, lb, :].rearrange("p (c par v) -> p c par v", c=2, par=2)
                    totv = tot.rearrange("p (c par v) -> p c par v", c=2, par=2)
                    for par in range(2):
                        ops, Tops = o_ps[par]
                        opv = ops[:, 0:256].rearrange("p (c v) -> p c v", c=2)
                        if dr == 0:
                            self.act(OFv[:, :, par, :], opv, AF.Copy, [Tops], [TOF])
                        else:
                            self.tt(totv[:, :, par, :], opv, OFv[:, :, par, :], ALU.add, [Tops, TOF], [Ttot])
                    if dr == 1:
                        self.act(sq, tot, AF.Square, [Ttot], [Tsq])
                        self.dve(lambda e, sq=sq, ssq=ssq: e.tensor_reduce(out=ssq, in_=sq.rearrange("p (h v) -> p h v", h=4), axis=AX.X, op=ALU.add), [Tsq], [Tssq])
                        self.act(ssq, ssq, AF.Sqrt, [Tssq], [Tssq], scale=1.0 / 128, bias=self.epsc)
                        self.dve(lambda e: e.reciprocal(out=ssq, in_=ssq), [Tssq], [Tssq])
                        self.tt(srg.rearrange("p (h v) -> p h v", h=4), sr_in.rearrange("p (h v) -> p h v", h=4),
                                gnb.unsqueeze(1).to_broadcast([128, 4, 128]), ALU.mult, [Tsr, Tgnb], [Tsrg], eng="pool")
                        self.tt(tot.rearrange("p (h v) -> p h v", h=4), tot.rearrange("p (h v) -> p h v", h=4),
                                ssq.unsqueeze(2).to_broadcast([128, 4, 128]), ALU.mult, [Ttot, Tssq], [Ttot])
                        og, Tog = ogrot.next()
                        self.tt(og, tot, srg, ALU.mult, [Ttot, Tsrg], [Tog])
                        for h in range(4):
                            self.tr(tpb[:, h * 128:(h + 1) * 128], og[:, h * 128:(h + 1) * 128], self.ident, [Tog, Tc], [Ttp])
                        ogT, TogT = ogTrot.next()
                        self.act(ogT, tpb[:, 0:512].rearrange("p (h t) -> p h t", h=4), AF.Copy, [Ttp], [TogT])
                        self.dma(S["OGT"][:, :, r0:r0 + 128].rearrange("h p t -> p h t"), ogT, [TogT], [TS["OGT"]])
            tok += S_len

    def _stt(self, out, in0, scalar, in1, op0, op1, R, W, eng="dve"):
        self.em.op(eng, lambda e: e.scalar_tensor_tensor(out=out, in0=in0, scalar=scalar, in1=in1, op0=op0, op1=op1), R, W)

    def phaseB(self):
        nc, I, S, TS = self.nc, self.I, self.S, self.TS
        A = self.arena
        A.reset()
        Tc = self.Tc
        maxS = max(self.seqs)
        misc_ps, Tmisc = self.banks[7][:, :], self.Tbank[7]
        rb = A.alloc([32, 4], F32)
        oh = A.alloc([32, 1280], F32)
        Tset = T("setupB")
        self.dma(rb, I["rel_bias"], [], [Tset])
        self.dma(oh, I["t5oh"], [], [Tset])
        bv_sb = A.alloc([4, 1280], F32)
        for (a, b) in ((0, 512), (512, 1024), (1024, 1280)):
            self.mm(misc_ps[0:4, 0:b - a], rb, oh[:, a:b], True, True, [Tset], [Tmisc])
            self.cp(bv_sb[:, a:b], misc_ps[0:4, 0:b - a], [Tmisc], [Tset])
        self.dma(S["BV"], bv_sb, [Tset], [TS["BV"]])
        TBf = A.alloc([128, 4, 1154], F32)
        TB = A.alloc([128, 4, 1154], BF16)
        TTB = T("TB")
        for h in range(4):
            src = bass.AP(S["BV"].tensor, h * 1280, [[1, 128], [1, 1153]])
            self.dma(TBf[:, h, 0:1153], src, [TS["BV"]], [TTB])
        self.cp(TB[:, :, 0:1153], TBf[:, :, 0:1153], [TTB], [TTB])
        Jf = A.alloc([128, 128], F32)
        Jb = A.alloc([128, 128], BF16)
        self.pool(lambda e: e.memset(Jf, 0.0), [], [TTB])
        self.pool(lambda e: e.affine_select(out=Jf, in_=Jf, compare_op=ALU.not_equal, fill=1.0, base=-127,
                                            pattern=[[1, 128]], channel_multiplier=1), [TTB], [TTB])
        self.pool(lambda e: e.tensor_copy(out=Jb, in_=Jf), [TTB], [TTB])
        cfar = A.alloc([128, 4, 2], F32)
        Tcf = T("cfar")
        for h in range(4):
            self.dma(cfar[:, h, 0:1], S["BV"][h:h + 1, 1279:1280].partition_broadcast(128), [TS["BV"]], [Tcf])
            self.dma(cfar[:, h, 1:2], S["BV"][h:h + 1, 1:2].partition_broadcast(128), [TS["BV"]], [Tcf])
        lq = A.alloc([1, 256], F32)
        lp = A.alloc([1, 128], F32)
        l2 = A.alloc([1, 2], F32)
        Tl = T("lam")
        self.dma(lq, I["lambda_qk"].rearrange("(o a) d -> o (a d)", o=1), [], [Tl])
        lqv = lq.rearrange("p (a b d) -> p a b d", a=2, b=2)
        self.tt(lp.rearrange("p (a d) -> p a d", a=2), lqv[:, :, 0, :], lqv[:, :, 1, :], ALU.mult, [Tl], [Tl])
        self.dve(lambda e: e.tensor_reduce(out=l2, in_=lp.rearrange("p (a d) -> p a d", a=2), axis=AX.X, op=ALU.add), [Tl], [Tl])
        self.act(l2, l2, AF.Exp, [Tl], [Tl])
        lam_init = 0.8 - 0.6 * math.exp(-0.3 * 0)
        self.tt(l2[:, 0:1], l2[:, 1:2], l2[:, 0:1], ALU.subtract, [Tl], [Tl])
        self.ts(l2[:, 0:1], l2[:, 0:1], -lam_init, None, ALU.add, [Tl], [Tl])
        ones1 = A.alloc([1, 128], F32)
        self.dve(lambda e: e.memset(ones1, 1.0), [], [Tl])
        neglam = A.alloc([128, 1], F32)
        self.mm(misc_ps[:, 0:2], ones1, l2[:, 0:2], True, True, [Tl], [Tmisc])
        self.cp(neglam, misc_ps[:, 0:1], [Tmisc], [Tl])
        gab = A.alloc([128, 128], F32)
        self.dma(gab, I["da_norm_g"].partition_broadcast(128), [], [Tl])
        self.ts(gab, gab, 1.0 - lam_init, None, ALU.mult, [Tl], [Tl])

        ktrot = Rot(A, 2, [128, maxS], BF16, "kt")
        vxrot = Rot(A, 2, [128, maxS // 128, 132], BF16, "vx")
        for (vb, Tv) in vxrot.bufs:
            self.pool(lambda e, vb=vb: e.memset(vb, 1.0), [], [Tv])
        qrot = Rot(A, 2, [128, 512], BF16, "q")
        ptrot = Rot(A, 3, [128, 512], BF16, "pt")
        osb = [(A.alloc([128, 128], F32), T("osb%d" % i)) for i in range(4)]
        o2rot = Rot(A, 2, [128, 128], F32, "o2")
        obrot = Rot(A, 2, [128, 128], BF16, "ob")
        rrot = Rot(A, 4, [128, 2], F32, "rz")
        junk = A.alloc([128, 128], F32)
        Tjunk = T("junkB")
        oaTrot = Rot(A, 2, [128, 512], BF16, "oaT")
        tpb, Ttp = self.banks[0][:, :].bitcast(BF16), self.Tbank[0]
        lg_ps = [(self.banks[1][:, :], self.Tbank[1]), (self.banks[2][:, :], self.Tbank[2])]
        acc_ps = [(self.banks[3 + i][:, :], self.Tbank[3 + i]) for i in range(4)]
        li = [0]
        tok = 0
        for S_len in self.seqs:
            nkb = S_len // 128
            for h in range(4):
                kt, Tkt = ktrot.next()
                vx, Tvx = vxrot.next()
                self.dma(kt[:, 0:S_len], S["KT"][h, :, tok:tok + S_len], [TS["KT"]], [Tkt])
                self.dma(vx[:, 0:nkb, 0:128], S["VA"][tok:tok + S_len, h * 128:(h + 1) * 128].rearrange("(kb p) v -> p kb v", p=128),
                         [TS["VA"]], [Tvx])
                steps = []
                for q0 in range(0, S_len, 512):
                    nq = min(512, S_len - q0)
                    ctx = {"q0": q0, "nq": nq}
                    for c in range(2):
                        for kb in range(nkb):
                            steps.append((ctx, c, kb))

                def prep(ctx):
                    if "qt" not in ctx:
                        ctx["qt"], ctx["Tq"] = qrot.next()
                        self.dma(ctx["qt"][:, 0:ctx["nq"]], S["QT"][h, :, tok + ctx["q0"]:tok + ctx["q0"] + ctx["nq"]], [TS["QT"]], [ctx["Tq"]])
                        ctx["oaT"], ctx["ToaT"] = oaTrot.next()

                def emit_lg(st):
                    ctx, c, kb = st
                    prep(ctx)
                    q0, nq, qt, Tq = ctx["q0"], ctx["nq"], ctx["qt"], ctx["Tq"]
                    pr = slice(c * 64, (c + 1) * 64)
                    k0 = kb * 128
                    delta = k0 - q0
                    near = (-256 < delta < 640)
                    lg, Tlg = lg_ps[li[0] % 2]
                    li[0] += 1
                    self.mm(lg[:, 0:nq], kt[pr, k0:k0 + 128], qt[pr, 0:nq], True, not near, [Tkt, Tq], [Tlg])
                    pt, Tpt = ptrot.next()
                    if near:
                        stt_ = 512 - delta + 1
                        self.mm(lg[:, 0:nq], Jb, TB[:, h, stt_:stt_ + nq], False, True, [Tc, TTB], [Tlg])
                        self.act(pt[:, 0:nq], lg[:, 0:nq], AF.Exp, [Tlg], [Tpt])
                    else:
                        self.act(pt[:, 0:nq], lg[:, 0:nq], AF.Exp, [Tlg, Tcf], [Tpt],
                                 bias=cfar[:, h, (0 if delta < 0 else 1):(1 if delta < 0 else 2)])
                    return pt, Tpt

                def emit_av(st, pt, Tpt):
                    ctx, c, kb = st
                    nq = ctx["nq"]
                    nqb = nq // 128
                    for qb in range(nqb):
                        ac, Tac = acc_ps[qb]
                        self.mm(ac[:, 0:129], pt[:, qb * 128:(qb + 1) * 128], vx[:, kb, 0:129], kb == 0, kb == nkb - 1, [Tpt, Tvx], [Tac])
                    if kb != nkb - 1:
                        return
                    for qb in range(nqb):
                        ac, Tac = acc_ps[qb]
                        ob_, Tob_ = osb[qb]
                        rz, Trz = rrot.next()
                        self.dve(lambda e, rz=rz, ac=ac: e.reciprocal(out=rz[:, 0:1], in_=ac[:, 128:129]), [Tac], [Trz])
                        if c == 0:
                            self.ts(ob_, ac[:, 0:128], rz[:, 0:1], None, ALU.mult, [Tac, Trz], [Tob_])
                        else:
                            self.ts(rz[:, 0:1], rz[:, 0:1], neglam[:, 0:1], None, ALU.mult, [Trz, Tl], [Trz])
                            o2, To2 = o2rot.next()
                            self._stt(o2, ac[:, 0:128], rz[:, 0:1], ob_, ALU.mult, ALU.add, [Tac, Trz, Tob_], [To2])
                            self.act(junk, o2, AF.Square, [To2], [Tjunk, Trz], accum_out=rz[:, 1:2])
                            self.act(rz[:, 1:2], rz[:, 1:2], AF.Sqrt, [Trz], [Trz], scale=1.0 / 128, bias=self.epsc)
                            self.dve(lambda e, rz=rz: e.reciprocal(out=rz[:, 1:2], in_=rz[:, 1:2]), [Trz], [Trz])
                            ob, Tob = obrot.next()
                            self._stt(ob, o2, rz[:, 1:2], gab, ALU.mult, ALU.mult, [To2, Trz, Tl], [Tob])
                            self.tr(tpb[:, qb * 128:(qb + 1) * 128], ob, self.ident, [Tob, Tc], [Ttp])
                    if c == 1:
                        q0 = ctx["q0"]
                        self.act(ctx["oaT"][:, 0:nq], tpb[:, 0:nq], AF.Copy, [Ttp], [ctx["ToaT"]])
                        self.dma(S["OAT"][h, :, tok + q0:tok + q0 + nq], ctx["oaT"][:, 0:nq], [ctx["ToaT"]], [TS["OAT"]])

                cur = emit_lg(steps[0])
                for i in range(len(steps)):
                    nxt = emit_lg(steps[i + 1]) if i + 1 < len(steps) else None
                    emit_av(steps[i], cur[0], cur[1])
                    cur = nxt
            tok += S_len

    def phaseM(self):
        nc, I, S, TS = self.nc, self.I, self.S, self.TS
        A = self.arena
        A.reset()
        Tc = self.Tc
        stage = Rot(A, 2, [128, 8, 512], F32, "wstage")
        wua = A.alloc([128, 4, D], BF16)
        wub = A.alloc([128, 4, D], BF16)
        wout = A.alloc([128, 8, D], BF16)
        Tw = T("wM")
        self.load_cast_weight(wua, I["w_up_a"], 4, D, None, stage, Tw)
        self.load_cast_weight(wub, I["w_up_b"], 4, D, None, stage, Tw)
        self.load_cast_weight(wout, I["w_out"], 8, D, None, stage, Tw)
        oarot = Rot(A, 2, [128, 4, 512], BF16, "oat")
        ogrot = Rot(A, 2, [128, 4, 512], BF16, "ogt")
        sgrot = Rot(A, 2, [128, 16, 512], BF16, "sgt")
        m1 = A.alloc([128, 512], F32)
        m2 = A.alloc([128, 512], F32)
        Tm1, Tm2 = T("m1"), T("m2")
        mTrot = Rot(A, 2, [128, 8, 512], BF16, "mT")
        xrot = Rot(A, 2, [128, D], F32, "xM")
        orot = Rot(A, 2, [128, D], F32, "oM")
        ya_ps = [(self.banks[0][:, :], self.Tbank[0]), (self.banks[1][:, :], self.Tbank[1])]
        yb_ps = [(self.banks[2][:, :], self.Tbank[2]), (self.banks[3][:, :], self.Tbank[3])]
        out_ps = [((self.banks[4][:, :], self.banks[5][:, :]), self.Tbank[4]), ((self.banks[6][:, :], self.banks[7][:, :]), self.Tbank[6])]
        yi = 0
        oi = 0
        for g0 in range(0, self.NT, 512):
            ntok = min(512, self.NT - g0)
            oat, Toat = oarot.next()
            ogt, Togt = ogrot.next()
            sgt, Tsgt = sgrot.next()
            self.dma(oat[:, :, 0:ntok], S["OAT"][:, :, g0:g0 + ntok].rearrange("h p t -> p h t"), [TS["OAT"]], [Toat])
            self.dma(ogt[:, :, 0:ntok], S["OGT"][:, :, g0:g0 + ntok].rearrange("h p t -> p h t"), [TS["OGT"]], [Togt])
            self.dma(sgt[:, :, 0:ntok], S["SGT"][:, :, g0:g0 + ntok].rearrange("j p t -> p j t"), [TS["SGT"]], [Tsgt])
            mT, TmT = mTrot.next()
            for j in range(8):
                pa, Tpa = ya_ps[yi % 2]
                pb, Tpb = yb_ps[yi % 2]
                yi += 1
                for f in range(4):
                    self.mm(pa[:, 0:ntok], wua[:, f, j * 128:(j + 1) * 128], oat[:, f, 0:ntok], f == 0, f == 3, [Tw, Toat], [Tpa])
                for f in range(4):
                    self.mm(pb[:, 0:ntok], wub[:, f, j * 128:(j + 1) * 128], ogt[:, f, 0:ntok], f == 0, f == 3, [Tw, Togt], [Tpb])
                self.tt(m1[:, 0:ntok], pa[:, 0:ntok], sgt[:, j, 0:ntok], ALU.mult, [Tpa, Tsgt], [Tm1])
                self.tt(m2[:, 0:ntok], pb[:, 0:ntok], sgt[:, 8 + j, 0:ntok], ALU.mult, [Tpb, Tsgt], [Tm2])
                self.tt(mT[:, j, 0:ntok], m1[:, 0:ntok], m2[:, 0:ntok], ALU.add, [Tm1, Tm2], [TmT], eng="pool")
            for b in range(ntok // 128):
                r0 = g0 + b * 128
                xin, Txin = xrot.next()
                self.dma(xin, I["x"][r0:r0 + 128, :], [], [Txin])
                (p0, p1), Tp = out_ps[oi % 2]
                oi += 1
                for half, pp in enumerate((p0, p1)):
                    for j in range(8):
                        self.mm(pp, mT[:, j, b * 128:(b + 1) * 128], wout[:, j, half * 512:(half + 1) * 512], j == 0, j == 7, [TmT, Tw], [Tp])
                xo, Txo = orot.next()
                self.tt(xo[:, 0:512], p0, xin[:, 0:512], ALU.add, [Tp, Txin], [Txo])
                self.tt(xo[:, 512:1024], p1, xin[:, 512:1024], ALU.add, [Tp, Txin], [Txo])
                self.dma(S["X1"][r0:r0 + 128, :], xo, [Txo], [TS["X1"]])

    def phaseP(self):
        self.phaseP0()
        self.em.barrier()
        self.phaseP1()
        self.em.barrier()
        self.phaseP2()

    def phaseP0(self):
        nc, I, S, TS = self.nc, self.I, self.S, self.TS
        A = self.arena
        A.reset()
        Tc = self.Tc
        urot = Rot(A, 3, [128, D], F32, "u_in")
        ubrot = Rot(A, 2, [128, D], BF16, "u_bf")
        utrot = Rot(A, 3, [128, D], BF16, "uT")
        vrot = Rot(A, 3, [128, D], F32, "v_in")
        vbrot = Rot(A, 3, [128, D], BF16, "v_bf")
        tps = [(self.banks[0][:, :].bitcast(BF16), self.Tbank[0]), (self.banks[1][:, :].bitcast(BF16), self.Tbank[1])]
        def loads(i):
            u_in, Tu = urot.next()
            self.dma(u_in, I["peer_u"][i * 128:(i + 1) * 128, :], [], [Tu])
            v_in, Tv = vrot.next()
            self.dma(v_in, I["peer_v"][i * 128:(i + 1) * 128, :], [], [Tv])
            return u_in, Tu, v_in, Tv

        nxt = loads(0)
        for i in range(128):
            u_in, Tu, v_in, Tv = nxt
            if i + 1 < 128:
                nxt = loads(i + 1)
            u_bf, Tub = ubrot.next()
            self.cp(u_bf, u_in, [Tu], [Tub])
            tp, Ttp = tps[i % 2]
            for c in range(8):
                self.tr(tp[:, c * 128:(c + 1) * 128], u_bf[:, c * 128:(c + 1) * 128], self.ident, [Tub, Tc], [Ttp])
            uT, TuT = utrot.next()
            self.act(uT, tp, AF.Copy, [Ttp], [TuT])
            v_bf, Tvb = vbrot.next()
            self.cp(v_bf[:, 0:512], v_in[:, 0:512], [Tv], [Tvb], eng="pool")
            self.cp(v_bf[:, 512:1024], v_in[:, 512:1024], [Tv], [Tvb], eng="dve")
            self.dma(S["UT"][i], uT, [TuT], [TS["UT"]])
            self.dma(S["VB"][i], v_bf, [Tvb], [TS["VB"]])

    def phaseP1(self):
        nc, I, S, TS = self.nc, self.I, self.S, self.TS
        A = self.arena
        A.reset()
        Tc = self.Tc
        NT = self.NT
        stage = Rot(A, 2, [128, 8, 256], F32, "wstage")
        g2 = A.alloc([128, 8], F32)
        Tw = T("wP1")
        self.dma(g2, I["norm2_g"].rearrange("(c p) -> p c", p=128), [], [Tc], slow=True)
        wq = A.alloc([128, 8, 2048], BF16)
        self.load_cast_weight(wq, I["peer_w_q"], 8, 2048, g2, stage, Tw, col_piece=256)
        skf = A.alloc([128, 16, 128], F32)
        skb = A.alloc([128, 16, 128], BF16)
        skT = A.alloc([128, 16, 128], BF16)
        self.dma(skf, I["peer_sub_keys"].rearrange("g n d -> n g d"), [], [Tw])
        self.cp(skb, skf, [Tw], [Tw])
        tpb0, Ttp0 = self.banks[0][:, :].bitcast(BF16), self.Tbank[0]
        for half in range(2):
            for g in range(8):
                self.tr(tpb0[:, g * 128:(g + 1) * 128], skb[:, half * 8 + g, :], self.ident, [Tw, Tc], [Ttp0])
            self.cp(skT[:, half * 8:(half + 1) * 8, :], tpb0.rearrange("p (g n) -> p g n", g=8), [Ttp0], [Tw])
        io_i = A.alloc([128, 128], I32)
        self.iota128 = self.calloc(128)
        self.pool(lambda e: e.iota(io_i, pattern=[[1, 128]], base=0, channel_multiplier=0), [], [Tc])
        self.pool(lambda e: e.tensor_copy(out=self.iota128, in_=io_i), [Tc], [Tc])
        iota16 = self.iota128[:, 0:16]
        xrot = Rot(A, 2, [128, D], F32, "x1in")
        scr = {
            "junk": (A.alloc([128, D], F32), T("junkP")),
            "ss": Rot(A, 2, [128, 1], F32, "ssP"),
            "xn": Rot(A, 2, [128, D], BF16, "xnP"),
            "tp": (self.banks[0][:, :].bitcast(BF16), self.Tbank[0]),
        }
        hTrot = Rot(A, 2, [128, 8, 512], BF16, "h2T")
        qT_sb = A.alloc([128, 16, 512], BF16)
        TqT = T("qT_sb")
        q_ps = [(self.banks[1][:, :], self.Tbank[1]), (self.banks[2][:, :], self.Tbank[2])]
        s_ps = [(self.banks[3 + i][:, :], self.Tbank[3 + i]) for i in range(4)]
        t_ps, Tt_ps = self.banks[7][:, :], self.Tbank[7]
        ssb_rot = Rot(A, 2, [128, 2048], F32, "s_sb")
        v16 = A.alloc([128, 16, 16], F32)
        i16 = A.alloc([128, 16, 16], U32)
        Tv16 = [T("v16_%d" % g) for g in range(16)]
        Ti16 = [T("i16_%d" % g) for g in range(16)]
        s2 = A.alloc([128, 16, 128], F32)
        Ts2 = [T("s2_%d" % g) for g in range(16)]
        cand = A.alloc([128, 8, 256], F32)
        Tcand = T("cand")
        c2b = A.alloc([128, 8, 256], F32)
        Tc2 = [T("c2b_%d" % h) for h in range(8)]
        tv = A.alloc([128, 8, 16], F32)
        pos = A.alloc([128, 8, 16], U32)
        Ttvl = [T("tv_%d" % h) for h in range(8)]
        Tposl = [T("pos_%d" % h) for h in range(8)]
        ex = A.alloc([128, 8, 16], F32)
        Tex = T("ex")
        zz = A.alloc([128, 8], F32)
        Tzz = T("zz")
        au = A.alloc([128, 8, 16], U32)
        bu = A.alloc([128, 8, 16], U32)
        af = A.alloc([128, 8, 16], F32)
        bf = A.alloc([128, 8, 16], F32)
        I1f = A.alloc([128, 8, 16], F32)
        I2f = A.alloc([128, 8, 16], F32)
        Tdec = T("decode")
        eq = A.alloc([128, 8, 16, 16], F32)
        Teq = T("eq")
        tabs = A.alloc([128, 3, 128], F32)
        Ttabs = T("tabs")
        slTrot = Rot(A, 2, [128, 3, 128], F32, "slT")
        qi = 0
        for g0 in range(0, NT, 512):
            ntok = min(512, NT - g0)
            hT, ThT = hTrot.next()
            for b in range(ntok // 128):
                xin, Txin = xrot.next()
                self.dma(xin, S["X1"][g0 + b * 128:g0 + (b + 1) * 128, :], [TS["X1"]], [Txin])
                self.rmsnorm_T(xin, Txin, None, hT, ThT, b * 128, ntok, scr)
            self.dma(S["H2T"][:, :, g0:g0 + ntok].rearrange("c p t -> p c t"), hT[:, :, 0:ntok], [ThT], [TS["H2T"]])
            for hc in range(16):
                ps, Tps = q_ps[qi % 2]
                qi += 1
                for c in range(8):
                    self.mm(ps[:, 0:ntok], wq[:, c, hc * 128:(hc + 1) * 128], hT[:, c, 0:ntok], c == 0, c == 7, [Tw, ThT], [Tps])
                self.act(qT_sb[:, hc, 0:ntok], ps[:, 0:ntok], AF.Copy, [Tps], [TqT])
            for b in range(ntok // 128):
                r0 = g0 + b * 128
                for hc in range(16):
                    bk, Tbk = s_ps[hc // 4]
                    self.mm(bk[:, (hc % 4) * 128:(hc % 4 + 1) * 128], qT_sb[:, hc, b * 128:(b + 1) * 128], skT[:, hc, :], True, True, [TqT, Tw], [Tbk])
                s_sb, Tssb = ssb_rot.next()
                for q in range(4):
                    bk, Tbk = s_ps[q]
                    self.act(s_sb[:, q * 512:(q + 1) * 512], bk, AF.Copy, [Tbk], [Tssb])
                self._top16_multi([(s_sb[:, g * 128:(g + 1) * 128], Tssb, s2[:, g, :], Ts2[g], v16[:, g, :], Tv16[g], i16[:, g, :], Ti16[g])
                                   for g in range(16)])
                v16v = v16.rearrange("p (h c) k -> p h c k", c=2)
                self.tt(cand.rearrange("p h (a b) -> p h a b", a=16),
                        v16v[:, :, 0, :].unsqueeze(3).to_broadcast([128, 8, 16, 16]),
                        v16v[:, :, 1, :].unsqueeze(2).to_broadcast([128, 8, 16, 16]), ALU.add, Tv16, [Tcand])
                self._top16_multi([(cand[:, h, :], Tcand, c2b[:, h, :], Tc2[h], tv[:, h, :], Ttvl[h], pos[:, h, :], Tposl[h]) for h in range(8)])
                self.tt(ex, tv, tv[:, :, 0:1].to_broadcast([128, 8, 16]), ALU.subtract, Ttvl, [Tex])
                self.act(ex, ex, AF.Exp, [Tex], [Tex])
                self.dve(lambda e: e.tensor_reduce(out=zz, in_=ex, axis=AX.X, op=ALU.add), [Tex], [Tzz])
                self.dve(lambda e: e.reciprocal(out=zz, in_=zz), [Tzz], [Tzz])
                self.tt(tabs[:, 2, :].rearrange("p (h k) -> p h k", h=8), ex, zz.unsqueeze(2).to_broadcast([128, 8, 16]), ALU.mult,
                        [Tex, Tzz], [Ttabs])
                self.dve(lambda e: e.tensor_single_scalar(out=au, in_=pos, scalar=4, op=ALU.logical_shift_right), Tposl, [Tdec])
                self.dve(lambda e: e.tensor_single_scalar(out=bu, in_=pos, scalar=15, op=ALU.bitwise_and), Tposl, [Tdec])
                self.cp(af, au, [Tdec], [Tdec])
                self.cp(bf, bu, [Tdec], [Tdec])
                i16v = i16.rearrange("p (h c) k -> p h c k", c=2)
                self.cp(I1f, i16v[:, :, 0, :], Ti16, [Tdec])
                self.cp(I2f, i16v[:, :, 1, :], Ti16, [Tdec])
                for (sel, tabf, j) in ((af, I1f, 0), (bf, I2f, 1)):
                    self.tt(eq, iota16.unsqueeze(1).unsqueeze(1).to_broadcast([128, 8, 16, 16]),
                            sel.unsqueeze(3).to_broadcast([128, 8, 16, 16]), ALU.is_equal, [Tc, Tdec], [Teq])
                    self.tt(eq, eq, tabf.unsqueeze(2).to_broadcast([128, 8, 16, 16]), ALU.mult, [Teq, Tdec], [Teq])
                    self.dve(lambda e, j=j: e.tensor_reduce(out=tabs[:, j, :].rearrange("p (h k) -> p h k", h=8), in_=eq, axis=AX.X, op=ALU.add),
                             [Teq], [Ttabs])
                for j in range(3):
                    self.tr(t_ps[:, j * 128:(j + 1) * 128], tabs[:, j, :], self.identf, [Ttabs, Tc], [Tt_ps])
                slT, TslT = slTrot.next()
                self.cp(slT, t_ps[:, 0:384].rearrange("p (j t) -> p j t", j=3), [Tt_ps], [TslT])
                self.dma(S["SL"][:, :, r0:r0 + 128].rearrange("j p t -> p j t"), slT, [TslT], [TS["SL"]])

    def _top16_multi(self, items):
        for (src, Tsrc, tmp, Ttmp, vout, Tvout, iout, Tiout) in items:
            self.dve(lambda e, vout=vout, src=src: e.max(out=vout[:, 0:8], in_=src), [Tsrc], [Tvout])
        for (src, Tsrc, tmp, Ttmp, vout, Tvout, iout, Tiout) in items:
            self.dve(lambda e, vout=vout, src=src, iout=iout: e.max_index(out=iout[:, 0:8], in_max=vout[:, 0:8], in_values=src), [Tsrc, Tvout], [Tiout])
        for (src, Tsrc, tmp, Ttmp, vout, Tvout, iout, Tiout) in items:
            self.dve(lambda e, vout=vout, src=src, tmp=tmp: e.match_replace(out=tmp, in_to_replace=vout[:, 0:8], in_values=src, imm_value=-1e30), [Tsrc, Tvout], [Ttmp])
        for (src, Tsrc, tmp, Ttmp, vout, Tvout, iout, Tiout) in items:
            self.dve(lambda e, vout=vout, tmp=tmp: e.max(out=vout[:, 8:16], in_=tmp), [Ttmp], [Tvout])
        for (src, Tsrc, tmp, Ttmp, vout, Tvout, iout, Tiout) in items:
            self.dve(lambda e, vout=vout, tmp=tmp, iout=iout: e.max_index(out=iout[:, 8:16], in_max=vout[:, 8:16], in_values=tmp), [Ttmp, Tvout], [Tiout])

    def _top16(self, src, Tsrc, tmp, Ttmp, vout, Tvout, iout, Tiout):
        self.dve(lambda e: e.max(out=vout[:, 0:8], in_=src), [Tsrc], [Tvout])
        self.dve(lambda e: e.max_index(out=iout[:, 0:8], in_max=vout[:, 0:8], in_values=src), [Tsrc, Tvout], [Tiout])
        self.dve(lambda e: e.match_replace(out=tmp, in_to_replace=vout[:, 0:8], in_values=src, imm_value=-1e30), [Tsrc, Tvout], [Ttmp])
        self.dve(lambda e: e.max(out=vout[:, 8:16], in_=tmp), [Ttmp], [Tvout])
        self.dve(lambda e: e.max_index(out=iout[:, 8:16], in_max=vout[:, 8:16], in_values=tmp), [Ttmp, Tvout], [Tiout])

    def phaseP2(self):
        nc, I, S, TS = self.nc, self.I, self.S, self.TS
        A = self.arena
        A.reset()
        Tc = self.Tc
        NT = self.NT
        TT = 256
        Gsb = A.alloc([128, 128, TT], BF16)
        TG = T("Gsb")
        utrot = Rot(A, 2, [128, 8, 1024], BF16, "ut")
        vbrot = Rot(A, 2, [128, 8, 1024], BF16, "vb")
        wrot = Rot(A, 2, [128, 16, 128], BF16, "Wt")
        yrot = Rot(A, 2, [128, 16, 128], BF16, "Yt")
        hrot = Rot(A, 2, [128, 8, TT], BF16, "h2t")
        slrot = Rot(A, 2, [128, 3, TT], F32, "sl")
        agrot = Rot(A, 3, [128, TT], BF16, "ag")
        garot = Rot(A, 3, [128, TT], BF16, "ga")
        xrot = Rot(A, 2, [128, D], F32, "x1p")
        orot = Rot(A, 2, [128, D], F32, "x2p")
        out_ps = [(self.banks[i][:, :], self.Tbank[i]) for i in range(4)]
        a_ps = [(self.banks[4][:, :], self.Tbank[4]), (self.banks[5][:, :], self.Tbank[5])]
        g_ps = [(self.banks[6][:, :], self.Tbank[6]), (self.banks[7][:, :], self.Tbank[7])]
        iota_b = self.iota128.unsqueeze(1).to_broadcast([128, 16, 128])
        ai = 0
        gi = 0
        grp_bufs = {}
        ntiles = (NT + TT - 1) // TT

        def prefetch(G):
            if G in grp_bufs or G >= ntiles * 16:
                return
            grp = G % 16
            ut, Tut = utrot.next()
            vb, Tvb = vbrot.next()
            self.dma(ut, S["UT"][grp * 8:(grp + 1) * 8].rearrange("j p f -> p j f"), [TS["UT"]], [Tut])
            self.dma(vb, S["VB"][grp * 8:(grp + 1) * 8].rearrange("j p f -> p j f"), [TS["VB"]], [Tvb])
            grp_bufs[G] = (ut, Tut, vb, Tvb)

        prefetch(0)
        prefetch(1)
        for ti, g0 in enumerate(range(0, NT, TT)):
            nt = min(TT, NT - g0)
            h2t, Th = hrot.next()
            sl, Tsl = slrot.next()
            self.dma(h2t[:, :, 0:nt], S["H2T"][:, :, g0:g0 + nt].rearrange("c p t -> p c t"), [TS["H2T"]], [Th])
            self.dma(sl[:, :, 0:nt], S["SL"][:, :, g0:g0 + nt].rearrange("j p t -> p j t"), [TS["SL"]], [Tsl])
            for t0 in range(0, nt, 16):
                Wt, TW = wrot.next()
                Yt, TY = yrot.next()
                for t in range(16):
                    tc_ = t0 + t
                    self.ts(Wt[:, t, :], self.iota128, sl[:, 0, tc_:tc_ + 1], sl[:, 2, tc_:tc_ + 1], ALU.is_equal, [Tc, Tsl], [TW], op1=ALU.mult)
                    self.ts(Yt[:, t, :], self.iota128, sl[:, 1, tc_:tc_ + 1], None, ALU.is_equal, [Tc, Tsl], [TY])
                for q4 in range(4):
                    gp, Tgp = g_ps[gi % 2]
                    gi += 1
                    for tl in range(4):
                        t = q4 * 4 + tl
                        self.mm(gp[:, tl * 128:(tl + 1) * 128], Yt[:, t, :], Wt[:, t, :], True, True, [TY, TW], [Tgp])
                    ta = t0 + q4 * 4
                    self.act(Gsb[:, :, ta:ta + 4].rearrange("p i t -> p t i"), gp.rearrange("p (t i) -> p t i", t=4), AF.Copy, [Tgp], [TG])
            def get_grp(grp):
                return grp_bufs[ti * 16 + grp]

            def emit_u(i1):
                grp, j = i1 // 8, i1 % 8
                ut, Tut, vb, Tvb = get_grp(grp)
                ap_, Tap = a_ps[i1 % 2]
                for c in range(8):
                    self.mm(ap_[:, 0:nt], ut[:, j, c * 128:(c + 1) * 128], h2t[:, c, 0:nt], c == 0, c == 7, [Tut, Th], [Tap])
                ag, Tag = agrot.next()
                self.act(ag[:, 0:nt], ap_[:, 0:nt], AF.Gelu_apprx_tanh, [Tap], [Tag])
                ga, Tga = garot.next()
                self.tt(ga[:, 0:nt], ag[:, 0:nt], Gsb[:, i1, 0:nt], ALU.mult, [Tag, TG], [Tga])
                return ga, Tga

            def emit_v(i1, ga, Tga):
                grp, j = i1 // 8, i1 % 8
                ut, Tut, vb, Tvb = get_grp(grp)
                for tb in range(nt // 128):
                    for half in range(2):
                        op_, Top = out_ps[tb * 2 + half]
                        self.mm(op_, ga[:, tb * 128:(tb + 1) * 128], vb[:, j, half * 512:(half + 1) * 512], i1 == 0, i1 == 127, [Tga, Tvb], [Top])

            cur = emit_u(0)
            for i1 in range(128):
                nxt = emit_u(i1 + 1) if i1 + 1 < 128 else None
                emit_v(i1, cur[0], cur[1])
                cur = nxt
                if i1 % 8 == 7:
                    prefetch(ti * 16 + i1 // 8 + 2)
            for tb in range(nt // 128):
                r0 = g0 + tb * 128
                xin, Txin = xrot.next()
                self.dma(xin, S["X1"][r0:r0 + 128, :], [TS["X1"]], [Txin])
                xo, Txo = orot.next()
                for half in range(2):
                    op_, Top = out_ps[tb * 2 + half]
                    self.tt(xo[:, half * 512:(half + 1) * 512], op_, xin[:, half * 512:(half + 1) * 512], ALU.add, [Top, Txin], [Txo])
                self.dma(S["X2"][r0:r0 + 128, :], xo, [Txo], [TS["X2"]])

    def phaseE(self):
        nc, I, S, TS = self.nc, self.I, self.S, self.TS
        A = self.arena
        A.reset()
        Tc = self.Tc
        stage = Rot(A, 2, [128, 8, 512], F32, "wstage")
        g3 = A.alloc([128, 8], F32)
        Tw = T("wE")
        self.dma(g3, I["norm3_g"].rearrange("(c p) -> p c", p=128), [], [Tc], slow=True)
        wg = A.alloc([128, 8, D], BF16)
        wp = A.alloc([128, 2, D], BF16)
        self.load_cast_weight(wg, I["ple_gate_w"], 8, D, g3, stage, Tw)
        self.load_cast_weight(wp, I["ple_w"], 2, D, None, stage, Tw)
        gfb = A.alloc([128, D], F32)
        self.dma(gfb, I["final_norm_g"].partition_broadcast(128), [], [Tw])
        xrot = Rot(A, 2, [128, D], F32, "xE")
        prot = Rot(A, 2, [128, 256], F32, "pE")
        pbrot = Rot(A, 2, [128, 256], BF16, "pbE")
        pTrot = Rot(A, 2, [128, 2, 128], BF16, "pT")
        scr = {
            "junk": (A.alloc([128, D], F32), T("junkE")),
            "ss": Rot(A, 2, [128, 1], F32, "ssE"),
            "xn": Rot(A, 2, [128, D], BF16, "xnE"),
            "tp": (self.banks[0][:, :].bitcast(BF16), self.Tbank[0]),
        }
        hTrot = Rot(A, 2, [128, 8, 128], BF16, "h3T")
        gate = A.alloc([128, D], F32)
        Tgate = T("gate")
        x3rot = Rot(A, 2, [128, D], F32, "x3")
        yrot = Rot(A, 2, [128, D], F32, "yE")
        ssf = Rot(A, 2, [128, 1], F32, "ssf")
        junk2 = A.alloc([128, D], F32)
        Tj2 = T("junk2")
        g_ps = ((self.banks[1][:, :], self.banks[2][:, :]), self.Tbank[1])
        p_ps = ((self.banks[3][:, :], self.banks[4][:, :]), self.Tbank[3])
        tp2, Ttp2 = self.banks[5][:, :].bitcast(BF16), self.Tbank[5]
        for gb in range(self.NB):
            r0 = gb * 128
            xin, Txin = xrot.next()
            self.dma(xin, S["X2"][r0:r0 + 128, :], [TS["X2"]], [Txin])
            pin, Tpin = prot.next()
            self.dma(pin, I["p"][r0:r0 + 128, :], [], [Tpin])
            pb, Tpb = pbrot.next()
            self.cp(pb, pin, [Tpin], [Tpb], eng="pool")
            for c2 in range(2):
                self.tr(tp2[:, c2 * 128:(c2 + 1) * 128], pb[:, c2 * 128:(c2 + 1) * 128], self.ident, [Tpb, Tc], [Ttp2])
            pT, TpT = pTrot.next()
            self.cp(pT, tp2[:, 0:256].rearrange("p (c t) -> p c t", c=2), [Ttp2], [TpT])
            hT, ThT = hTrot.next()
            self.rmsnorm_T(xin, Txin, None, hT, ThT, 0, 128, scr)
            (g0_, g1_), Tg = g_ps
            (q0_, q1_), Tq = p_ps
            for half, pp in enumerate((g0_, g1_)):
                for c in range(8):
                    self.mm(pp, hT[:, c, :], wg[:, c, half * 512:(half + 1) * 512], c == 0, c == 7, [ThT, Tw], [Tg])
            for half, pp in enumerate((q0_, q1_)):
                for c2 in range(2):
                    self.mm(pp, pT[:, c2, :], wp[:, c2, half * 512:(half + 1) * 512], c2 == 0, c2 == 1, [TpT, Tw], [Tq])
            self.act(gate[:, 0:512], g0_, AF.Sigmoid, [Tg], [Tgate])
            self.act(gate[:, 512:1024], g1_, AF.Sigmoid, [Tg], [Tgate])
            x3, Tx3 = x3rot.next()
            self.tt(x3[:, 0:512], q0_, gate[:, 0:512], ALU.mult, [Tq, Tgate], [Tx3])
            self.tt(x3[:, 512:1024], q1_, gate[:, 512:1024], ALU.mult, [Tq, Tgate], [Tx3])
            self.tt(x3, x3, xin, ALU.add, [Tx3, Txin], [Tx3], eng="pool")
            sf, Tsf = ssf.next()
            self.act(junk2, x3, AF.Square, [Tx3], [Tj2, Tsf], accum_out=sf)
            self.act(sf, sf, AF.Sqrt, [Tsf], [Tsf], scale=1.0 / D, bias=self.epsc)
            self.dve(lambda e, sf=sf: e.reciprocal(out=sf, in_=sf), [Tsf], [Tsf])
            yo, Tyo = yrot.next()
            self._stt(yo, x3, sf, gfb, ALU.mult, ALU.mult, [Tx3, Tsf, Tw], [Tyo])
            self.dma(self.y[r0:r0 + 128, :], yo, [Tyo], [])


def t5_onehot():
    rel = 640 - np.arange(1280)
    nb, max_exact = 16, 8
    ret = (rel > 0).astype(np.int64) * nb
    n = np.abs(rel)
    nf = np.maximum(n, 1).astype(np.float32)
    large = max_exact + (np.log(nf / max_exact) / math.log(128 / max_exact) * (nb - max_exact)).astype(np.int32)
    large = np.minimum(large, nb - 1)
    bucket = ret + np.where(n < max_exact, n, large)
    oh = np.zeros((32, 1280), np.float32)
    oh[bucket, np.arange(1280)] = 1.0
    return oh


def core_inputs(inputs, c, seqs=SEQS_FULL):
    f = lambda a: np.ascontiguousarray(np.asarray(a, dtype=np.float32))
    xp = inputs["x_prompt"][c]
    xs0 = inputs["x_sample"][2 * c]
    xs1 = inputs["x_sample"][2 * c + 1]
    pp = inputs["p_prompt"][0, c]
    ps0 = inputs["p_sample"][0, 2 * c]
    ps1 = inputs["p_sample"][0, 2 * c + 1]
    m = {
        "x": f(np.concatenate([xp, xs0, xs1], axis=0)),
        "p": f(np.concatenate([pp, ps0, ps1], axis=0)),
    }
    m.update(shared_inputs(inputs))
    return m


def shared_inputs(inputs):
    f = lambda a: np.ascontiguousarray(np.asarray(a, dtype=np.float32))
    return {
        "rel_bias": f(inputs["rel_bias"]),
        "norm1_g": f(inputs["norm1_g"][0]),
        "w_in": f(inputs["w_in"][0]),
        "lambda_qk": f(inputs["lambda_qk"][0]),
        "da_norm_g": f(inputs["da_norm_g"][0]),
        "gla_alpha_w": f(inputs["gla_alpha_w"][0]),
        "gla_alpha_b": f(inputs["gla_alpha_b"][0]),
        "gla_norm_g": f(inputs["gla_norm_g"][0]),
        "w_up_a": f(inputs["w_up_a"][0]),
        "w_up_b": f(inputs["w_up_b"][0]),
        "w_out": f(inputs["w_out"][0]),
        "norm2_g": f(inputs["norm2_g"][0]),
        "peer_w_q": f(inputs["peer_w_q"][0]),
        "peer_sub_keys": f(np.asarray(inputs["peer_sub_keys"][0]).reshape(16, 128, 128)),
        "peer_u": f(inputs["peer_u"][0]),
        "peer_v": f(inputs["peer_v"][0]),
        "norm3_g": f(inputs["norm3_g"][0]),
        "ple_w": f(inputs["ple_w"][0]),
        "ple_gate_w": f(inputs["ple_gate_w"][0]),
        "final_norm_g": f(inputs["final_norm_g"]),
        "t5oh": t5_onehot(),
    }


def kernel(**inputs):
    nc = K(SEQS_FULL).build()
    in_maps = [core_inputs(inputs, c) for c in range(8)]
    res = run_bass_kernel_spmd(nc, in_maps, core_ids=list(range(8)))
    yp = np.zeros((8, 4096, D), np.float32)
    ys = np.zeros((16, 2048, D), np.float32)
    for c in range(8):
        y = res.results[c]["y"]
        yp[c] = y[0:4096]
        ys[2 * c] = y[4096:6144]
        ys[2 * c + 1] = y[6144:8192]
    return (yp, ys)
```

```python
import math
import numpy as np
from contextlib import ExitStack
import concourse.bass as bass
import concourse.mybir as mybir
from concourse.bass_utils import run_bass_kernel_spmd

F32 = mybir.dt.float32
BF16 = mybir.dt.bfloat16
U32 = mybir.dt.uint32
I32 = mybir.dt.int32
AF = mybir.ActivationFunctionType
ALU = mybir.AluOpType
AX = mybir.AxisListType

D = 1024
IN_DIM = 5152
C_QA, C_KA, C_VA, C_QG, C_KG, C_VG, C_RG, C_LR, C_GL = 0, 512, 1024, 1536, 1792, 2048, 2560, 3072, 3104
EPS = 1e-6
NE = 16384
SEQS_FULL = (4096, 2048, 2048)


class T:
    __slots__ = ("name", "w", "r")

    def __init__(self, name=""):
        self.name = name
        self.w = {}
        self.r = {}


class Op:
    __slots__ = ("eng", "key", "fn", "deps", "ticket", "needed", "is_dma", "order")

    def __init__(self, eng, key, fn, is_dma):
        self.eng = eng
        self.key = key
        self.fn = fn
        self.deps = {}
        self.ticket = None
        self.needed = False
        self.is_dma = is_dma
        self.order = 0


class Emitter:
    NRING = 32
    ENGS = ("pe", "act", "dve", "pool", "sp")

    def __init__(self):
        self.progs = {k: [] for k in self.ENGS}
        self.ring_last = [None] * self.NRING
        self.ring_i = 0
        self.all_dma = []
        self.last_op = {}
        self.pending = {k: [] for k in self.ENGS}
        self.nops = 0

    def _add_dep(self, op, d):
        if d is op:
            return
        if d.key == op.key and op.key == "pe":
            return
        cur = op.deps.get(d.key)
        if cur is None or cur.order < d.order:
            op.deps[d.key] = d

    def op(self, eng, fn, reads=(), writes=(), dma=False):
        self.nops += 1
        if dma:
            slot = self.ring_i % self.NRING
            self.ring_i += 1
            key = "dma%d" % slot
        else:
            key = eng
        o = Op(eng, key, fn, dma)
        o.order = self.nops
        if dma:
            prev = self.ring_last[slot]
            self.ring_last[slot] = o
            if prev is not None:
                o.deps[key] = prev
            self.all_dma.append(o)
        for d in self.pending[eng]:
            self._add_dep(o, d)
        self.pending[eng] = []
        for t in reads:
            for k, d in t.w.items():
                self._add_dep(o, d)
        for t in writes:
            for k, d in t.w.items():
                if k == eng and not dma and eng != "pool":
                    continue
                self._add_dep(o, d)
            for k, d in t.r.items():
                if k == eng and not dma and eng != "pool":
                    continue
                self._add_dep(o, d)
        for t in reads:
            t.r[key] = o
        for t in writes:
            t.w[key] = o
        self.progs[eng].append(o)
        self.last_op[key] = o
        return o

    def barrier(self):
        lasts = list(self.last_op.values())
        for k in self.ENGS:
            self.pending[k] = list(lasts)

    def finalize(self):
        for prog in self.progs.values():
            for o in prog:
                for d in o.deps.values():
                    d.needed = True
        counters = {}
        for prog in self.progs.values():
            for o in prog:
                if o.needed and not o.is_dma:
                    counters[o.key] = counters.get(o.key, 0) + 1
                    o.ticket = counters[o.key]
        for o in self.all_dma:
            o.needed = True
            counters[o.key] = counters.get(o.key, 0) + 1
            o.ticket = counters[o.key]

    EPOCH = 16000

    def sem_of(self, sems, op):
        if op.is_dma:
            return sems[op.key], op.ticket * 16, (op.key, 0)
        ep, loc = divmod(op.ticket - 1, self.EPOCH)
        return sems["%s_%d" % (op.key, ep)], loc + 1, (op.key, ep)

    def n_epochs(self, key):
        mx = 0
        for o in self.progs[key]:
            if o.ticket is not None and not o.is_dma:
                mx = max(mx, o.ticket)
        return (mx - 1) // self.EPOCH + 1 if mx else 1

    def emit_engine(self, eng, engine_obj, sems, final_wait=False):
        waited = {}
        for o in self.progs[eng]:
            for k, d in o.deps.items():
                sem, val, wk = self.sem_of(sems, d)
                if waited.get(wk, 0) >= val:
                    continue
                engine_obj.wait_ge(sem, val)
                waited[wk] = val
            ins = o.fn(engine_obj)
            if o.needed:
                sem, _, _ = self.sem_of(sems, o)
                ins.then_inc(sem, 16 if o.is_dma else 1)
        if final_wait:
            for slot in range(self.NRING):
                last = self.ring_last[slot]
                if last is not None:
                    val = last.ticket * 16
                    if waited.get((last.key, 0), 0) < val:
                        engine_obj.wait_ge(sems[last.key], val)
                        waited[(last.key, 0)] = val


class Arena:
    def __init__(self, ap, nelem):
        self.ap = ap
        self.n = nelem
        self.off = 0

    def reset(self):
        self.off = 0

    def alloc(self, shape, dt):
        esz = 4 if dt in (F32, U32, I32) else 2
        n = 1
        for s in shape[1:]:
            n *= s
        nb = n * esz
        nb = (nb + 31) // 32 * 32
        assert self.off * 2 + nb <= self.n * 2, "arena overflow: need %d have %d" % (self.off * 2 + nb, self.n * 2)
        v = self.ap[:, self.off:self.off + nb // 2]
        self.off += nb // 2
        if esz == 4:
            v = v.bitcast(dt)
        elif dt != BF16:
            v = v.bitcast(dt)
        v = v[:, 0:n]
        if len(shape) > 2:
            names = " ".join("d%d" % i for i in range(len(shape) - 1))
            kw = {"d%d" % i: shape[i + 1] for i in range(len(shape) - 2)}
            v = v.rearrange("p (%s) -> p %s" % (names, names), **kw)
        if shape[0] < 128:
            v = v[0:shape[0]]
        return v


class Rot:
    def __init__(self, arena, n, shape, dt, name):
        self.bufs = [(arena.alloc(shape, dt), T("%s%d" % (name, i))) for i in range(n)]
        self.i = 0

    def next(self):
        b = self.bufs[self.i % len(self.bufs)]
        self.i += 1
        return b


class K:
    def __init__(self, seqs, debug=False, phases="0AGBMPE"):
        self.seqs = tuple(seqs)
        self.NT = sum(seqs)
        self.NB = self.NT // 128
        self.debug = debug
        self.phases = phases
        self.nc = bass.Bass("TRN2", target_bir_lowering=False)
        self.em = Emitter()
        self.dbg_names = []

    def act(self, out, in_, func, R, W, **kw):
        self.em.op("act", lambda e: e.activation(out=out, in_=in_, func=func, **kw), R, W)

    def dve(self, fn, R, W):
        self.em.op("dve", fn, R, W)

    def pool(self, fn, R, W):
        self.em.op("pool", fn, R, W)

    def mm(self, out, lhsT, rhs, start, stop, R, W):
        self.em.op("pe", lambda e: e.matmul(out, lhsT=lhsT, rhs=rhs, start=start, stop=stop), R, W)

    def tr(self, out, in_, ident, R, W):
        self.em.op("pe", lambda e: e.transpose(out=out, in_=in_, identity=ident), R, W)

    def dma(self, out, in_, R, W, q="sp", slow=False):
        q = "sp"
        if slow:
            self.em.op(q, lambda e: e.dma_start(out=out, in_=in_, allow_slow_non_contiguous=True), R, W, dma=True)
        else:
            self.em.op(q, lambda e: e.dma_start(out=out, in_=in_), R, W, dma=True)

    def tt(self, out, in0, in1, op, R, W, eng="dve"):
        self.em.op(eng, lambda e: e.tensor_tensor(out=out, in0=in0, in1=in1, op=op), R, W)

    def ts(self, out, in0, s1, s2, op0, R, W, op1=None, eng="dve", **kw):
        if op1 is None:
            self.em.op(eng, lambda e: e.tensor_scalar(out=out, in0=in0, scalar1=s1, scalar2=s2, op0=op0, **kw), R, W)
        else:
            self.em.op(eng, lambda e: e.tensor_scalar(out=out, in0=in0, scalar1=s1, scalar2=s2, op0=op0, op1=op1, **kw), R, W)

    def cp(self, out, in_, R, W, eng="dve"):
        self.em.op(eng, lambda e: e.tensor_copy(out=out, in_=in_), R, W)

    def dram_in(self, name, shape, dt=F32):
        return self.nc.dram_tensor(name, list(shape), dt, kind="ExternalInput").ap()

    def scratch(self, name, shape, dt):
        kind = "ExternalOutput" if (self.debug and name in self.debug) else "Internal"
        if kind == "ExternalOutput":
            self.dbg_names.append(name)
        return self.nc.dram_tensor(name, list(shape), dt, kind=kind).ap()

    def build(self):
        nc, em = self.nc, self.em
        NT, NB = self.NT, self.NB
        I = {}
        I["x"] = self.dram_in("x", [NT, D])
        I["p"] = self.dram_in("p", [NT, 256])
        I["rel_bias"] = self.dram_in("rel_bias", [32, 4])
        I["norm1_g"] = self.dram_in("norm1_g", [D])
        I["w_in"] = self.dram_in("w_in", [D, IN_DIM])
        I["lambda_qk"] = self.dram_in("lambda_qk", [4, 64])
        I["da_norm_g"] = self.dram_in("da_norm_g", [128])
        I["gla_alpha_w"] = self.dram_in("gla_alpha_w", [2, 16, 256])
        I["gla_alpha_b"] = self.dram_in("gla_alpha_b", [2, 256])
        I["gla_norm_g"] = self.dram_in("gla_norm_g", [128])
        I["w_up_a"] = self.dram_in("w_up_a", [512, D])
        I["w_up_b"] = self.dram_in("w_up_b", [512, D])
        I["w_out"] = self.dram_in("w_out", [D, D])
        I["norm2_g"] = self.dram_in("norm2_g", [D])
        I["peer_w_q"] = self.dram_in("peer_w_q", [D, 2048])
        I["peer_sub_keys"] = self.dram_in("peer_sub_keys", [16, 128, 128])
        I["peer_u"] = self.dram_in("peer_u", [NE, D])
        I["peer_v"] = self.dram_in("peer_v", [NE, D])
        I["norm3_g"] = self.dram_in("norm3_g", [D])
        I["ple_w"] = self.dram_in("ple_w", [256, D])
        I["ple_gate_w"] = self.dram_in("ple_gate_w", [D, D])
        I["final_norm_g"] = self.dram_in("final_norm_g", [D])
        I["t5oh"] = self.dram_in("t5oh", [32, 1280])
        self.I = I
        self.y = nc.dram_tensor("y", [NT, D], F32, kind="ExternalOutput").ap()

        S = {}
        S["QT"] = self.scratch("QT", [4, 128, NT], BF16)
        S["KT"] = self.scratch("KT", [4, 128, NT], BF16)
        S["VA"] = self.scratch("VA", [NT, 512], BF16)
        S["GQ"] = self.scratch("GQ", [NB, 128, 512], BF16)
        S["GK"] = self.scratch("GK", [NB, 128, 512], BF16)
        S["GD"] = self.scratch("GD", [NB, 128, 512], BF16)
        S["GV"] = self.scratch("GV", [NT, 512], BF16)
        S["SR"] = self.scratch("SR", [NT, 512], BF16)
        S["SGT"] = self.scratch("SGT", [16, 128, NT], BF16)
        S["OAT"] = self.scratch("OAT", [4, 128, NT], BF16)
        S["OGT"] = self.scratch("OGT", [4, 128, NT], BF16)
        S["X1"] = self.scratch("X1", [NT, D], F32)
        S["H2T"] = self.scratch("H2T", [8, 128, NT], BF16)
        S["SL"] = self.scratch("SL", [3, 128, NT], F32)
        S["X2"] = self.scratch("X2", [NT, D], F32)
        S["UT"] = self.scratch("UT", [128, 128, 1024], BF16)
        S["VB"] = self.scratch("VB", [128, 128, 1024], BF16)
        S["BV"] = self.scratch("BV", [4, 1280], F32)
        self.S = S
        self.TS = {k: T("S_" + k) for k in S}

        with ExitStack() as es:
            self.es = es
            arena_t = es.enter_context(nc.sbuf_tensor("arena", [128, 94000], BF16))
            self.arena = Arena(arena_t[:, :], 94000)
            cons_t = es.enter_context(nc.sbuf_tensor("cons", [128, 5000], F32))
            self.cons = cons_t
            self.cons_off = 0
            self.banks = [es.enter_context(nc.psum_tensor("bank%d" % i, [128, 512], F32)) for i in range(8)]
            self.Tbank = [T("bank%d" % i) for i in range(8)]

            self.phase0()
            if "A" in self.phases:
                em.barrier()
                self.phaseA()
            if "G" in self.phases:
                em.barrier()
                self.phaseG()
            if "B" in self.phases:
                em.barrier()
                self.phaseB()
            if "M" in self.phases:
                em.barrier()
                self.phaseM()
            if "P" in self.phases:
                em.barrier()
                self.phaseP()
            if "E" in self.phases:
                em.barrier()
                self.phaseE()

            em.finalize()
            sems = {}
            for k in ["dma%d" % i for i in range(em.NRING)]:
                sems[k] = es.enter_context(nc.semaphore("s_" + k))
            for k in ["pe", "act", "dve", "pool"]:
                for ep in range(em.n_epochs(k)):
                    sems["%s_%d" % (k, ep)] = es.enter_context(nc.semaphore("s_%s_%d" % (k, ep)))
            with nc.Block() as block:
                @block.sync
                def _(e):
                    em.emit_engine("sp", e, sems, final_wait=True)

                @block.scalar
                def _(e):
                    em.emit_engine("act", e, sems)

                @block.vector
                def _(e):
                    em.emit_engine("dve", e, sems)

                @block.gpsimd
                def _(e):
                    em.emit_engine("pool", e, sems, final_wait=True)

                @block.tensor
                def _(e):
                    em.emit_engine("pe", e, sems)
        return nc

    def calloc(self, n):
        v = self.cons[:, self.cons_off:self.cons_off + n]
        self.cons_off += n
        assert self.cons_off <= 5000
        return v

    def phase0(self):
        nc, I = self.nc, self.I
        Tc = self.Tc = T("consts")
        self.identf = self.calloc(128)
        self.ident = self.calloc(64).bitcast(BF16)
        self.triF = self.calloc(128)
        self.triB = self.calloc(128)
        self.triSU = self.calloc(128)
        self.triSL = self.calloc(128)
        self.maskF = self.calloc(128)
        self.maskB = self.calloc(128)
        self.epsc = self.calloc(1)
        self.onec = self.calloc(1)
        po = lambda fn: self.pool(fn, [Tc], [Tc])
        po(lambda e: e.memset(self.identf, 0.0))
        po(lambda e: e.affine_select(out=self.identf, in_=self.identf, compare_op=ALU.not_equal, fill=1.0,
                                     base=0, pattern=[[-1, 128]], channel_multiplier=1))
        po(lambda e: e.tensor_copy(out=self.ident, in_=self.identf))

        def tri(dst, val, cmul, step, base, cmp):
            po(lambda e: e.memset(dst, val))
            po(lambda e: e.affine_select(out=dst, in_=dst, compare_op=cmp, fill=0.0, base=base,
                                         pattern=[[step, 128]], channel_multiplier=cmul))
        tri(self.triF, -1.0 / 16, -1, 1, 0, ALU.is_ge)
        tri(self.triB, -1.0 / 16, 1, -1, 0, ALU.is_ge)
        tri(self.triSU, -1.0 / 16, 1, -1, 0, ALU.is_gt)
        tri(self.triSL, -1.0 / 16, -1, 1, 0, ALU.is_gt)
        tri(self.maskF, 1.0, -1, 1, 0, ALU.is_ge)
        tri(self.maskB, 1.0, 1, -1, 0, ALU.is_gt)
        po(lambda e: e.memset(self.epsc, EPS))
        po(lambda e: e.memset(self.onec, 1.0))

    def load_cast_weight(self, dst, src_ap, nchunk, ncols, gvec, stage_rot, Tdst, col_piece=512):
        for c0 in range(0, ncols, col_piece):
            cw = min(col_piece, ncols - c0)
            st, Tst = stage_rot.next()
            stv = st[:, 0:nchunk, 0:cw]
            self.dma(stv, src_ap[:, c0:c0 + cw].rearrange("(c p) n -> p c n", p=128), [], [Tst])
            for c in range(nchunk):
                if gvec is not None:
                    self.ts(dst[:, c, c0:c0 + cw], stv[:, c, :], gvec[:, c:c + 1], None, ALU.mult, [Tst, self.Tc], [Tdst],
                            eng=("dve" if c % 2 == 0 else "pool"))
                else:
                    self.cp(dst[:, c, c0:c0 + cw], stv[:, c, :], [Tst], [Tdst], eng=("dve" if c % 2 == 0 else "pool"))

    def rmsnorm_T(self, xin, Txin, gain_none_out_bf, hT_dst, ThT, tok0, ntok_tile, scr):
        junk, Tjunk = scr["junk"]
        ss, Tss = scr["ss"].next()
        xn, Txn = scr["xn"].next()
        tpb, Ttp = scr["tp"]
        self.act(junk, xin, AF.Square, [Txin], [Tjunk, Tss], accum_out=ss)
        self.act(ss, ss, AF.Sqrt, [Tss], [Tss], scale=1.0 / D, bias=self.epsc)
        self.dve(lambda e: e.reciprocal(out=ss, in_=ss), [Tss], [Tss])
        self.ts(xn, xin, ss, None, ALU.mult, [Txin, Tss], [Txn])
        for c in range(8):
            self.tr(tpb[:, c * 128:(c + 1) * 128], xn[:, c * 128:(c + 1) * 128], self.ident, [Txn, self.Tc], [Ttp])
        self.act(hT_dst[:, :, tok0:tok0 + 128], tpb.rearrange("p (c t) -> p c t", c=8), AF.Copy, [Ttp], [ThT])

    def phaseA(self):
        nc, I, S, TS = self.nc, self.I, self.S, self.TS
        A = self.arena
        A.reset()
        NT = self.NT
        Tc = self.Tc
        win = A.alloc([128, 8, IN_DIM], BF16)
        Twin = T("win")
        g1 = A.alloc([128, 8], F32)
        Tg1 = T("g1")
        self.dma(g1, I["norm1_g"].rearrange("(c p) -> p c", p=128), [], [Tc], slow=True)
        stage = Rot(A, 2, [128, 8, 512], F32, "wstage")
        self.load_cast_weight(win, I["w_in"], 8, IN_DIM, g1, stage, Twin)
        awf = A.alloc([33, 512], F32)
        aw = A.alloc([33, 512], BF16)
        Taw = T("aw")
        self.pool(lambda e: e.memset(awf, 0.0), [], [Taw])
        self.dma(awf[0:16, 0:256], I["gla_alpha_w"][0], [], [Taw])
        self.dma(awf[16:32, 256:512], I["gla_alpha_w"][1], [], [Taw])
        self.dma(awf[32:33, :], I["gla_alpha_b"].rearrange("(o j) k -> o (j k)", o=1), [], [Taw])
        self.cp(aw, awf, [Taw], [Taw])

        xrot = Rot(A, 2, [128, D], F32, "xin")
        scr = {
            "junk": (A.alloc([128, D], F32), T("junk")),
            "ss": Rot(A, 2, [128, 1], F32, "ss"),
            "xn": Rot(A, 2, [128, D], BF16, "xn"),
            "tp": (self.banks[0][:, :].bitcast(BF16), self.Tbank[0]),
        }
        hTrot = Rot(A, 2, [128, 8, 512], BF16, "hT")
        fm_ps = [(self.banks[1][:, :], self.Tbank[1]), (self.banks[2][:, :], self.Tbank[2])]
        tm_ps = [(self.banks[3][:, :], self.Tbank[3]), (self.banks[4][:, :], self.Tbank[4])]
        z_ps = (self.banks[5][:, :], self.Tbank[5])
        d_ps = (self.banks[6][:, :], self.Tbank[6])
        bt_ps = (self.banks[7][:, :], self.Tbank[7])
        fmrot = Rot(A, 3, [128, 512], BF16, "fmout")
        qg_sb = A.alloc([128, 2, 512], F32)
        Tqg = T("qg_sb")
        kg_sb = A.alloc([128, 2, 512], F32)
        Tkg = T("kg_sb")
        lrT = A.alloc([33, 512], BF16)
        TlrT = T("lrT")
        self.pool(lambda e: e.memset(lrT[32:33, :], 1.0), [], [TlrT])
        tmrot = Rot(A, 3, [128, 512], BF16, "tmout")
        e1 = A.alloc([128, 512], F32)
        Te1 = T("e1")
        Lt = A.alloc([128, 512], F32)
        TL = T("L")
        kde = A.alloc([128, 512], F32)
        Tkde = T("kde")
        epos = A.alloc([128, 512], F32)
        Tep = T("epos")
        eneg = A.alloc([128, 512], F32)
        Ten = T("eneg")
        gqrot = Rot(A, 2, [128, 512], BF16, "gq")
        gkrot = Rot(A, 2, [128, 512], BF16, "gk")
        gdrot = Rot(A, 2, [128, 512], BF16, "gd")
        self.DEC = self.calloc(self.NB * 4)
        self.TDEC = T("DEC")

        fmi = [0]
        tmi = [0]

        def fm_proj(hT, ThT, col0, M, ntok):
            ps, Tps = fm_ps[fmi[0] % 2]
            fmi[0] += 1
            for c in range(8):
                self.mm(ps[0:M, 0:ntok], win[:, c, col0:col0 + M], hT[:, c, 0:ntok], c == 0, c == 7, [Twin, ThT], [Tps])
            return ps, Tps

        def tm_proj(hT, ThT, col0, ncols, t0):
            ps, Tps = tm_ps[tmi[0] % 2]
            tmi[0] += 1
            for c in range(8):
                self.mm(ps[:, 0:ncols], hT[:, c, t0:t0 + 128], win[:, c, col0:col0 + ncols], c == 0, c == 7, [ThT, Twin], [Tps])
            return ps, Tps

        tok = 0
        for S_len in self.seqs:
            for tile0 in range(0, S_len, 512):
                ntok = min(512, S_len - tile0)
                g0 = tok + tile0
                hT, ThT = hTrot.next()
                for b in range(ntok // 128):
                    xin, Txin = xrot.next()
                    self.dma(xin, I["x"][g0 + b * 128:g0 + (b + 1) * 128, :], [], [Txin])
                    self.rmsnorm_T(xin, Txin, None, hT, ThT, b * 128, ntok, scr)
                for h in range(4):
                    ps, Tps = fm_proj(hT, ThT, C_QA + h * 128, 128, ntok)
                    o, To = fmrot.next()
                    self.act(o[:, 0:ntok], ps[:, 0:ntok], AF.Copy, [Tps], [To], scale=0.125)
                    self.dma(S["QT"][h, :, g0:g0 + ntok], o[:, 0:ntok], [To], [TS["QT"]], q="pool")
                    ps, Tps = fm_proj(hT, ThT, C_KA + h * 128, 128, ntok)
                    o, To = fmrot.next()
                    self.act(o[:, 0:ntok], ps[:, 0:ntok], AF.Copy, [Tps], [To])
                    self.dma(S["KT"][h, :, g0:g0 + ntok], o[:, 0:ntok], [To], [TS["KT"]], q="pool")
                for c2 in range(2):
                    ps, Tps = fm_proj(hT, ThT, C_QG + c2 * 128, 128, ntok)
                    self.act(qg_sb[:, c2, 0:ntok], ps[:, 0:ntok], AF.Copy, [Tps], [Tqg], scale=0.125)
                    ps, Tps = fm_proj(hT, ThT, C_KG + c2 * 128, 128, ntok)
                    self.act(kg_sb[:, c2, 0:ntok], ps[:, 0:ntok], AF.Copy, [Tps], [Tkg])
                ps, Tps = fm_proj(hT, ThT, C_LR, 32, ntok)
                self.act(lrT[0:32, 0:ntok], ps[0:32, 0:ntok], AF.Copy, [Tps], [TlrT])
                for j in range(16):
                    ps, Tps = fm_proj(hT, ThT, C_GL + j * 128, 128, ntok)
                    o, To = fmrot.next()
                    self.act(o[:, 0:ntok], ps[:, 0:ntok], AF.Sigmoid, [Tps], [To])
                    self.dma(S["SGT"][j, :, g0:g0 + ntok], o[:, 0:ntok], [To], [TS["SGT"]], q="pool")
                for b in range(ntok // 128):
                    gb = (g0 + b * 128) // 128
                    r0 = g0 + b * 128
                    t0 = b * 128
                    ps, Tps = tm_proj(hT, ThT, C_VA, 512, t0)
                    o, To = tmrot.next()
                    self.cp(o, ps, [Tps], [To])
                    self.dma(S["VA"][r0:r0 + 128, :], o, [To], [TS["VA"]], q="pool")
                    ps, Tps = tm_proj(hT, ThT, C_VG, 512, t0)
                    o, To = tmrot.next()
                    self.cp(o, ps, [Tps], [To])
                    self.dma(S["GV"][r0:r0 + 128, :], o, [To], [TS["GV"]], q="pool")
                    ps, Tps = tm_proj(hT, ThT, C_RG, 512, t0)
                    o, To = tmrot.next()
                    self.act(o, ps, AF.Silu, [Tps], [To])
                    self.dma(S["SR"][r0:r0 + 128, :], o, [To], [TS["SR"]], q="pool")
                    zp, Tzp = z_ps
                    self.mm(zp, lrT[0:33, t0:t0 + 128], aw[0:33, :], True, True, [TlrT, Taw], [Tzp])
                    self.act(e1, zp, AF.Exp, [Tzp], [Te1], scale=-1.0)
                    self.act(Lt, e1, AF.Ln, [Te1], [TL], bias=self.onec)
                    dp, Tdp = d_ps
                    self.mm(dp[:, 0:256], self.triSU, Lt[:, 0:256], True, True, [Tc, TL], [Tdp])
                    self.mm(dp[:, 256:512], self.triSL, Lt[:, 256:512], True, True, [Tc, TL], [Tdp])
                    self.act(kde, dp, AF.Exp, [Tdp], [Tkde])
                    kps, Tkps = tm_proj(hT, ThT, C_KG, 256, t0)
                    gd, Tgd = gdrot.next()
                    self.tt(gd.rearrange("p (j k) -> p j k", j=2), kde.rearrange("p (j k) -> p j k", j=2),
                            kps[:, 0:256].unsqueeze(1).to_broadcast([128, 2, 256]), ALU.mult, [Tkde, Tkps], [Tgd])
                    self.dma(S["GD"][gb], gd, [Tgd], [TS["GD"]], q="pool")
                    bp, Tbp = bt_ps
                    for dr in range(2):
                        for c2 in range(2):
                            idx = dr * 2 + c2
                            self.mm(bp[:, idx * 128:(idx + 1) * 128], Lt[:, dr * 256 + c2 * 128: dr * 256 + (c2 + 1) * 128],
                                    self.triF if dr == 0 else self.triB, True, True, [TL, Tc], [Tbp])
                    self.act(epos, bp, AF.Exp, [Tbp], [Tep])
                    self.act(eneg, bp, AF.Exp, [Tbp], [Ten], scale=-1.0)
                    gq, Tgq = gqrot.next()
                    gk, Tgk = gkrot.next()
                    self.tt(gq.rearrange("p (j c t) -> p j c t", j=2, c=2), epos.rearrange("p (j c t) -> p j c t", j=2, c=2),
                            qg_sb[:, :, t0:t0 + 128].unsqueeze(1).to_broadcast([128, 2, 2, 128]), ALU.mult, [Tep, Tqg], [Tgq])
                    self.tt(gk.rearrange("p (j c t) -> p j c t", j=2, c=2), eneg.rearrange("p (j c t) -> p j c t", j=2, c=2),
                            kg_sb[:, :, t0:t0 + 128].unsqueeze(1).to_broadcast([128, 2, 2, 128]), ALU.mult, [Ten, Tkg], [Tgk],
                            eng="pool")
                    self.dma(S["GQ"][gb], gq, [Tgq], [TS["GQ"]], q="pool")
                    self.dma(S["GK"][gb], gk, [Tgk], [TS["GK"]], q="pool")
                    ev = epos.rearrange("p (i t) -> p i t", i=4)
                    self.cp(self.DEC[:, gb * 4:gb * 4 + 2], ev[:, 0:2, 127], [Tep], [self.TDEC])
                    self.cp(self.DEC[:, gb * 4 + 2:gb * 4 + 4], ev[:, 2:4, 0], [Tep], [self.TDEC])
            tok += S_len

    def phaseG(self):
        nc, I, S, TS = self.nc, self.I, self.S, self.TS
        A = self.arena
        A.reset()
        Tc = self.Tc
        maxS = max(self.seqs)
        OF = A.alloc([128, maxS // 128, 512], F32)
        TOF = T("OF")
        gnb = A.alloc([128, 128], F32)
        Tgnb = T("gnb")
        self.dma(gnb, I["gla_norm_g"].partition_broadcast(128), [], [Tgnb])
        qrot = Rot(A, 2, [128, 2, 128], BF16, "gq_in")
        krot = Rot(A, 2, [128, 2, 128], BF16, "gk_in")
        drot = Rot(A, 2, [128, 256], BF16, "gd_in")
        vrot = Rot(A, 2, [128, 512], BF16, "gv_in")
        srrot = Rot(A, 2, [128, 512], BF16, "sr_in")
        attrot = Rot(A, 2, [128, 4, 128], BF16, "att_sb")
        st_f = A.alloc([128, 2, 128], F32)
        st_b = A.alloc([128, 2, 128], BF16)
        Tst = T("state")
        Tstb = T("state_bf")
        tot = A.alloc([128, 512], F32)
        Ttot = T("tot")
        sq = A.alloc([128, 512], F32)
        Tsq = T("sq")
        ssq = A.alloc([128, 4], F32)
        Tssq = T("ssq")
        srg = A.alloc([128, 512], F32)
        Tsrg = T("srg")
        ogrot = Rot(A, 2, [128, 512], BF16, "og")
        ogTrot = Rot(A, 2, [128, 4, 128], BF16, "ogT")
        tpb, Ttp = self.banks[0][:, :].bitcast(BF16), self.Tbank[0]
        att_ps = [(self.banks[1][:, :], self.Tbank[1]), (self.banks[2][:, :], self.Tbank[2])]
        o_ps = [(self.banks[3][:, :], self.Tbank[3]), (self.banks[4][:, :], self.Tbank[4])]
        kv_ps = [(self.banks[5][:, :], self.Tbank[5]), (self.banks[6][:, :], self.Tbank[6])]
        it = 0
        tok = 0
        for S_len in self.seqs:
            nbs = S_len // 128
            gb0 = tok // 128
            for dr in range(2):
                self.dve(lambda e: e.memset(st_f, 0.0), [], [Tst])
                self.pool(lambda e: e.memset(st_b, 0.0), [], [Tstb])
                order = range(nbs) if dr == 0 else range(nbs - 1, -1, -1)
                for lb in order:
                    gb = gb0 + lb
                    r0 = gb * 128
                    q_in, Tq = qrot.next()
                    k_in, Tk = krot.next()
                    d_in, Td = drot.next()
                    v_in, Tv = vrot.next()
                    self.dma(q_in, S["GQ"][gb][:, dr * 256:(dr + 1) * 256].rearrange("p (c t) -> p c t", c=2), [TS["GQ"]], [Tq])
                    self.dma(k_in, S["GK"][gb][:, dr * 256:(dr + 1) * 256].rearrange("p (c t) -> p c t", c=2), [TS["GK"]], [Tk])
                    self.dma(d_in, S["GD"][gb][:, dr * 256:(dr + 1) * 256], [TS["GD"]], [Td])
                    self.dma(v_in, S["GV"][r0:r0 + 128, :], [TS["GV"]], [Tv])
                    if dr == 1:
                        sr_in, Tsr = srrot.next()
                        self.dma(sr_in, S["SR"][r0:r0 + 128, :], [TS["SR"]], [Tsr])
                    kps, Tkps = kv_ps[it % 2]
                    it += 1
                    for h in range(4):
                        c2, par = h // 2, h % 2
                        ph = par * 64
                        aps, Taps = att_ps[par]
                        self.mm(aps[:, c2 * 128:(c2 + 1) * 128], k_in[ph:ph + 64, c2, :], q_in[ph:ph + 64, c2, :], True, True, [Tk, Tq], [Taps])
                    att, Tatt = attrot.next()
                    mask = self.maskF if dr == 0 else self.maskB
                    attv = att.rearrange("p (c par) t -> p c par t", par=2)
                    for par in range(2):
                        aps, Taps = att_ps[par]
                        self.tt(attv[:, :, par, :], aps[:, 0:256].rearrange("p (c t) -> p c t", c=2),
                                mask.unsqueeze(1).to_broadcast([128, 2, 128]), ALU.mult, [Taps, Tc], [Tatt])
                    for h in range(4):
                        c2, par = h // 2, h % 2
                        ph = par * 64
                        ops, Tops = o_ps[par]
                        self.mm(ops[:, c2 * 128:(c2 + 1) * 128], att[:, h, :], v_in[:, h * 128:(h + 1) * 128], True, False, [Tatt, Tv], [Tops])
                        self.mm(ops[:, c2 * 128:(c2 + 1) * 128], q_in[ph:ph + 64, c2, :], st_b[ph:ph + 64, c2, :], False, True, [Tq, Tstb], [Tops])
                    for c2 in range(2):
                        self.mm(kps[:, c2 * 256:(c2 + 1) * 256], d_in[:, c2 * 128:(c2 + 1) * 128], v_in[:, c2 * 256:(c2 + 1) * 256], True, True, [Td, Tv], [Tkps])
                    for h in range(4):
                        c2, hh = h // 2, h % 2
                        rows = slice(hh * 64, (hh + 1) * 64)
                        col = gb * 4 + dr * 2 + c2
                        self._stt(st_f[rows, c2, :], st_f[rows, c2, :], self.DEC[rows, col:col + 1],
                                  kps[rows, c2 * 256 + hh * 128: c2 * 256 + (hh + 1) * 128], ALU.mult, ALU.add, [Tst, Tkps, self.TDEC], [Tst])
                    self.cp(st_b, st_f, [Tst], [Tstb], eng="pool")
                    OFv = OF[:, lb, :].rearrange("p (c par v) -> p c par v", c=2, par=2)
                    totv = tot.rearrange("p (c par v) -> p c par v", c=2, par=2)
                    for par in range(2):
                        ops, Tops = o_ps[par]
                        opv = ops[:, 0:256].rearrange("p (c v) -> p c v", c=2)
                        if dr == 0:
                            self.act(OFv[:, :, par, :], opv, AF.Copy, [Tops], [TOF])
                        else:
                            self.tt(totv[:, :, par, :], opv, OFv[:, :, par, :], ALU.add, [Tops, TOF], [Ttot])
                    if dr == 1:
                        self.act(sq, tot, AF.Square, [Ttot], [Tsq])
                        self.dve(lambda e, sq=sq, ssq=ssq: e.tensor_reduce(out=ssq, in_=sq.rearrange("p (h v) -> p h v", h=4), axis=AX.X, op=ALU.add), [Tsq], [Tssq])
                        self.act(ssq, ssq, AF.Sqrt, [Tssq], [Tssq], scale=1.0 / 128, bias=self.epsc)
                        self.dve(lambda e: e.reciprocal(out=ssq, in_=ssq), [Tssq], [Tssq])
                        self.tt(srg.rearrange("p (h v) -> p h v", h=4), sr_in.rearrange("p (h v) -> p h v", h=4),
                                gnb.unsqueeze(1).to_broadcast([128, 4, 128]), ALU.mult, [Tsr, Tgnb], [Tsrg], eng="pool")
                        self.tt(tot.rearrange("p (h v) -> p h v", h=4), tot.rearrange("p (h v) -> p h v", h=4),
                                ssq.unsqueeze(2).to_broadcast([128, 4, 128]), ALU.mult, [Ttot, Tssq], [Ttot])
                        og, Tog = ogrot.next()
                        self.tt(og, tot, srg, ALU.mult, [Ttot, Tsrg], [Tog])
                        for h in range(4):
                            self.tr(tpb[:, h * 128:(h + 1) * 128], og[:, h * 128:(h + 1) * 128], self.ident, [Tog, Tc], [Ttp])
                        ogT, TogT = ogTrot.next()
                        self.act(ogT, tpb[:, 0:512].rearrange("p (h t) -> p h t", h=4), AF.Copy, [Ttp], [TogT])
                        self.dma(S["OGT"][:, :, r0:r0 + 128].rearrange("h p t -> p h t"), ogT, [TogT], [TS["OGT"]])
            tok += S_len

    def _stt(self, out, in0, scalar, in1, op0, op1, R, W, eng="dve"):
        self.em.op(eng, lambda e: e.scalar_tensor_tensor(out=out, in0=in0, scalar=scalar, in1=in1, op0=op0, op1=op1), R, W)

    def phaseB(self):
        nc, I, S, TS = self.nc, self.I, self.S, self.TS
        A = self.arena
        A.reset()
        Tc = self.Tc
        maxS = max(self.seqs)
        misc_ps, Tmisc = self.banks[7][:, :], self.Tbank[7]
        rb = A.alloc([32, 4], F32)
        oh = A.alloc([32, 1280], F32)
        Tset = T("setupB")
        self.dma(rb, I["rel_bias"], [], [Tset])
        self.dma(oh, I["t5oh"], [], [Tset])
        bv_sb = A.alloc([4, 1280], F32)
        for (a, b) in ((0, 512), (512, 1024), (1024, 1280)):
            self.mm(misc_ps[0:4, 0:b - a], rb, oh[:, a:b], True, True, [Tset], [Tmisc])
            self.cp(bv_sb[:, a:b], misc_ps[0:4, 0:b - a], [Tmisc], [Tset])
        self.dma(S["BV"], bv_sb, [Tset], [TS["BV"]])
        TBf = A.alloc([128, 4, 1154], F32)
        TB = A.alloc([128, 4, 1154], BF16)
        TTB = T("TB")
        for h in range(4):
            src = bass.AP(S["BV"].tensor, h * 1280, [[1, 128], [1, 1153]])
            self.dma(TBf[:, h, 0:1153], src, [TS["BV"]], [TTB])
        self.cp(TB[:, :, 0:1153], TBf[:, :, 0:1153], [TTB], [TTB])
        Jf = A.alloc([128, 128], F32)
        Jb = A.alloc([128, 128], BF16)
        self.pool(lambda e: e.memset(Jf, 0.0), [], [TTB])
        self.pool(lambda e: e.affine_select(out=Jf, in_=Jf, compare_op=ALU.not_equal, fill=1.0, base=-127,
                                            pattern=[[1, 128]], channel_multiplier=1), [TTB], [TTB])
        self.pool(lambda e: e.tensor_copy(out=Jb, in_=Jf), [TTB], [TTB])
        cfar = A.alloc([128, 4, 2], F32)
        Tcf = T("cfar")
        for h in range(4):
            self.dma(cfar[:, h, 0:1], S["BV"][h:h + 1, 1279:1280].partition_broadcast(128), [TS["BV"]], [Tcf])
            self.dma(cfar[:, h, 1:2], S["BV"][h:h + 1, 1:2].partition_broadcast(128), [TS["BV"]], [Tcf])
        lq = A.alloc([1, 256], F32)
        lp = A.alloc([1, 128], F32)
        l2 = A.alloc([1, 2], F32)
        Tl = T("lam")
        self.dma(lq, I["lambda_qk"].rearrange("(o a) d -> o (a d)", o=1), [], [Tl])
        lqv = lq.rearrange("p (a b d) -> p a b d", a=2, b=2)
        self.tt(lp.rearrange("p (a d) -> p a d", a=2), lqv[:, :, 0, :], lqv[:, :, 1, :], ALU.mult, [Tl], [Tl])
        self.dve(lambda e: e.tensor_reduce(out=l2, in_=lp.rearrange("p (a d) -> p a d", a=2), axis=AX.X, op=ALU.add), [Tl], [Tl])
        self.act(l2, l2, AF.Exp, [Tl], [Tl])
        lam_init = 0.8 - 0.6 * math.exp(-0.3 * 0)
        self.tt(l2[:, 0:1], l2[:, 1:2], l2[:, 0:1], ALU.subtract, [Tl], [Tl])
        self.ts(l2[:, 0:1], l2[:, 0:1], -lam_init, None, ALU.add, [Tl], [Tl])
        ones1 = A.alloc([1, 128], F32)
        self.dve(lambda e: e.memset(ones1, 1.0), [], [Tl])
        neglam = A.alloc([128, 1], F32)
        self.mm(misc_ps[:, 0:2], ones1, l2[:, 0:2], True, True, [Tl], [Tmisc])
        self.cp(neglam, misc_ps[:, 0:1], [Tmisc], [Tl])
        gab = A.alloc([128, 128], F32)
        self.dma(gab, I["da_norm_g"].partition_broadcast(128), [], [Tl])
        self.ts(gab, gab, 1.0 - lam_init, None, ALU.mult, [Tl], [Tl])

        ktrot = Rot(A, 2, [128, maxS], BF16, "kt")
        vxrot = Rot(A, 2, [128, maxS // 128, 132], BF16, "vx")
        for (vb, Tv) in vxrot.bufs:
            self.pool(lambda e, vb=vb: e.memset(vb, 1.0), [], [Tv])
        qrot = Rot(A, 2, [128, 512], BF16, "q")
        ptrot = Rot(A, 3, [128, 512], BF16, "pt")
        osb = [(A.alloc([128, 128], F32), T("osb%d" % i)) for i in range(4)]
        o2rot = Rot(A, 2, [128, 128], F32, "o2")
        obrot = Rot(A, 2, [128, 128], BF16, "ob")
        rrot = Rot(A, 4, [128, 2], F32, "rz")
        junk = A.alloc([128, 128], F32)
        Tjunk = T("junkB")
        oaTrot = Rot(A, 2, [128, 512], BF16, "oaT")
        tpb, Ttp = self.banks[0][:, :].bitcast(BF16), self.Tbank[0]
        lg_ps = [(self.banks[1][:, :], self.Tbank[1]), (self.banks[2][:, :], self.Tbank[2])]
        acc_ps = [(self.banks[3 + i][:, :], self.Tbank[3 + i]) for i in range(4)]
        li = [0]
        tok = 0
        total_steps = sum((S_ // 512 if S_ >= 512 else 1) * 2 * (S_ // 128) * 4 for S_ in self.seqs)
        p0 = self.p0_gen(A, (misc_ps.bitcast(BF16), Tmisc)) if "P" in self.phases else iter(())
        every = max(1, total_steps // 128)
        stepno = [0]
        for S_len in self.seqs:
            nkb = S_len // 128
            for h in range(4):
                kt, Tkt = ktrot.next()
                vx, Tvx = vxrot.next()
                self.dma(kt[:, 0:S_len], S["KT"][h, :, tok:tok + S_len], [TS["KT"]], [Tkt])
                self.dma(vx[:, 0:nkb, 0:128], S["VA"][tok:tok + S_len, h * 128:(h + 1) * 128].rearrange("(kb p) v -> p kb v", p=128),
                         [TS["VA"]], [Tvx])
                steps = []
                for q0 in range(0, S_len, 512):
                    nq = min(512, S_len - q0)
                    ctx = {"q0": q0, "nq": nq}
                    for c in range(2):
                        for kb in range(nkb):
                            steps.append((ctx, c, kb))

                def prep(ctx):
                    if "qt" not in ctx:
                        ctx["qt"], ctx["Tq"] = qrot.next()
                        self.dma(ctx["qt"][:, 0:ctx["nq"]], S["QT"][h, :, tok + ctx["q0"]:tok + ctx["q0"] + ctx["nq"]], [TS["QT"]], [ctx["Tq"]])
                        ctx["oaT"], ctx["ToaT"] = oaTrot.next()

                def emit_lg(st):
                    ctx, c, kb = st
                    prep(ctx)
                    q0, nq, qt, Tq = ctx["q0"], ctx["nq"], ctx["qt"], ctx["Tq"]
                    pr = slice(c * 64, (c + 1) * 64)
                    k0 = kb * 128
                    delta = k0 - q0
                    near = (-256 < delta < 640)
                    lg, Tlg = lg_ps[li[0] % 2]
                    li[0] += 1
                    self.mm(lg[:, 0:nq], kt[pr, k0:k0 + 128], qt[pr, 0:nq], True, not near, [Tkt, Tq], [Tlg])
                    pt, Tpt = ptrot.next()
                    if near:
                        stt_ = 512 - delta + 1
                        self.mm(lg[:, 0:nq], Jb, TB[:, h, stt_:stt_ + nq], False, True, [Tc, TTB], [Tlg])
                        self.act(pt[:, 0:nq], lg[:, 0:nq], AF.Exp, [Tlg], [Tpt])
                    else:
                        self.act(pt[:, 0:nq], lg[:, 0:nq], AF.Exp, [Tlg, Tcf], [Tpt],
                                 bias=cfar[:, h, (0 if delta < 0 else 1):(1 if delta < 0 else 2)])
                    return pt, Tpt

                def emit_av(st, pt, Tpt):
                    ctx, c, kb = st
                    nq = ctx["nq"]
                    nqb = nq // 128
                    for qb in range(nqb):
                        ac, Tac = acc_ps[qb]
                        self.mm(ac[:, 0:129], pt[:, qb * 128:(qb + 1) * 128], vx[:, kb, 0:129], kb == 0, kb == nkb - 1, [Tpt, Tvx], [Tac])
                    if kb != nkb - 1:
                        return
                    for qb in range(nqb):
                        ac, Tac = acc_ps[qb]
                        ob_, Tob_ = osb[qb]
                        rz, Trz = rrot.next()
                        self.dve(lambda e, rz=rz, ac=ac: e.reciprocal(out=rz[:, 0:1], in_=ac[:, 128:129]), [Tac], [Trz])
                        if c == 0:
                            self.ts(ob_, ac[:, 0:128], rz[:, 0:1], None, ALU.mult, [Tac, Trz], [Tob_])
                        else:
                            self.ts(rz[:, 0:1], rz[:, 0:1], neglam[:, 0:1], None, ALU.mult, [Trz, Tl], [Trz])
                            o2, To2 = o2rot.next()
                            self._stt(o2, ac[:, 0:128], rz[:, 0:1], ob_, ALU.mult, ALU.add, [Tac, Trz, Tob_], [To2])
                            self.act(junk, o2, AF.Square, [To2], [Tjunk, Trz], accum_out=rz[:, 1:2])
                            self.act(rz[:, 1:2], rz[:, 1:2], AF.Sqrt, [Trz], [Trz], scale=1.0 / 128, bias=self.epsc)
                            self.dve(lambda e, rz=rz: e.reciprocal(out=rz[:, 1:2], in_=rz[:, 1:2]), [Trz], [Trz])
                            ob, Tob = obrot.next()
                            self._stt(ob, o2, rz[:, 1:2], gab, ALU.mult, ALU.mult, [To2, Trz, Tl], [Tob])
                            self.tr(tpb[:, qb * 128:(qb + 1) * 128], ob, self.ident, [Tob, Tc], [Ttp])
                    if c == 1:
                        q0 = ctx["q0"]
                        self.act(ctx["oaT"][:, 0:nq], tpb[:, 0:nq], AF.Copy, [Ttp], [ctx["ToaT"]])
                        self.dma(S["OAT"][h, :, tok + q0:tok + q0 + nq], ctx["oaT"][:, 0:nq], [ctx["ToaT"]], [TS["OAT"]])

                cur = emit_lg(steps[0])
                for i in range(len(steps)):
                    nxt = emit_lg(steps[i + 1]) if i + 1 < len(steps) else None
                    emit_av(steps[i], cur[0], cur[1])
                    cur = nxt
                    stepno[0] += 1
                    if stepno[0] % every == 0:
                        next(p0, None)
            tok += S_len
        for _ in p0:
            pass

    def phaseM(self):
        nc, I, S, TS = self.nc, self.I, self.S, self.TS
        A = self.arena
        A.reset()
        Tc = self.Tc
        stage = Rot(A, 2, [128, 8, 512], F32, "wstage")
        wua = A.alloc([128, 4, D], BF16)
        wub = A.alloc([128, 4, D], BF16)
        wout = A.alloc([128, 8, D], BF16)
        Tw = T("wM")
        self.load_cast_weight(wua, I["w_up_a"], 4, D, None, stage, Tw)
        self.load_cast_weight(wub, I["w_up_b"], 4, D, None, stage, Tw)
        self.load_cast_weight(wout, I["w_out"], 8, D, None, stage, Tw)
        oarot = Rot(A, 2, [128, 4, 512], BF16, "oat")
        ogrot = Rot(A, 2, [128, 4, 512], BF16, "ogt")
        sgrot = Rot(A, 2, [128, 16, 512], BF16, "sgt")
        m1 = A.alloc([128, 512], F32)
        m2 = A.alloc([128, 512], F32)
        Tm1, Tm2 = T("m1"), T("m2")
        mTrot = Rot(A, 2, [128, 8, 512], BF16, "mT")
        xrot = Rot(A, 2, [128, D], F32, "xM")
        orot = Rot(A, 2, [128, D], F32, "oM")
        ya_ps = [(self.banks[0][:, :], self.Tbank[0]), (self.banks[1][:, :], self.Tbank[1])]
        yb_ps = [(self.banks[2][:, :], self.Tbank[2]), (self.banks[3][:, :], self.Tbank[3])]
        out_ps = [((self.banks[4][:, :], self.banks[5][:, :]), self.Tbank[4]), ((self.banks[6][:, :], self.banks[7][:, :]), self.Tbank[6])]
        yi = 0
        oi = 0
        for g0 in range(0, self.NT, 512):
            ntok = min(512, self.NT - g0)
            oat, Toat = oarot.next()
            ogt, Togt = ogrot.next()
            sgt, Tsgt = sgrot.next()
            self.dma(oat[:, :, 0:ntok], S["OAT"][:, :, g0:g0 + ntok].rearrange("h p t -> p h t"), [TS["OAT"]], [Toat])
            self.dma(ogt[:, :, 0:ntok], S["OGT"][:, :, g0:g0 + ntok].rearrange("h p t -> p h t"), [TS["OGT"]], [Togt])
            self.dma(sgt[:, :, 0:ntok], S["SGT"][:, :, g0:g0 + ntok].rearrange("j p t -> p j t"), [TS["SGT"]], [Tsgt])
            mT, TmT = mTrot.next()
            for j in range(8):
                pa, Tpa = ya_ps[yi % 2]
                pb, Tpb = yb_ps[yi % 2]
                yi += 1
                for f in range(4):
                    self.mm(pa[:, 0:ntok], wua[:, f, j * 128:(j + 1) * 128], oat[:, f, 0:ntok], f == 0, f == 3, [Tw, Toat], [Tpa])
                for f in range(4):
                    self.mm(pb[:, 0:ntok], wub[:, f, j * 128:(j + 1) * 128], ogt[:, f, 0:ntok], f == 0, f == 3, [Tw, Togt], [Tpb])
                self.tt(m1[:, 0:ntok], pa[:, 0:ntok], sgt[:, j, 0:ntok], ALU.mult, [Tpa, Tsgt], [Tm1])
                self.tt(m2[:, 0:ntok], pb[:, 0:ntok], sgt[:, 8 + j, 0:ntok], ALU.mult, [Tpb, Tsgt], [Tm2])
                self.tt(mT[:, j, 0:ntok], m1[:, 0:ntok], m2[:, 0:ntok], ALU.add, [Tm1, Tm2], [TmT], eng="pool")
            for b in range(ntok // 128):
                r0 = g0 + b * 128
                xin, Txin = xrot.next()
                self.dma(xin, I["x"][r0:r0 + 128, :], [], [Txin])
                (p0, p1), Tp = out_ps[oi % 2]
                oi += 1
                for half, pp in enumerate((p0, p1)):
                    for j in range(8):
                        self.mm(pp, mT[:, j, b * 128:(b + 1) * 128], wout[:, j, half * 512:(half + 1) * 512], j == 0, j == 7, [TmT, Tw], [Tp])
                xo, Txo = orot.next()
                self.tt(xo[:, 0:512], p0, xin[:, 0:512], ALU.add, [Tp, Txin], [Txo])
                self.tt(xo[:, 512:1024], p1, xin[:, 512:1024], ALU.add, [Tp, Txin], [Txo])
                self.dma(S["X1"][r0:r0 + 128, :], xo, [Txo], [TS["X1"]])

    def phaseP(self):
        self.phaseP0()
        self.em.barrier()
        self.phaseP1()
        self.em.barrier()
        self.phaseP2()

    def phaseP0(self):
        if getattr(self, "p0_done", False):
            return
        self.arena.reset()
        for _ in self.p0_gen(self.arena, (self.banks[0][:, :].bitcast(BF16), self.Tbank[0])):
            pass

    def p0_gen(self, A, tpbank):
        nc, I, S, TS = self.nc, self.I, self.S, self.TS
        Tc = self.Tc
        urot = Rot(A, 3, [128, D], F32, "u_in")
        ubrot = Rot(A, 2, [128, D], BF16, "u_bf")
        utrot = Rot(A, 3, [128, D], BF16, "uT")
        vrot = Rot(A, 3, [128, D], F32, "v_in")
        vbrot = Rot(A, 3, [128, D], BF16, "v_bf")
        tp, Ttp = tpbank

        def loads(i):
            u_in, Tu = urot.next()
            self.dma(u_in, I["peer_u"][i * 128:(i + 1) * 128, :], [], [Tu])
            v_in, Tv = vrot.next()
            self.dma(v_in, I["peer_v"][i * 128:(i + 1) * 128, :], [], [Tv])
            return u_in, Tu, v_in, Tv

        nxt = loads(0)
        for i in range(128):
            u_in, Tu, v_in, Tv = nxt
            if i + 1 < 128:
                nxt = loads(i + 1)
            u_bf, Tub = ubrot.next()
            self.cp(u_bf, u_in, [Tu], [Tub], eng="pool")
            for c in range(8):
                self.tr(tp[:, c * 128:(c + 1) * 128], u_bf[:, c * 128:(c + 1) * 128], self.ident, [Tub, Tc], [Ttp])
            uT, TuT = utrot.next()
            self.cp(uT, tp, [Ttp], [TuT])
            v_bf, Tvb = vbrot.next()
            self.cp(v_bf, v_in, [Tv], [Tvb], eng="pool")
            self.dma(S["UT"][i], uT, [TuT], [TS["UT"]])
            self.dma(S["VB"][i], v_bf, [Tvb], [TS["VB"]])
            yield i
        self.p0_done = True

    def phaseP1(self):
        nc, I, S, TS = self.nc, self.I, self.S, self.TS
        A = self.arena
        A.reset()
        Tc = self.Tc
        NT = self.NT
        stage = Rot(A, 2, [128, 8, 256], F32, "wstage")
        g2 = A.alloc([128, 8], F32)
        Tw = T("wP1")
        self.dma(g2, I["norm2_g"].rearrange("(c p) -> p c", p=128), [], [Tc], slow=True)
        wq = A.alloc([128, 8, 2048], BF16)
        self.load_cast_weight(wq, I["peer_w_q"], 8, 2048, g2, stage, Tw, col_piece=256)
        skf = A.alloc([128, 16, 128], F32)
        skb = A.alloc([128, 16, 128], BF16)
        skT = A.alloc([128, 16, 128], BF16)
        self.dma(skf, I["peer_sub_keys"].rearrange("g n d -> n g d"), [], [Tw])
        self.cp(skb, skf, [Tw], [Tw])
        tpb0, Ttp0 = self.banks[0][:, :].bitcast(BF16), self.Tbank[0]
        for half in range(2):
            for g in range(8):
                self.tr(tpb0[:, g * 128:(g + 1) * 128], skb[:, half * 8 + g, :], self.ident, [Tw, Tc], [Ttp0])
            self.cp(skT[:, half * 8:(half + 1) * 8, :], tpb0.rearrange("p (g n) -> p g n", g=8), [Ttp0], [Tw])
        io_i = A.alloc([128, 128], I32)
        self.iota128 = self.calloc(128)
        self.pool(lambda e: e.iota(io_i, pattern=[[1, 128]], base=0, channel_multiplier=0), [], [Tc])
        self.pool(lambda e: e.tensor_copy(out=self.iota128, in_=io_i), [Tc], [Tc])
        self.iota_bf = self.calloc(64).bitcast(BF16)
        self.pool(lambda e: e.tensor_copy(out=self.iota_bf, in_=io_i), [Tc], [Tc])
        iota16 = self.iota128[:, 0:16]
        xrot = Rot(A, 2, [128, D], F32, "x1in")
        scr = {
            "junk": (A.alloc([128, D], F32), T("junkP")),
            "ss": Rot(A, 2, [128, 1], F32, "ssP"),
            "xn": Rot(A, 2, [128, D], BF16, "xnP"),
            "tp": (self.banks[0][:, :].bitcast(BF16), self.Tbank[0]),
        }
        hTrot = Rot(A, 2, [128, 8, 512], BF16, "h2T")
        qT_sb = A.alloc([128, 16, 512], BF16)
        TqT = T("qT_sb")
        q_ps = [(self.banks[1][:, :], self.Tbank[1]), (self.banks[2][:, :], self.Tbank[2])]
        s_ps = [(self.banks[3 + i][:, :], self.Tbank[3 + i]) for i in range(4)]
        t_ps, Tt_ps = self.banks[7][:, :], self.Tbank[7]
        ssb_rot = Rot(A, 2, [128, 2048], F32, "s_sb")
        v16 = A.alloc([128, 16, 16], F32)
        i16 = A.alloc([128, 16, 16], U32)
        Tv16 = [T("v16_%d" % g) for g in range(16)]
        Ti16 = [T("i16_%d" % g) for g in range(16)]
        s2 = A.alloc([128, 16, 128], F32)
        Ts2 = [T("s2_%d" % g) for g in range(16)]
        cand = A.alloc([128, 8, 256], F32)
        Tcand = T("cand")
        c2b = A.alloc([128, 8, 256], F32)
        Tc2 = [T("c2b_%d" % h) for h in range(8)]
        tv = A.alloc([128, 8, 16], F32)
        pos = A.alloc([128, 8, 16], U32)
        Ttvl = [T("tv_%d" % h) for h in range(8)]
        Tposl = [T("pos_%d" % h) for h in range(8)]
        ex = A.alloc([128, 8, 16], F32)
        Tex = T("ex")
        zz = A.alloc([128, 8], F32)
        Tzz = T("zz")
        au = A.alloc([128, 8, 16], U32)
        bu = A.alloc([128, 8, 16], U32)
        af = A.alloc([128, 8, 16], F32)
        bf = A.alloc([128, 8, 16], F32)
        I1f = A.alloc([128, 8, 16], F32)
        I2f = A.alloc([128, 8, 16], F32)
        Tdec = T("decode")
        eq = A.alloc([128, 8, 16, 16], F32)
        Teq = T("eq")
        tabs = A.alloc([128, 3, 128], F32)
        Ttabs = T("tabs")
        slTrot = Rot(A, 2, [128, 3, 128], F32, "slT")
        qi = 0
        for g0 in range(0, NT, 512):
            ntok = min(512, NT - g0)
            hT, ThT = hTrot.next()
            for b in range(ntok // 128):
                xin, Txin = xrot.next()
                self.dma(xin, S["X1"][g0 + b * 128:g0 + (b + 1) * 128, :], [TS["X1"]], [Txin])
                self.rmsnorm_T(xin, Txin, None, hT, ThT, b * 128, ntok, scr)
            self.dma(S["H2T"][:, :, g0:g0 + ntok].rearrange("c p t -> p c t"), hT[:, :, 0:ntok], [ThT], [TS["H2T"]])
            for hc in range(16):
                ps, Tps = q_ps[qi % 2]
                qi += 1
                for c in range(8):
                    self.mm(ps[:, 0:ntok], wq[:, c, hc * 128:(hc + 1) * 128], hT[:, c, 0:ntok], c == 0, c == 7, [Tw, ThT], [Tps])
                self.act(qT_sb[:, hc, 0:ntok], ps[:, 0:ntok], AF.Copy, [Tps], [TqT])
            for b in range(ntok // 128):
                r0 = g0 + b * 128
                for hc in range(16):
                    bk, Tbk = s_ps[hc // 4]
                    self.mm(bk[:, (hc % 4) * 128:(hc % 4 + 1) * 128], qT_sb[:, hc, b * 128:(b + 1) * 128], skT[:, hc, :], True, True, [TqT, Tw], [Tbk])
                s_sb, Tssb = ssb_rot.next()
                for q in range(4):
                    bk, Tbk = s_ps[q]
                    self.act(s_sb[:, q * 512:(q + 1) * 512], bk, AF.Copy, [Tbk], [Tssb])
                self._top16_multi([(s_sb[:, g * 128:(g + 1) * 128], Tssb, s2[:, g, :], Ts2[g], v16[:, g, :], Tv16[g], i16[:, g, :], Ti16[g])
                                   for g in range(16)])
                v16v = v16.rearrange("p (h c) k -> p h c k", c=2)
                self.tt(cand.rearrange("p h (a b) -> p h a b", a=16),
                        v16v[:, :, 0, :].unsqueeze(3).to_broadcast([128, 8, 16, 16]),
                        v16v[:, :, 1, :].unsqueeze(2).to_broadcast([128, 8, 16, 16]), ALU.add, Tv16, [Tcand])
                self._top16_multi([(cand[:, h, :], Tcand, c2b[:, h, :], Tc2[h], tv[:, h, :], Ttvl[h], pos[:, h, :], Tposl[h]) for h in range(8)])
                self.tt(ex, tv, tv[:, :, 0:1].to_broadcast([128, 8, 16]), ALU.subtract, Ttvl, [Tex])
                self.act(ex, ex, AF.Exp, [Tex], [Tex])
                self.dve(lambda e: e.tensor_reduce(out=zz, in_=ex, axis=AX.X, op=ALU.add), [Tex], [Tzz])
                self.dve(lambda e: e.reciprocal(out=zz, in_=zz), [Tzz], [Tzz])
                self.tt(tabs[:, 2, :].rearrange("p (h k) -> p h k", h=8), ex, zz.unsqueeze(2).to_broadcast([128, 8, 16]), ALU.mult,
                        [Tex, Tzz], [Ttabs])
                self.dve(lambda e: e.tensor_single_scalar(out=au, in_=pos, scalar=4, op=ALU.logical_shift_right), Tposl, [Tdec])
                self.dve(lambda e: e.tensor_single_scalar(out=bu, in_=pos, scalar=15, op=ALU.bitwise_and), Tposl, [Tdec])
                self.cp(af, au, [Tdec], [Tdec])
                self.cp(bf, bu, [Tdec], [Tdec])
                i16v = i16.rearrange("p (h c) k -> p h c k", c=2)
                self.cp(I1f, i16v[:, :, 0, :], Ti16, [Tdec])
                self.cp(I2f, i16v[:, :, 1, :], Ti16, [Tdec])
                for (sel, tabf, j) in ((af, I1f, 0), (bf, I2f, 1)):
                    self.tt(eq, iota16.unsqueeze(1).unsqueeze(1).to_broadcast([128, 8, 16, 16]),
                            sel.unsqueeze(3).to_broadcast([128, 8, 16, 16]), ALU.is_equal, [Tc, Tdec], [Teq])
                    self.tt(eq, eq, tabf.unsqueeze(2).to_broadcast([128, 8, 16, 16]), ALU.mult, [Teq, Tdec], [Teq])
                    self.dve(lambda e, j=j: e.tensor_reduce(out=tabs[:, j, :].rearrange("p (h k) -> p h k", h=8), in_=eq, axis=AX.X, op=ALU.add),
                             [Teq], [Ttabs])
                for j in range(3):
                    self.tr(t_ps[:, j * 128:(j + 1) * 128], tabs[:, j, :], self.identf, [Ttabs, Tc], [Tt_ps])
                slT, TslT = slTrot.next()
                self.cp(slT, t_ps[:, 0:384].rearrange("p (j t) -> p j t", j=3), [Tt_ps], [TslT])
                self.dma(S["SL"][:, :, r0:r0 + 128].rearrange("j p t -> p j t"), slT, [TslT], [TS["SL"]])

    def _top16_multi(self, items):
        for (src, Tsrc, tmp, Ttmp, vout, Tvout, iout, Tiout) in items:
            self.dve(lambda e, vout=vout, src=src: e.max(out=vout[:, 0:8], in_=src), [Tsrc], [Tvout])
        for (src, Tsrc, tmp, Ttmp, vout, Tvout, iout, Tiout) in items:
            self.dve(lambda e, vout=vout, src=src, iout=iout: e.max_index(out=iout[:, 0:8], in_max=vout[:, 0:8], in_values=src), [Tsrc, Tvout], [Tiout])
        for (src, Tsrc, tmp, Ttmp, vout, Tvout, iout, Tiout) in items:
            self.dve(lambda e, vout=vout, src=src, tmp=tmp: e.match_replace(out=tmp, in_to_replace=vout[:, 0:8], in_values=src, imm_value=-1e30), [Tsrc, Tvout], [Ttmp])
        for (src, Tsrc, tmp, Ttmp, vout, Tvout, iout, Tiout) in items:
            self.dve(lambda e, vout=vout, tmp=tmp: e.max(out=vout[:, 8:16], in_=tmp), [Ttmp], [Tvout])
        for (src, Tsrc, tmp, Ttmp, vout, Tvout, iout, Tiout) in items:
            self.dve(lambda e, vout=vout, tmp=tmp, iout=iout: e.max_index(out=iout[:, 8:16], in_max=vout[:, 8:16], in_values=tmp), [Ttmp, Tvout], [Tiout])

    def _top16(self, src, Tsrc, tmp, Ttmp, vout, Tvout, iout, Tiout):
        self.dve(lambda e: e.max(out=vout[:, 0:8], in_=src), [Tsrc], [Tvout])
        self.dve(lambda e: e.max_index(out=iout[:, 0:8], in_max=vout[:, 0:8], in_values=src), [Tsrc, Tvout], [Tiout])
        self.dve(lambda e: e.match_replace(out=tmp, in_to_replace=vout[:, 0:8], in_values=src, imm_value=-1e30), [Tsrc, Tvout], [Ttmp])
        self.dve(lambda e: e.max(out=vout[:, 8:16], in_=tmp), [Ttmp], [Tvout])
        self.dve(lambda e: e.max_index(out=iout[:, 8:16], in_max=vout[:, 8:16], in_values=tmp), [Ttmp, Tvout], [Tiout])

    def phaseP2(self):
        nc, I, S, TS = self.nc, self.I, self.S, self.TS
        A = self.arena
        A.reset()
        Tc = self.Tc
        NT = self.NT
        TT = 256
        Gsb = A.alloc([128, 128, TT], BF16)
        TG = T("Gsb")
        utrot = Rot(A, 2, [128, 8, 1024], BF16, "ut")
        vbrot = Rot(A, 2, [128, 8, 1024], BF16, "vb")
        wrot = Rot(A, 2, [128, 16, 128], BF16, "Wt")
        yrot = Rot(A, 2, [128, 16, 128], BF16, "Yt")
        hrot = Rot(A, 2, [128, 8, TT], BF16, "h2t")
        slrot = Rot(A, 2, [128, 3, TT], F32, "sl")
        agrot = Rot(A, 3, [128, TT], BF16, "ag")
        garot = Rot(A, 3, [128, TT], BF16, "ga")
        xrot = Rot(A, 2, [128, D], F32, "x1p")
        orot = Rot(A, 2, [128, D], F32, "x2p")
        out_ps = [(self.banks[i][:, :], self.Tbank[i]) for i in range(4)]
        a_ps = [(self.banks[4][:, :], self.Tbank[4]), (self.banks[5][:, :], self.Tbank[5])]
        g_ps = [(self.banks[6][:, :], self.Tbank[6]), (self.banks[7][:, :], self.Tbank[7])]
        iota_b = self.iota128.unsqueeze(1).to_broadcast([128, 16, 128])
        ai = 0
        gi = 0
        grp_bufs = {}
        ntiles = (NT + TT - 1) // TT

        def prefetch(G):
            if G in grp_bufs or G >= ntiles * 16:
                return
            grp = G % 16
            ut, Tut = utrot.next()
            vb, Tvb = vbrot.next()
            self.dma(ut, S["UT"][grp * 8:(grp + 1) * 8].rearrange("j p f -> p j f"), [TS["UT"]], [Tut])
            self.dma(vb, S["VB"][grp * 8:(grp + 1) * 8].rearrange("j p f -> p j f"), [TS["VB"]], [Tvb])
            grp_bufs[G] = (ut, Tut, vb, Tvb)

        prefetch(0)
        prefetch(1)
        for ti, g0 in enumerate(range(0, NT, TT)):
            nt = min(TT, NT - g0)
            h2t, Th = hrot.next()
            sl, Tsl = slrot.next()
            self.dma(h2t[:, :, 0:nt], S["H2T"][:, :, g0:g0 + nt].rearrange("c p t -> p c t"), [TS["H2T"]], [Th])
            self.dma(sl[:, :, 0:nt], S["SL"][:, :, g0:g0 + nt].rearrange("j p t -> p j t"), [TS["SL"]], [Tsl])
            for t0 in range(0, nt, 16):
                Wt, TW = wrot.next()
                Yt, TY = yrot.next()
                for t in range(16):
                    tc_ = t0 + t
                    self.ts(Wt[:, t, :], self.iota_bf, sl[:, 0, tc_:tc_ + 1], sl[:, 2, tc_:tc_ + 1], ALU.is_equal, [Tc, Tsl], [TW], op1=ALU.mult)
                    self.ts(Yt[:, t, :], self.iota_bf, sl[:, 1, tc_:tc_ + 1], None, ALU.is_equal, [Tc, Tsl], [TY])
                for q4 in range(4):
                    gp, Tgp = g_ps[gi % 2]
                    gi += 1
                    for tl in range(4):
                        t = q4 * 4 + tl
                        self.mm(gp[:, tl * 128:(tl + 1) * 128], Yt[:, t, :], Wt[:, t, :], True, True, [TY, TW], [Tgp])
                    ta = t0 + q4 * 4
                    self.act(Gsb[:, :, ta:ta + 4].rearrange("p i t -> p t i"), gp.rearrange("p (t i) -> p t i", t=4), AF.Copy, [Tgp], [TG])
            def get_grp(grp):
                return grp_bufs[ti * 16 + grp]

            def emit_u(i1):
                grp, j = i1 // 8, i1 % 8
                ut, Tut, vb, Tvb = get_grp(grp)
                ap_, Tap = a_ps[i1 % 2]
                for c in range(8):
                    self.mm(ap_[:, 0:nt], ut[:, j, c * 128:(c + 1) * 128], h2t[:, c, 0:nt], c == 0, c == 7, [Tut, Th], [Tap])
                ag, Tag = agrot.next()
                self.act(ag[:, 0:nt], ap_[:, 0:nt], AF.Gelu_apprx_tanh, [Tap], [Tag])
                ga, Tga = garot.next()
                self.tt(ga[:, 0:nt], ag[:, 0:nt], Gsb[:, i1, 0:nt], ALU.mult, [Tag, TG], [Tga])
                return ga, Tga

            def emit_v(i1, ga, Tga):
                grp, j = i1 // 8, i1 % 8
                ut, Tut, vb, Tvb = get_grp(grp)
                for tb in range(nt // 128):
                    for half in range(2):
                        op_, Top = out_ps[tb * 2 + half]
                        self.mm(op_, ga[:, tb * 128:(tb + 1) * 128], vb[:, j, half * 512:(half + 1) * 512], i1 == 0, i1 == 127, [Tga, Tvb], [Top])

            cur = emit_u(0)
            for i1 in range(128):
                nxt = emit_u(i1 + 1) if i1 + 1 < 128 else None
                emit_v(i1, cur[0], cur[1])
                cur = nxt
                if i1 % 8 == 7:
                    prefetch(ti * 16 + i1 // 8 + 2)
            for tb in range(nt // 128):
                r0 = g0 + tb * 128
                xin, Txin = xrot.next()
                self.dma(xin, S["X1"][r0:r0 + 128, :], [TS["X1"]], [Txin])
                xo, Txo = orot.next()
                for half in range(2):
                    op_, Top = out_ps[tb * 2 + half]
                    self.tt(xo[:, half * 512:(half + 1) * 512], op_, xin[:, half * 512:(half + 1) * 512], ALU.add, [Top, Txin], [Txo])
                self.dma(S["X2"][r0:r0 + 128, :], xo, [Txo], [TS["X2"]])

    def phaseE(self):
        nc, I, S, TS = self.nc, self.I, self.S, self.TS
        A = self.arena
        A.reset()
        Tc = self.Tc
        stage = Rot(A, 2, [128, 8, 512], F32, "wstage")
        g3 = A.alloc([128, 8], F32)
        Tw = T("wE")
        self.dma(g3, I["norm3_g"].rearrange("(c p) -> p c", p=128), [], [Tc], slow=True)
        wg = A.alloc([128, 8, D], BF16)
        wp = A.alloc([128, 2, D], BF16)
        self.load_cast_weight(wg, I["ple_gate_w"], 8, D, g3, stage, Tw)
        self.load_cast_weight(wp, I["ple_w"], 2, D, None, stage, Tw)
        gfb = A.alloc([128, D], F32)
        self.dma(gfb, I["final_norm_g"].partition_broadcast(128), [], [Tw])
        xrot = Rot(A, 2, [128, D], F32, "xE")
        prot = Rot(A, 2, [128, 256], F32, "pE")
        pbrot = Rot(A, 2, [128, 256], BF16, "pbE")
        pTrot = Rot(A, 2, [128, 2, 128], BF16, "pT")
        scr = {
            "junk": (A.alloc([128, D], F32), T("junkE")),
            "ss": Rot(A, 2, [128, 1], F32, "ssE"),
            "xn": Rot(A, 2, [128, D], BF16, "xnE"),
            "tp": (self.banks[0][:, :].bitcast(BF16), self.Tbank[0]),
        }
        hTrot = Rot(A, 2, [128, 8, 128], BF16, "h3T")
        gate = A.alloc([128, D], F32)
        Tgate = T("gate")
        x3rot = Rot(A, 2, [128, D], F32, "x3")
        yrot = Rot(A, 2, [128, D], F32, "yE")
        ssf = Rot(A, 2, [128, 1], F32, "ssf")
        junk2 = A.alloc([128, D], F32)
        Tj2 = T("junk2")
        g_ps = ((self.banks[1][:, :], self.banks[2][:, :]), self.Tbank[1])
        p_ps = ((self.banks[3][:, :], self.banks[4][:, :]), self.Tbank[3])
        tp2, Ttp2 = self.banks[5][:, :].bitcast(BF16), self.Tbank[5]
        for gb in range(self.NB):
            r0 = gb * 128
            xin, Txin = xrot.next()
            self.dma(xin, S["X2"][r0:r0 + 128, :], [TS["X2"]], [Txin])
            pin, Tpin = prot.next()
            self.dma(pin, I["p"][r0:r0 + 128, :], [], [Tpin])
            pb, Tpb = pbrot.next()
            self.cp(pb, pin, [Tpin], [Tpb], eng="pool")
            for c2 in range(2):
                self.tr(tp2[:, c2 * 128:(c2 + 1) * 128], pb[:, c2 * 128:(c2 + 1) * 128], self.ident, [Tpb, Tc], [Ttp2])
            pT, TpT = pTrot.next()
            self.cp(pT, tp2[:, 0:256].rearrange("p (c t) -> p c t", c=2), [Ttp2], [TpT])
            hT, ThT = hTrot.next()
            self.rmsnorm_T(xin, Txin, None, hT, ThT, 0, 128, scr)
            (g0_, g1_), Tg = g_ps
            (q0_, q1_), Tq = p_ps
            for half, pp in enumerate((g0_, g1_)):
                for c in range(8):
                    self.mm(pp, hT[:, c, :], wg[:, c, half * 512:(half + 1) * 512], c == 0, c == 7, [ThT, Tw], [Tg])
            for half, pp in enumerate((q0_, q1_)):
                for c2 in range(2):
                    self.mm(pp, pT[:, c2, :], wp[:, c2, half * 512:(half + 1) * 512], c2 == 0, c2 == 1, [TpT, Tw], [Tq])
            self.act(gate[:, 0:512], g0_, AF.Sigmoid, [Tg], [Tgate])
            self.act(gate[:, 512:1024], g1_, AF.Sigmoid, [Tg], [Tgate])
            x3, Tx3 = x3rot.next()
            self.tt(x3[:, 0:512], q0_, gate[:, 0:512], ALU.mult, [Tq, Tgate], [Tx3])
            self.tt(x3[:, 512:1024], q1_, gate[:, 512:1024], ALU.mult, [Tq, Tgate], [Tx3])
            self.tt(x3, x3, xin, ALU.add, [Tx3, Txin], [Tx3], eng="pool")
            sf, Tsf = ssf.next()
            self.act(junk2, x3, AF.Square, [Tx3], [Tj2, Tsf], accum_out=sf)
            self.act(sf, sf, AF.Sqrt, [Tsf], [Tsf], scale=1.0 / D, bias=self.epsc)
            self.dve(lambda e, sf=sf: e.reciprocal(out=sf, in_=sf), [Tsf], [Tsf])
            yo, Tyo = yrot.next()
            self._stt(yo, x3, sf, gfb, ALU.mult, ALU.mult, [Tx3, Tsf, Tw], [Tyo])
            self.dma(self.y[r0:r0 + 128, :], yo, [Tyo], [])


def t5_onehot():
    rel = 640 - np.arange(1280)
    nb, max_exact = 16, 8
    ret = (rel > 0).astype(np.int64) * nb
    n = np.abs(rel)
    nf = np.maximum(n, 1).astype(np.float32)
    large = max_exact + (np.log(nf / max_exact) / math.log(128 / max_exact) * (nb - max_exact)).astype(np.int32)
    large = np.minimum(large, nb - 1)
    bucket = ret + np.where(n < max_exact, n, large)
    oh = np.zeros((32, 1280), np.float32)
    oh[bucket, np.arange(1280)] = 1.0
    return oh


def core_inputs(inputs, c, seqs=SEQS_FULL):
    f = lambda a: np.ascontiguousarray(np.asarray(a, dtype=np.float32))
    xp = inputs["x_prompt"][c]
    xs0 = inputs["x_sample"][2 * c]
    xs1 = inputs["x_sample"][2 * c + 1]
    pp = inputs["p_prompt"][0, c]
    ps0 = inputs["p_sample"][0, 2 * c]
    ps1 = inputs["p_sample"][0, 2 * c + 1]
    m = {
        "x": f(np.concatenate([xp, xs0, xs1], axis=0)),
        "p": f(np.concatenate([pp, ps0, ps1], axis=0)),
    }
    m.update(shared_inputs(inputs))
    return m


def shared_inputs(inputs):
    f = lambda a: np.ascontiguousarray(np.asarray(a, dtype=np.float32))
    return {
        "rel_bias": f(inputs["rel_bias"]),
        "norm1_g": f(inputs["norm1_g"][0]),
        "w_in": f(inputs["w_in"][0]),
        "lambda_qk": f(inputs["lambda_qk"][0]),
        "da_norm_g": f(inputs["da_norm_g"][0]),
        "gla_alpha_w": f(inputs["gla_alpha_w"][0]),
        "gla_alpha_b": f(inputs["gla_alpha_b"][0]),
        "gla_norm_g": f(inputs["gla_norm_g"][0]),
        "w_up_a": f(inputs["w_up_a"][0]),
        "w_up_b": f(inputs["w_up_b"][0]),
        "w_out": f(inputs["w_out"][0]),
        "norm2_g": f(inputs["norm2_g"][0]),
        "peer_w_q": f(inputs["peer_w_q"][0]),
        "peer_sub_keys": f(np.asarray(inputs["peer_sub_keys"][0]).reshape(16, 128, 128)),
        "peer_u": f(inputs["peer_u"][0]),
        "peer_v": f(inputs["peer_v"][0]),
        "norm3_g": f(inputs["norm3_g"][0]),
        "ple_w": f(inputs["ple_w"][0]),
        "ple_gate_w": f(inputs["ple_gate_w"][0]),
        "final_norm_g": f(inputs["final_norm_g"]),
        "t5oh": t5_onehot(),
    }


def kernel(**inputs):
    nc = K(SEQS_FULL).build()
    in_maps = [core_inputs(inputs, c) for c in range(8)]
    res = run_bass_kernel_spmd(nc, in_maps, core_ids=list(range(8)))
    yp = np.zeros((8, 4096, D), np.float32)
    ys = np.zeros((16, 2048, D), np.float32)
    for c in range(8):
        y = res.results[c]["y"]
        yp[c] = y[0:4096]
        ys[2 * c] = y[4096:6144]
        ys[2 * c + 1] = y[6144:8192]
    return (yp, ys)
```
